# Optimizing a Trainium2 kernel written in Bass

```python
import math
import jax, jax.numpy as jnp
from jax import lax
import numpy as np

D_MODEL = 1024
BATCH = 16
SEQ = 4096
DEPTH = 1
DEC_BATCH = 8
DEC_SEQ = 2048
PAST_LEN = 128

HEAD_DIM = 128
N_Q_HEADS = 8
N_KV_HEADS = 2
GROUP = N_Q_HEADS // N_KV_HEADS
Q_WIDTH = N_Q_HEADS * HEAD_DIM
KV_WIDTH = N_KV_HEADS * HEAD_DIM
WINDOW = 128
BLOCK = 128
ROT_DIM = HEAD_DIM // 4
ROPE_THETA = 500000.0
NEG_BIG = -1e30
LRU_WIDTH = 1280
LRU_BLOCKS = 10
LRU_BLOCK_DIM = LRU_WIDTH // LRU_BLOCKS
CONV_WIDTH = 4
CONV_PAD_LEFT = 2
LRU_C = 8.0
N_DIRS = 2
D_FF = ((8 * D_MODEL // 3 + 255) // 256) * 256
EPS = 1e-6
IN_WIDTH = Q_WIDTH + 2 * KV_WIDTH + 2 * LRU_WIDTH + 2 * D_MODEL

kernel_name = "hybrid_swa_rglru_parallel_encoder"


def rmsnorm(x, w):
    x32 = x.astype(jnp.float32)
    y = x32 * lax.rsqrt(jnp.mean(x32 * x32, axis=-1, keepdims=True) + EPS)
    return (y * w.astype(jnp.float32)).astype(x.dtype)


def partial_rope(x, pos):
    half = ROT_DIM // 2
    inv = ROPE_THETA ** (-jnp.arange(half, dtype=jnp.float32) / half)
    ang = pos.astype(jnp.float32)[:, None] * inv[None, :]
    cos = jnp.cos(ang)[None, :, None, :]
    sin = jnp.sin(ang)[None, :, None, :]
    xr = x[..., :ROT_DIM].astype(jnp.float32)
    x1, x2 = xr[..., :half], xr[..., half:]
    rot = jnp.concatenate([x1 * cos - x2 * sin, x2 * cos + x1 * sin], axis=-1)
    return jnp.concatenate([rot.astype(x.dtype), x[..., ROT_DIM:]], axis=-1)


def local_attention(q, k, v, sink):
    B, S = q.shape[0], q.shape[1]
    nb = S // BLOCK
    qb = q.reshape(B, nb, BLOCK, N_KV_HEADS, GROUP, HEAD_DIM)

    def neighbours(t):
        tb = t.reshape(B, nb, BLOCK, N_KV_HEADS, HEAD_DIM)
        tp = jnp.pad(tb, ((0, 0), (1, 1), (0, 0), (0, 0), (0, 0)))
        return jnp.concatenate([tp[:, :-2], tp[:, 1:-1], tp[:, 2:]], axis=2)

    kn, vn = neighbours(k), neighbours(v)
    scale = 1.0 / math.sqrt(HEAD_DIM)
    s = jnp.einsum('bnqkgd,bnskd->bnkgqs', qb, kn).astype(jnp.float32) * scale
    blk = jnp.arange(nb)[:, None, None]
    qpos = blk * BLOCK + jnp.arange(BLOCK)[None, :, None]
    kpos = (blk - 1) * BLOCK + jnp.arange(3 * BLOCK)[None, None, :]
    valid = (jnp.abs(kpos - qpos) <= WINDOW) & (kpos >= 0) & (kpos < S)
    s = jnp.where(valid[None, :, None, None], s, NEG_BIG)
    sink_l = sink.astype(jnp.float32).reshape(N_KV_HEADS, GROUP)[None, None, :, :, None, None]
    m = jnp.maximum(jnp.max(s, axis=-1, keepdims=True), sink_l)
    p = jnp.exp(s - m)
    p = p / (jnp.sum(p, axis=-1, keepdims=True) + jnp.exp(sink_l - m))
    o = jnp.einsum('bnkgqs,bnskd->bnqkgd', p.astype(v.dtype), vn)
    return o.reshape(B, S, Q_WIDTH)


def centred_conv(x, w, b):
    S = x.shape[1]
    xp = jnp.pad(x, ((0, 0), (CONV_PAD_LEFT, CONV_WIDTH - 1 - CONV_PAD_LEFT), (0, 0)))
    y = b + xp[:, 0:S] * w[0]
    for t in range(1, CONV_WIDTH):
        y = y + xp[:, t:t + S] * w[t]
    return y


def rglru_coeffs(xc, w_a, b_a, w_i, b_i, lam):
    B, S, _ = xc.shape
    xb = xc.reshape(B, S, LRU_BLOCKS, LRU_BLOCK_DIM)
    ga = jnp.einsum('bsnc,ncd->bsnd', xb, w_a).reshape(B, S, LRU_WIDTH) + b_a
    gi = jnp.einsum('bsnc,ncd->bsnd', xb, w_i).reshape(B, S, LRU_WIDTH) + b_i
    r = jax.nn.sigmoid(ga.astype(jnp.float32))
    i = jax.nn.sigmoid(gi.astype(jnp.float32))
    log_a = -LRU_C * r * jax.nn.softplus(-lam.astype(jnp.float32))
    a = jnp.exp(log_a)
    u = jnp.sqrt(-jnp.expm1(2.0 * log_a)) * (i * xc.astype(jnp.float32))
    return a, u


def linear_scan(a, u):
    def combine(l, r):
        a1, b1 = l
        a2, b2 = r
        return a1 * a2, a2 * b1 + b2
    _, h = lax.associative_scan(combine, (a, u), axis=1)
    return h


def bidir_rglru(xc, w_a, b_a, w_i, b_i, lam):
    a_f, u_f = rglru_coeffs(xc, w_a[0], b_a[0], w_i[0], b_i[0], lam[0])
    a_b, u_b = rglru_coeffs(xc, w_a[1], b_a[1], w_i[1], b_i[1], lam[1])
    h_f = linear_scan(a_f, u_f)
    h_b = jnp.flip(linear_scan(jnp.flip(a_b, 1), jnp.flip(u_b, 1)), 1)
    return (h_f + h_b).astype(xc.dtype)


def encoder_layer(x, norm_mix_pre, w_in, attn_sink, conv_w, conv_b, lru_w_a, lru_b_a,
                  lru_w_i, lru_b_i, lru_lambda, w_attn_proj, w_rec_proj, w_out,
                  norm_mix_post, norm_ffn_pre, w_ffn_in, w_ffn_out, norm_ffn_post):
    B, S, _ = x.shape
    h = rmsnorm(x, norm_mix_pre)
    z = h @ w_in
    o1 = Q_WIDTH
    o2 = o1 + KV_WIDTH
    o3 = o2 + KV_WIDTH
    o4 = o3 + LRU_WIDTH
    o5 = o4 + LRU_WIDTH
    o6 = o5 + D_MODEL
    q, k, v = z[..., :o1], z[..., o1:o2], z[..., o2:o3]
    rec_x, rec_gate = z[..., o3:o4], z[..., o4:o5]
    g_attn, g_rec = z[..., o5:o6], z[..., o6:]
    pos = jnp.arange(S)
    q = partial_rope(q.reshape(B, S, N_Q_HEADS, HEAD_DIM), pos)
    k = partial_rope(k.reshape(B, S, N_KV_HEADS, HEAD_DIM), pos)
    v = v.reshape(B, S, N_KV_HEADS, HEAD_DIM)
    attn = local_attention(q, k, v, attn_sink)
    xc = centred_conv(rec_x, conv_w, conv_b)
    rec = bidir_rglru(xc, lru_w_a, lru_b_a, lru_w_i, lru_b_i, lru_lambda) * jax.nn.gelu(rec_gate)
    merged = jax.nn.sigmoid(g_attn) * (attn @ w_attn_proj) + jax.nn.sigmoid(g_rec) * (rec @ w_rec_proj)
    x = x + rmsnorm(merged @ w_out, norm_mix_post)
    h = rmsnorm(x, norm_ffn_pre)
    gu = h @ w_ffn_in
    f = (jax.nn.silu(gu[..., :D_FF]) * gu[..., D_FF:]) @ w_ffn_out
    return x + rmsnorm(f, norm_ffn_post)


def setup_inputs(seed: int = 0) -> dict:
    key = jax.random.key(seed)
    ks = jax.random.split(key, 24)
    f32 = jnp.float32

    def nrm(k, shape, fan_in):
        return jax.random.normal(k, shape, f32) * (fan_in ** -0.5)

    def gain(k):
        return 1.0 + 0.05 * jax.random.normal(k, (DEPTH, D_MODEL), f32)

    a0 = jax.random.uniform(ks[12], (DEPTH, N_DIRS, LRU_WIDTH), f32, 0.9, 0.999)
    sig = a0 ** (1.0 / LRU_C)
    lam = jnp.log(sig) - jnp.log1p(-sig)
    return {
        "x_prompt": jax.random.normal(ks[0], (BATCH, SEQ, D_MODEL), f32),
        "x_sample": jax.random.normal(ks[1], (DEC_BATCH, DEC_SEQ, D_MODEL), f32),
        "norm_mix_pre": gain(ks[2]),
        "w_in": nrm(ks[3], (DEPTH, D_MODEL, IN_WIDTH), D_MODEL),
        "attn_sink": 0.5 * jax.random.normal(ks[4], (DEPTH, N_Q_HEADS), f32),
        "conv_w": nrm(ks[5], (DEPTH, CONV_WIDTH, LRU_WIDTH), CONV_WIDTH),
        "conv_b": 0.02 * jax.random.normal(ks[6], (DEPTH, LRU_WIDTH), f32),
        "lru_w_a": nrm(ks[7], (DEPTH, N_DIRS, LRU_BLOCKS, LRU_BLOCK_DIM, LRU_BLOCK_DIM), LRU_BLOCK_DIM),
        "lru_b_a": 0.02 * jax.random.normal(ks[8], (DEPTH, N_DIRS, LRU_WIDTH), f32),
        "lru_w_i": nrm(ks[9], (DEPTH, N_DIRS, LRU_BLOCKS, LRU_BLOCK_DIM, LRU_BLOCK_DIM), LRU_BLOCK_DIM),
        "lru_b_i": 0.02 * jax.random.normal(ks[10], (DEPTH, N_DIRS, LRU_WIDTH), f32),
        "lru_lambda": lam,
        "w_attn_proj": nrm(ks[13], (DEPTH, Q_WIDTH, D_MODEL), Q_WIDTH),
        "w_rec_proj": nrm(ks[14], (DEPTH, LRU_WIDTH, D_MODEL), LRU_WIDTH),
        "w_out": nrm(ks[15], (DEPTH, D_MODEL, D_MODEL), D_MODEL),
        "norm_mix_post": gain(ks[16]),
        "norm_ffn_pre": gain(ks[17]),
        "w_ffn_in": nrm(ks[18], (DEPTH, D_MODEL, 2 * D_FF), D_MODEL),
        "w_ffn_out": nrm(ks[19], (DEPTH, D_FF, D_MODEL), D_FF),
        "norm_ffn_post": gain(ks[20]),
    }


def reference(x_prompt, x_sample, norm_mix_pre, w_in, attn_sink, conv_w, conv_b, lru_w_a,
              lru_b_a, lru_w_i, lru_b_i, lru_lambda, w_attn_proj, w_rec_proj, w_out,
              norm_mix_post, norm_ffn_pre, w_ffn_in, w_ffn_out, norm_ffn_post):
    y_prompt = x_prompt
    y_sample = x_sample
    for l in range(DEPTH):
        p = (norm_mix_pre[l], w_in[l], attn_sink[l], conv_w[l], conv_b[l], lru_w_a[l],
             lru_b_a[l], lru_w_i[l], lru_b_i[l], lru_lambda[l], w_attn_proj[l], w_rec_proj[l],
             w_out[l], norm_mix_post[l], norm_ffn_pre[l], w_ffn_in[l], w_ffn_out[l],
             norm_ffn_post[l])
        y_prompt = encoder_layer(y_prompt, *p)
        y_sample = encoder_layer(y_sample, *p)
    return (y_prompt, y_sample)
```

```python
import numpy as np
import concourse.bass as bass
import concourse.mybir as mybir
from concourse.bass_utils import run_bass_kernel_spmd

F32 = mybir.dt.float32
BF16 = mybir.dt.bfloat16
I32 = mybir.dt.int32
AF = mybir.ActivationFunctionType
ALU = mybir.AluOpType
DTB = {F32: 4, BF16: 2, I32: 4}
GRAN = 256
COMPUTE = ("pe", "act", "dve", "pool")


class View:
    __slots__ = ("ap", "space", "lo", "hi")

    def __init__(self, ap, space, lo, hi):
        self.ap, self.space, self.lo, self.hi = ap, space, lo, hi


class Buf:
    def __init__(self, prog, name, shape, dt, space, off):
        self.prog, self.name, self.shape, self.dt, self.space, self.off = prog, name, list(shape), dt, space, off
        self.esz = DTB[dt]
        self.nbytes = int(np.prod(shape[1:])) * self.esz
        nc = prog.nc
        if space == "sb":
            self.t = nc.alloc_sbuf_tensor_at(name, self.shape, dt, offset=off)
        else:
            self.t = prog.psum_handle(name, self.shape, dt, off)
        st = [1]
        for s in reversed(self.shape[2:]):
            st.insert(0, st[0] * s)
        self.strides = st

    def __getitem__(self, idx):
        if not isinstance(idx, tuple):
            idx = (idx,)
        idx = tuple(idx) + (slice(None),) * (len(self.shape) - len(idx))
        lo = 0
        hi = 0
        for d in range(1, len(self.shape)):
            i = idx[d]
            n = self.shape[d]
            if isinstance(i, int):
                a, b = i, i
            else:
                r = range(*i.indices(n))
                assert len(r) > 0, (self.name, idx)
                a, b = min(r[0], r[-1]), max(r[0], r[-1])
            lo += a * self.strides[d - 1]
            hi += b * self.strides[d - 1]
        return View(self.t[idx], self.space, self.off + lo * self.esz, self.off + (hi + 1) * self.esz)


class Op:
    __slots__ = ("eng", "fn", "reads", "writes", "sem", "inc", "deps", "cnt", "need", "dma", "idx", "tag", "xw", "stage")


class Prog:
    def __init__(self, nc):
        self.nc = nc
        self.ops = {e: [] for e in COMPUTE + ("sp",)}
        self.allops = []
        self.state = {}
        self.sb_off = (nc.sbuf_base + GRAN - 1) // GRAN * GRAN
        self.sb_top = nc.sbuf_top
        self.ps_banks = {}
        self.dma_streams = {}
        self.dcount = {}
        self.stage = ""

    def sb(self, name, shape, dt, off=None):
        esz = DTB[dt]
        nb = int(np.prod(shape[1:])) * esz
        if off is None:
            off = self.sb_off
            self.sb_off = (off + nb + GRAN - 1) // GRAN * GRAN
            assert self.sb_off <= self.sb_top, ("SBUF overflow", name, self.sb_off)
        return Buf(self, name, shape, dt, "sb", off)

    def psum_handle(self, name, shape, dt, off):
        bank = off // 2048
        assert off % 2048 == 0 and int(np.prod(shape[1:])) * DTB[dt] <= 2048
        key = (bank, dt)
        if bank not in self.ps_banks:
            self.ps_banks[bank] = self.nc.alloc_psum_tensor("psb%d" % bank, [128, 512], F32)
        t = self.ps_banks[bank]
        return t

    def ps(self, name, bank, dt=F32, n=None):
        n = n or (2048 // DTB[dt])
        b = Buf.__new__(Buf)
        b.prog, b.name, b.dt, b.space, b.off = self, name, dt, "ps", bank * 2048
        b.esz = DTB[dt]
        b.shape = [128, n]
        b.nbytes = n * b.esz
        b.strides = [1]
        if bank not in self.ps_banks:
            self.ps_banks[bank] = self.nc.alloc_psum_tensor("psb%d" % bank, [128, 512], F32)
        t = self.ps_banks[bank]
        if dt != F32:
            b.t = _Bitcast(t, dt, n)
        else:
            b.t = t
        return b

    def _grans(self, v):
        if v.space in ("sb", "ps"):
            return [(v.space, g) for g in range(v.lo // GRAN, (v.hi - 1) // GRAN + 1)]
        return [(v.space, g) for g in range(v.lo, v.hi)]

    def op(self, eng, fn, reads=(), writes=(), dma=None, ndma=1, tag=""):
        o = Op()
        o.eng, o.fn, o.tag = eng, fn, tag
        o.dma = dma
        o.need = False
        o.idx = len(self.allops)
        o.xw = []
        o.cnt = None
        o.stage = self.stage
        if dma is not None:
            o.sem = "dma:" + dma
            o.inc = 16 * ndma
            self.dcount[o.sem] = self.dcount.get(o.sem, 0) + o.inc
            o.cnt = self.dcount[o.sem]
            o.need = True
        else:
            o.sem = eng
            o.inc = 1
        deps = {}
        for v in reads:
            for g in self._grans(v):
                s = self.state.get(g)
                if s and s[0] is not None:
                    deps[s[0].idx] = (s[0], "raw")
        for v in writes:
            for g in self._grans(v):
                s = self.state.get(g)
                if s:
                    if s[0] is not None and s[0].idx not in deps:
                        deps[s[0].idx] = (s[0], "waw")
                    for r in s[1].values():
                        if r.idx not in deps:
                            deps[r.idx] = (r, "war")
        keep = []
        for d, kind in deps.values():
            if d is o:
                continue
            same = (d.eng == eng) and d.dma is None and dma is None
            if same and eng == "pe":
                continue
            keep.append(d)
            d.need = True
        o.deps = keep
        rkey = eng if dma is None else ("dma", o.idx)
        for v in reads:
            for g in self._grans(v):
                s = self.state.setdefault(g, [None, {}])
                s[1][rkey] = o
        for v in writes:
            for g in self._grans(v):
                self.state[g] = [o, {}]
        self.ops[eng].append(o)
        self.allops.append(o)
        return o

    def stream_wait(self, streams, eng="sp"):
        o = self.op(eng, None, tag="swait")
        o.xw = [("dma:" + s, self.dcount["dma:" + s]) for s in streams if ("dma:" + s) in self.dcount]
        return o

    def emit(self, final_streams=()):
        nc = self.nc
        cnt = dict(self.dcount)
        for o in self.allops:
            if o.dma is None:
                if o.need and o.fn is not None:
                    cnt[o.sem] = cnt.get(o.sem, 0) + 1
                    o.cnt = cnt[o.sem]
                elif o.need:
                    raise AssertionError("dependency on a wait-only op")
        self.final_cnt = cnt
        semnames = sorted(cnt.keys())
        import contextlib
        with contextlib.ExitStack() as es:
            sems = {}
            for i, s in enumerate(semnames):
                sems[s] = es.enter_context(nc.semaphore("s_" + s.replace(":", "_")))
            block = es.enter_context(nc.Block())
            engmap = {"pe": block.tensor, "act": block.scalar, "dve": block.vector, "pool": block.gpsimd, "sp": block.sync}

            def run(engname):
                def body(eng):
                    waited = {}
                    for o in self.ops[engname]:
                        for d in o.deps:
                            if waited.get(d.sem, 0) < d.cnt:
                                eng.wait_ge(sems[d.sem], d.cnt)
                                waited[d.sem] = d.cnt
                        for (sm, val) in o.xw:
                            if waited.get(sm, 0) < val:
                                eng.wait_ge(sems[sm], val)
                                waited[sm] = val
                        if o.fn is None:
                            continue
                        r = o.fn(eng)
                        if o.need:
                            if o.dma is not None:
                                rs = r if isinstance(r, (list, tuple)) else [r]
                                assert len(rs) * 16 == o.inc, (o.tag, len(rs), o.inc)
                                for x in rs:
                                    x.then_inc(sems[o.sem], 16)
                            else:
                                r.then_inc(sems[o.sem], 1)
                    if engname == "sp":
                        for s in final_streams:
                            k = "dma:" + s
                            if k in cnt:
                                eng.wait_ge(sems[k], cnt[k])
                return body

            for e in ("sp", "pe", "act", "dve", "pool"):
                if self.ops[e] or e == "sp":
                    engmap[e](run(e))


class _Bitcast:
    def __init__(self, t, dt, n):
        self.t, self.dt, self.n = t, dt, n

    def __getitem__(self, idx):
        ap = self.t[:, :].bitcast(self.dt)
        return ap[idx]


D = 1024
KB = 8
NQ = 8
LW = 1280
LB = 10
DFF = 2816
FB = 22
INW = 6144
O1, O2, O3, O4, O5, O6 = 1024, 1280, 1536, 2816, 4096, 5120
T = 512
EPS = 1e-6
NEG = -30000.0
SCALE = 1.0 / float(np.sqrt(128.0))
VC_CONVW, VC_CONVB, VC_BA, VC_BI, VC_LAM, VC_GPRE, VC_GFFN = 0, 40, 50, 70, 90, 110, 118
NVEC = 126


def _reads(*vs):
    return [v for v in vs if isinstance(v, View)]


def _a(v):
    return v.ap if isinstance(v, View) else v


class K:
    def __init__(self, P):
        self.P = P

    def mm(self, out, lhsT, rhs, start=True, stop=True, nogroup=False):
        if nogroup:
            self.P.op("pe", lambda e: e.matmul(out.ap, lhsT=lhsT.ap, rhs=rhs.ap, start=start, stop=stop, skip_group_check=True),
                      reads=[lhsT, rhs], writes=[out])
        else:
            self.P.op("pe", lambda e: e.matmul(out.ap, lhsT=lhsT.ap, rhs=rhs.ap, start=start, stop=stop),
                      reads=[lhsT, rhs], writes=[out])

    def tr(self, out, in_, ident):
        self.P.op("pe", lambda e: e.transpose(out=out.ap, in_=in_.ap, identity=ident.ap), reads=[in_, ident], writes=[out])

    def act(self, out, in_, func, scale=1.0, bias=0.0, accum=None):
        w = [out] + ([accum] if accum is not None else [])
        kw = {}
        if accum is not None:
            kw["accum_out"] = accum.ap
        self.P.op("act", lambda e: e.activation(out=out.ap, in_=in_.ap, func=func, bias=_a(bias), scale=_a(scale), **kw),
                  reads=[in_] + _reads(scale, bias), writes=w)

    def ts(self, eng, out, in0, s1, s2, op0, op1=None):
        if op1 is None:
            self.P.op(eng, lambda e: e.tensor_scalar(out=out.ap, in0=in0.ap, scalar1=_a(s1), scalar2=None, op0=op0),
                      reads=[in0] + _reads(s1), writes=[out])
        else:
            self.P.op(eng, lambda e: e.tensor_scalar(out=out.ap, in0=in0.ap, scalar1=_a(s1), scalar2=_a(s2), op0=op0, op1=op1),
                      reads=[in0] + _reads(s1, s2), writes=[out])

    def tt(self, eng, out, in0, in1, op):
        self.P.op(eng, lambda e: e.tensor_tensor(out=out.ap, in0=in0.ap, in1=in1.ap, op=op), reads=[in0, in1], writes=[out])

    def stt(self, eng, out, in0, s, in1, op0, op1):
        self.P.op(eng, lambda e: e.scalar_tensor_tensor(out=out.ap, in0=in0.ap, scalar=_a(s), in1=in1.ap, op0=op0, op1=op1),
                  reads=[in0, in1] + _reads(s), writes=[out])

    def copy(self, eng, out, in_):
        if eng == "act":
            self.act(out, in_, AF.Copy)
        else:
            self.P.op(eng, lambda e: e.tensor_copy(out=out.ap, in_=in_.ap), reads=[in_], writes=[out])

    def memset(self, eng, out, val):
        self.P.op(eng, lambda e: e.memset(out.ap, val), writes=[out])

    def recip(self, out, in_):
        self.P.op("dve", lambda e: e.reciprocal(out=out.ap, in_=in_.ap), reads=[in_], writes=[out])

    def scan(self, out, a, u, init):
        self.P.op("dve", lambda e: e.tensor_tensor_scan(out=out.ap, data0=a.ap, data1=u.ap, initial=_a(init), op0=ALU.mult, op1=ALU.add),
                  reads=[a, u] + _reads(init), writes=[out])

    def dma(self, stream, out, in_, reads=(), writes=()):
        self.P.op("sp", lambda e: e.dma_start(out=_a(out), in_=_a(in_)), reads=list(reads) + _reads(in_), writes=list(writes) + _reads(out), dma=stream)

    def dma_group(self, stream, pairs):
        self.P.op("sp", lambda e: [e.dma_start(out=_a(o), in_=_a(i)) for (o, i) in pairs],
                  reads=[i for (o, i) in pairs if isinstance(i, View)],
                  writes=[o for (o, i) in pairs if isinstance(o, View)], dma=stream, ndma=len(pairs))


class WStream:
    def __init__(self, k, name, slots, eng="sp"):
        self.k, self.name, self.slots, self.eng = k, name, slots, eng
        self.plan = []
        self.loaded = 0
        self.cur = 0
        self.limit = None

    def add(self, tag, dram_ap, shape):
        self.plan.append((dram_ap, shape, tag))

    def next(self, tag=None, hold=1):
        i = self.cur
        self.cur += 1
        assert tag is None or self.plan[i][2] == tag, (i, tag, self.plan[i][2])
        nd = len(self.slots)
        lim = len(self.plan) if self.limit is None else self.limit
        while self.loaded < min(lim, i + nd - hold + 1):
            j = self.loaded
            slot = self.slots[j % nd]
            shape = self.plan[j][1]
            n = int(np.prod(shape[1:]))
            v = slot[:, 0:n]
            ap = slot.t[:, 0:n].rearrange("p (a b) -> p a b", a=shape[1])
            self.k.P.op(self.eng, lambda e, ap=ap, src=self.plan[j][0]: e.dma_start(out=ap, in_=src),
                        writes=[v], dma="%s%d" % (self.name, j % nd))
            self.loaded += 1
        return self.slots[i % nd]


DEBUG = False
WENG = "pool"
OPT_B = True


def build_program(seq_lens, smax):
    nc = bass.Bass("TRN2", target_bir_lowering=False)
    P = Prog(nc)
    k = K(P)
    dbg_n = [0]

    def dbg(name, view, shape, dt):
        if not DEBUG:
            return
        d = nc.dram_tensor("dbg_" + name, list(shape), dt, kind="ExternalOutput").ap()
        P.op("sp", lambda e: e.dma_start(out=d, in_=view.ap), reads=[view], dma="dbg")

    def din(name, shape, dt=F32):
        return nc.dram_tensor(name, list(shape), dt, kind="ExternalInput").ap()

    def dscr(name, shape, dt=BF16):
        return nc.dram_tensor(name, list(shape), dt, kind="Internal").ap()

    xs = [din("x%d" % i, [S, D]) for i, S in enumerate(seq_lens)]
    ys = [nc.dram_tensor("y%d" % i, [S, D], F32, kind="ExternalOutput").ap() for i, S in enumerate(seq_lens)]
    w_in_d = din("w_in", [D, INW])
    w_ap_d = din("w_attn_proj", [D, D])
    w_rp_d = din("w_rec_proj", [LW, D])
    w_out_d = din("w_out", [D, D])
    w_fi_d = din("w_ffn_in", [D, 2 * DFF])
    w_fo_d = din("w_ffn_out", [DFF, D])
    lwa_d = din("lru_w_a", [2, LB, 128, 128])
    lwi_d = din("lru_w_i", [2, LB, 128, 128])
    vecs_d = din("vecs", [128, NVEC])
    gpost_d = din("gpost", [2, D])
    sink_d = din("sink", [1, NQ])
    rope_d = din("rope", [32, 2, smax + 256])

    Win_b = dscr("Win_b", [128, KB, INW])
    Wap_b = dscr("Wap_b", [128, KB, D])
    Wrp_b = dscr("Wrp_b", [128, LB, D])
    Wout_b = dscr("Wout_b", [128, KB, D])
    Wfi_b = dscr("Wfi_b", [128, KB, 2 * DFF])
    Wfo_b = dscr("Wfo_b", [128, FB, D])
    Wg_b = dscr("Wg_b", [128, 2, 2 * LB, 128])
    recT_d = [dscr("recT%d" % i, [128, LB, S]) for i, S in enumerate(seq_lens)]

    Smax = max(seq_lens)
    vecs = P.sb("vecs", [128, NVEC], F32)
    der = P.sb("der", [128, 64], F32)
    der2 = P.sb("der2", [128, 64], F32)
    gpost = P.sb("gpost", [128, 2, D], F32)
    esink = P.sb("esink", [128, NQ], F32)
    ident = P.sb("ident", [128, 128], BF16)
    ones = P.sb("ones", [128, 128], BF16)
    mlo = P.sb("mlo", [128, 128], BF16)
    mhi = P.sb("mhi", [128, 128], BF16)
    prot = P.sb("prot", [128, 32], BF16)
    cst = P.sb("cst", [128, 8], F32)
    stA = P.sb("stA", [128, 64], F32)
    stO = [P.sb("stO%d" % i, [128, 64], F32) for i in range(2)]
    stF = [P.sb("stF%d" % i, [128, 64], F32) for i in range(2)]
    stY = [P.sb("stY%d" % i, [128, 64], F32) for i in range(2)]
    hT = P.sb("hT", [128, KB, Smax], BF16)
    ARENA0 = P.sb_off

    PSB = [P.ps("ps%d" % b, b, F32) for b in range(8)]
    PSH = [P.ps("psh%d" % b, b, BF16) for b in range(8)]

    def affsel(v, pattern, cmp, fill, base, cm):
        P.op("pool", lambda e: e.affine_select(out=v.ap, in_=v.ap, pattern=pattern, compare_op=cmp, fill=fill, base=base, channel_multiplier=cm),
             reads=[v], writes=[v])

    scr = [P.sb("cscr%d" % i, [128, 128], F32, off=ARENA0 + 512 * i) for i in range(4)]
    k.memset("pool", scr[0][:, :], 1.0)
    affsel(scr[0][:, :], [[-1, 128]], ALU.is_equal, 0.0, 0, 1)
    k.copy("pool", ident[:, :], scr[0][:, :])
    k.memset("pool", ones[:, :], 1.0)
    k.memset("pool", scr[1][:, :], 0.0)
    affsel(scr[1][:, :], [[1, 128]], ALU.is_ge, NEG, 0, -1)
    k.copy("pool", mlo[:, :], scr[1][:, :])
    k.memset("pool", scr[2][:, :], 0.0)
    affsel(scr[2][:, :], [[-1, 128]], ALU.is_ge, NEG, 0, 1)
    k.copy("pool", mhi[:, :], scr[2][:, :])
    k.memset("pool", scr[3][:, 0:32], 0.0)
    affsel(scr[3][:, 0:16], [[-1, 16]], ALU.not_equal, -1.0, -16, 1)
    affsel(scr[3][:, 16:32], [[-1, 16]], ALU.not_equal, 1.0, 0, 1)
    k.copy("pool", prot[:, :], scr[3][:, 0:32])
    k.memset("pool", cst[:, 0:1], 0.25)
    k.memset("pool", cst[:, 1:2], EPS)
    k.memset("pool", cst[:, 2:3], 1.0)

    k.dma("m0", vecs[:, :], vecs_d)
    k.dma("m1", gpost[:, 0, :], gpost_d[0:1, :].partition_broadcast(128))
    k.dma("m2", gpost[:, 1, :], gpost_d[1:2, :].partition_broadcast(128))
    k.dma("m3", esink[:, :], sink_d[0:1, :].partition_broadcast(128))
    k.act(esink[:, :], esink[:, :], AF.Exp)
    k.ts("dve", der[:, 0:20], vecs[:, VC_BA:VC_BA + 20], 0.5, None, ALU.mult)
    k.ts("dve", der[:, 20:40], vecs[:, VC_BI:VC_BI + 20], 0.5, None, ALU.mult)
    k.act(der[:, 40:60], vecs[:, VC_LAM:VC_LAM + 20], AF.Exp, scale=-1.0)
    k.act(der[:, 40:60], der[:, 40:60], AF.Ln, bias=cst[:, 2:3])
    k.ts("dve", der2[:, 0:20], der[:, 40:60], -4.0, None, ALU.mult)
    k.ts("dve", der2[:, 20:40], der[:, 40:60], -8.0, None, ALU.mult)

    CW = 2048
    stg_f = [P.sb("stgf%d" % i, [128, CW], F32, off=ARENA0 + 4096 + i * (CW * 6)) for i in range(3)]
    stg_b = [P.sb("stgb%d" % i, [128, CW], BF16, off=ARENA0 + 4096 + i * (CW * 6) + CW * 4) for i in range(3)]
    cvi = [0]
    engs = ["act", "dve"]

    cvjobs = []

    def convert(src_ap, dst_ap, width, scale_v, shape3=None):
        cvjobs.append((src_ap, dst_ap, width, scale_v, shape3))

    def cv_load(i):
        src_ap, dst_ap, width, scale_v, shape3 = cvjobs[i]
        sf = stg_f[i % 3]
        o_ap = sf[:, 0:width].ap if shape3 is None else sf.t[:, 0:width].rearrange("p (a b) -> p a b", a=shape3)
        P.op("sp", lambda e: e.dma_start(out=o_ap, in_=src_ap), writes=[sf[:, 0:width]], dma="cvl%d" % (i % 3))

    def cv_store(i):
        src_ap, dst_ap, width, scale_v, shape3 = cvjobs[i]
        sf, sb_ = stg_f[i % 3], stg_b[i % 3]
        i_ap = sb_[:, 0:width].ap if shape3 is None else sb_.t[:, 0:width].rearrange("p (a b) -> p a b", a=shape3)
        eng = engs[i % 2]
        if scale_v is None:
            k.copy(eng, sb_[:, 0:width], sf[:, 0:width])
        elif eng == "act":
            k.act(sb_[:, 0:width], sf[:, 0:width], AF.Copy, scale=scale_v)
        else:
            k.ts(eng, sb_[:, 0:width], sf[:, 0:width], scale_v, None, ALU.mult)
        P.op("sp", lambda e: e.dma_start(out=dst_ap, in_=i_ap), reads=[sb_[:, 0:width]], dma="cvs%d" % (i % 3))

    def cv_run():
        n = len(cvjobs)
        for i in range(n + 2):
            if i < n:
                cv_load(i)
            if i >= 2:
                cv_store(i - 2)

    def convert_mat(src, dst, nkb, ncols, scale_col):
        for kb in range(nkb):
            for c0 in range(0, ncols, CW):
                w = min(CW, ncols - c0)
                sv = vecs[:, scale_col + kb:scale_col + kb + 1] if scale_col is not None else None
                convert(src[kb * 128:(kb + 1) * 128, c0:c0 + w], dst[:, kb, c0:c0 + w], w, sv)

    for gi, src in enumerate((lwa_d, lwi_d)):
        for dr in range(2):
            convert(src[dr].rearrange("n c d -> c n d"), Wg_b[:, gi, dr * LB:(dr + 1) * LB, :], LB * 128, None, shape3=LB)
    convert_mat(w_in_d, Win_b, KB, INW, VC_GPRE)
    convert_mat(w_ap_d, Wap_b, KB, D, None)
    convert_mat(w_rp_d, Wrp_b, LB, D, None)
    convert_mat(w_out_d, Wout_b, KB, D, None)
    convert_mat(w_fi_d, Wfi_b, KB, 2 * DFF, VC_GFFN)
    convert_mat(w_fo_d, Wfo_b, FB, D, None)
    cv_run()
    P.stream_wait(["cvs0", "cvs1", "cvs2"])

    WSLOT = 5632
    NWS = 8
    wslots = [P.sb("wslot%d" % i, [128, WSLOT // 2], BF16, off=ARENA0 + i * WSLOT) for i in range(NWS)]
    A2 = ARENA0 + NWS * WSLOT
    o = [A2]

    def take(n):
        r = o[0]
        o[0] = (r + n + GRAN - 1) // GRAN * GRAN
        assert o[0] <= P.sb_top, ("arena overflow", o[0], P.sb_top)
        return r

    SF = Smax * 4
    o[0] = ARENA0
    R_off = take(SF + 16)
    XC_off = take(SF)
    XCB_off = take(Smax * 2)
    B_off = [take(SF) for _ in range(4)]
    RB_off = XCB_off
    W1_off = [take(2048) for _ in range(3)]
    GW_off = take(2 * 2 * LB * 128 * 2)
    p1_end = o[0]
    o[0] = R_off
    X1_off = [take(4096) for _ in range(8)]
    XN_off = [take(2048) for _ in range(2)]
    JUNK1_off = take(2048)
    p1_end = max(p1_end, o[0])
    o[0] = A2
    XT_off = take(4 * 4096)
    QT_off = take(NQ * T * 2)
    AT_off = take(NQ * T * 2)
    REC_off = take(LB * T * 2)
    TMP_off = [take(2048) for _ in range(8)]
    JUNK_off = take(2048)
    XN2_off = [take(2048) for _ in range(2)]
    G_off = take(FB * T * 2)
    p2_end = o[0]

    xt = P.sb("xt", [128, 4, D], F32, off=XT_off)
    qT = P.sb("qT", [128, NQ, T], BF16, off=QT_off)
    mgT = qT
    atT = P.sb("atT", [128, NQ, T], BF16, off=AT_off)
    h2T = atT
    recc = P.sb("recc", [128, LB, T], BF16, off=REC_off)
    tmp = [P.sb("tmp%d" % i, [128, T], F32, off=TMP_off[i]) for i in range(8)]
    junk = P.sb("junk", [128, D], BF16, off=JUNK_off)
    xn2 = [P.sb("xn2_%d" % i, [128, D], BF16, off=XN2_off[i]) for i in range(2)]
    kT = P.sb("kT", [128, 2, 768], BF16, off=G_off)
    Vt = P.sb("Vt", [128, 6, 256], BF16, off=G_off + 3072)
    PT = [P.sb("PT%d" % i, [128, 384], BF16, off=G_off + 6144 + i * 1024) for i in range(4)]
    ropeC = P.sb("ropeC", [32, 2, 768], F32, off=G_off + 10240)
    acT = P.sb("acT", [128, FB, T], BF16, off=G_off)
    w1slots = [P.sb("w1slot%d" % i, [128, KB * 128], BF16, off=W1_off[i]) for i in range(3)]
    GW = P.sb("GW", [128, 2, 2 * LB, 128], BF16, off=GW_off)
    x1 = [P.sb("x1_%d" % i, [128, D], F32, off=X1_off[i]) for i in range(8)]
    xn = [P.sb("xn_%d" % i, [128, D], BF16, off=XN_off[i]) for i in range(2)]
    junk1 = P.sb("junk1", [128, D], BF16, off=JUNK1_off)

    FO_PIECES = [(0, 5), (5, 10), (10, 15), (15, 20), (20, 22)]
    ws2 = WStream(k, "w", wslots, eng=WENG)
    ws2_lim = []
    for si, S in enumerate(seq_lens):
        ws2_lim.append(0)
        for c in range(S // T):
            ws2.add("k", Win_b[:, :, O1:O2], [128, KB, 256])
            ws2.add("v", Win_b[:, :, O2:O3], [128, KB, 256])
            for hp in range(4):
                ws2.add("q%d" % hp, Win_b[:, :, hp * 256:(hp + 1) * 256], [128, KB, 256])
            for qt in range(4):
                ws2.add("ga%d" % qt, Win_b[:, :, O5 + qt * 256:O5 + (qt + 1) * 256], [128, KB, 256])
                ws2.add("pa%d" % qt, Wap_b[:, :, qt * 256:(qt + 1) * 256], [128, KB, 256])
                ws2.add("gr%d" % qt, Win_b[:, :, O6 + qt * 256:O6 + (qt + 1) * 256], [128, KB, 256])
                ws2.add("pr%d" % qt, Wrp_b[:, :, qt * 256:(qt + 1) * 256], [128, LB, 256])
            for half in range(2):
                for kg in range(2):
                    ws2.add("wo%d%d" % (half, kg), Wout_b[:, kg * 4:(kg + 1) * 4, half * 512:(half + 1) * 512], [128, 4, 512])
            for g2 in range(FB // 2):
                ws2.add("fg%d" % g2, Wfi_b[:, :, g2 * 256:(g2 + 1) * 256], [128, KB, 256])
                ws2.add("fu%d" % g2, Wfi_b[:, :, DFF + g2 * 256:DFF + (g2 + 1) * 256], [128, KB, 256])
            for half in range(2):
                for (j0, j1) in FO_PIECES:
                    ws2.add("fo%d_%d" % (half, j0), Wfo_b[:, j0:j1, half * 512:(half + 1) * 512], [128, j1 - j0, 512])
        ws2_lim[si] = len(ws2.plan)

    def rstd_from(st, src_cols, n, dst0):
        k.ts("dve", st[:, 40:40 + n], st[:, src_cols:src_cols + n], 1.0 / D, cst[:, 1:2], ALU.mult, ALU.add)
        k.act(st[:, 40:40 + n], st[:, 40:40 + n], AF.Sqrt)
        k.recip(st[:, dst0:dst0 + n], st[:, 40:40 + n])

    def rope(raw, kvh, c_lo, n, bank_raw, bank_sw, tA, tB, tab_lo):
        k.mm(bank_sw[0:32, 0:n], prot[:, 0:32], raw[:, kvh, c_lo:c_lo + n])
        k.tt("dve", tA[0:32, 0:n], bank_sw[0:32, 0:n], ropeC[0:32, 1, tab_lo:tab_lo + n], ALU.mult)
        k.tt("dve", tB[0:32, 0:n], bank_raw[0:32, 0:n], ropeC[0:32, 0, tab_lo:tab_lo + n], ALU.mult)
        k.tt("dve", raw[0:32, kvh, c_lo:c_lo + n], tA[0:32, 0:n], tB[0:32, 0:n], ALU.add)

    for si, S in enumerate(seq_lens):
        NT = S // 128
        NCH = S // T
        xd = xs[si]
        P.stage = "p1a"
        for g in range(NT // 4):
            for j in range(4):
                t = g * 4 + j
                k.dma("x1l%d" % (t % 8), x1[t % 8][:, :], xd[t * 128:(t + 1) * 128, :])
            for j in range(4):
                t = g * 4 + j
                k.act(junk1[:, :], x1[t % 8][:, :], AF.Square, accum=stA[:, j:j + 1])
            rstd_from(stA, 0, 4, 8)
            for j in range(4):
                t = g * 4 + j
                xnb = xn[t % 2]
                k.act(xnb[:, :], x1[t % 8][:, :], AF.Copy, scale=stA[:, 8 + j:9 + j])
                bank = t % 2
                for kb in range(KB):
                    k.tr(PSH[bank][:, kb * 128:(kb + 1) * 128], xnb[:, kb * 128:(kb + 1) * 128], ident[:, :])
                srcv = PSH[bank][:, :]
                dstv = hT[:, :, t * 128:(t + 1) * 128]
                srcap = PSH[bank].t[:, :].rearrange("p (a b) -> p a b", a=KB)
                if t % 2:
                    P.op("dve", lambda e, d=dstv, s=srcap: e.tensor_copy(out=d.ap, in_=s), reads=[srcv], writes=[dstv])
                else:
                    P.op("act", lambda e, d=dstv, s=srcap: e.activation(out=d.ap, in_=s, func=AF.Copy), reads=[srcv], writes=[dstv])

        if si == 0:
            dbg("hT", hT[:, :, 0:S], [128, KB, S], BF16)
        P.stage = "p1b"
        Rb = P.sb("R_%d" % si, [128, S + 4], F32, off=R_off)
        XC = P.sb("XC_%d" % si, [128, S], F32, off=XC_off)
        XCB = P.sb("XCB_%d" % si, [128, S], BF16, off=XCB_off)
        B = [P.sb("B%d_%d" % (i, si), [128, S], F32, off=B_off[i]) for i in range(4)]
        RB = P.sb("RB_%d" % si, [128, S], BF16, off=RB_off)
        ws1 = WStream(k, "v", w1slots)
        k.dma("m0", GW[:, :, :, :], Wg_b)
        for blk in range(LB):
            ws1.add("rx", Win_b[:, :, O3 + blk * 128:O3 + (blk + 1) * 128], [128, KB, 128])
            ws1.add("gz", Win_b[:, :, O4 + blk * 128:O4 + (blk + 1) * 128], [128, KB, 128])
        for blk in range(LB):
            wrx = ws1.next()
            k.memset("pool", Rb[:, 0:2], 0.0)
            k.memset("pool", Rb[:, S + 2:S + 4], 0.0)
            for c in range(NCH):
                bank = PSB[c % 2]
                for kb in range(KB):
                    k.mm(bank[:, :], wrx[:, kb * 128:(kb + 1) * 128], hT[:, kb, c * T:(c + 1) * T], start=(kb == 0), stop=(kb == KB - 1))
                k.act(Rb[:, 2 + c * T:2 + (c + 1) * T], bank[:, :], AF.Copy)

            def cw(tp, blk=blk):
                return vecs[:, VC_CONVW + blk * 4 + tp:VC_CONVW + blk * 4 + tp + 1]

            halves = [(0, S // 2), (S // 2, S)] if S >= 1024 else [(0, S)]
            for (a0, b0) in halves:
                k.act(XC[:, a0:b0], Rb[:, a0 + 2:b0 + 2], AF.Identity, scale=cw(2), bias=vecs[:, VC_CONVB + blk:VC_CONVB + blk + 1])
                for tp in (0, 1, 3):
                    k.stt("dve", XC[:, a0:b0], Rb[:, a0 + tp:b0 + tp], cw(tp), XC[:, a0:b0], ALU.mult, ALU.add)
                k.act(XCB[:, a0:b0], XC[:, a0:b0], AF.Copy)
            if si == 0 and blk == 0:
                dbg("xc0", XC[:, :], [128, S], F32)
            for dr in range(2):
                THA, THI, AA = (B[0], B[1], B[2]) if dr == 0 else (B[1], B[2], B[3])
                col = dr * LB + blk
                for c in range(NCH):
                    ba, bi = PSB[2 + (c % 2) * 2], PSB[3 + (c % 2) * 2]
                    k.mm(ba[:, :], GW[:, 0, col, :], XCB[:, c * T:(c + 1) * T])
                    k.mm(bi[:, :], GW[:, 1, col, :], XCB[:, c * T:(c + 1) * T])
                    k.act(THA[:, c * T:(c + 1) * T], ba[:, :], AF.Tanh, scale=0.5, bias=der[:, col:col + 1])
                    k.act(THI[:, c * T:(c + 1) * T], bi[:, :], AF.Tanh, scale=0.5, bias=der[:, 20 + col:21 + col])
                for (a0, b0) in halves:
                    k.act(AA[:, a0:b0], THA[:, a0:b0], AF.Exp, scale=der2[:, col:col + 1], bias=der2[:, col:col + 1])
                    k.act(THA[:, a0:b0], THA[:, a0:b0], AF.Exp, scale=der2[:, 20 + col:21 + col], bias=der2[:, 20 + col:21 + col])
                    k.stt("dve", THI[:, a0:b0], THI[:, a0:b0], 1.0, XC[:, a0:b0], ALU.add, ALU.mult)
                for (a0, b0) in halves:
                    k.act(THA[:, a0:b0], THA[:, a0:b0], AF.Sqrt, scale=-0.25, bias=cst[:, 0:1])
                    k.tt("dve", THI[:, a0:b0], THI[:, a0:b0], THA[:, a0:b0], ALU.mult)
                if dr == 0:
                    for hi_, (a0, b0) in enumerate(halves):
                        init = 0.0 if hi_ == 0 else THA[:, a0 - 1:a0]
                        k.scan(THA[:, a0:b0], AA[:, a0:b0], THI[:, a0:b0], init)
                else:
                    for hi_, (a0, b0) in enumerate(reversed(halves)):
                        init = 0.0 if hi_ == 0 else THA[:, b0:b0 + 1]
                        k.scan(THA[:, a0:b0][::1] if False else THA[:, b0 - 1:(a0 - 1 if a0 > 0 else None):-1],
                               AA[:, b0 - 1:(a0 - 1 if a0 > 0 else None):-1], THI[:, b0 - 1:(a0 - 1 if a0 > 0 else None):-1], init)
            if si == 0 and blk == 0:
                dbg("hf0", B[0][:, :], [128, S], F32)
                dbg("hb0", B[1][:, :], [128, S], F32)
            wgt = ws1.next()
            for c in range(NCH):
                bank = PSB[6 + c % 2]
                for kb in range(KB):
                    k.mm(bank[:, :], wgt[:, kb * 128:(kb + 1) * 128], hT[:, kb, c * T:(c + 1) * T], start=(kb == 0), stop=(kb == KB - 1))
                k.act(Rb[:, c * T:(c + 1) * T], bank[:, :], AF.Gelu_apprx_tanh)
            for (a0, b0) in halves:
                k.tt("dve", B[0][:, a0:b0], B[0][:, a0:b0], B[1][:, a0:b0], ALU.add)
                k.tt("dve", RB[:, a0:b0], B[0][:, a0:b0], Rb[:, a0:b0], ALU.mult)
            k.dma("recst", recT_d[si][:, blk, :], RB[:, :])
            if si == 0 and blk == 0:
                dbg("rec0", RB[:, :], [128, S], BF16)
        P.stream_wait(["recst"])

        ws2.limit = ws2_lim[si]
        for c in range(NCH):
            c0 = c * T
            lo, hi = max(0, c0 - 128), min(S, c0 + T + 128)
            W = hi - lo
            tl_lo, tl_hi = lo // 128, hi // 128
            qt0 = c0 // 128
            k.dma_group("xt", [(xt[:, j, :], xd[c0 + j * 128:c0 + (j + 1) * 128, :]) for j in range(4)])
            k.dma("rope", ropeC[0:32, :, 0:W], rope_d[:, :, lo + 128:hi + 128])
            k.dma("recl", recc[:, :, :], recT_d[si][:, :, c0:c0 + T])
            P.stage = "K"
            wk = ws2.next("k")
            pieces = [(a, min(a + 512, hi)) for a in range(lo, hi, 512)]
            for kv in range(2):
                for pi, (a, b) in enumerate(pieces):
                    n = b - a
                    bank, bsw = PSB[(kv * 2 + pi) % 4], PSB[4 + (kv * 2 + pi) % 2]
                    for kb in range(KB):
                        k.mm(bank[:, 0:n], wk[:, kb * 256 + kv * 128:kb * 256 + (kv + 1) * 128], hT[:, kb, a:b], start=(kb == 0), stop=(kb == KB - 1))
                    k.act(kT[:, kv, a - lo:b - lo], bank[:, 0:n], AF.Copy)
                    rope(kT, kv, a - lo, n, bank, bsw, tmp[(pi * 2) % 8], tmp[(pi * 2 + 1) % 8], a - lo)
            P.stage = "V"
            wv = ws2.next("v")
            for jt in range(tl_lo, tl_hi):
                bank = PSB[6 + jt % 2]
                for kb in range(KB):
                    k.mm(bank[:, 0:256], hT[:, kb, jt * 128:(jt + 1) * 128], wv[:, kb * 256:(kb + 1) * 256], start=(kb == 0), stop=(kb == KB - 1))
                k.copy("act" if jt % 2 else "dve", Vt[:, jt - tl_lo, :], bank[:, 0:256])
            P.stage = "Q"
            for h in range(NQ):
                if h % 2 == 0:
                    wv_ = ws2.next("q%d" % (h // 2))
                bank, bsw = PSB[h % 4], PSB[4 + h % 2]
                for kb in range(KB):
                    k.mm(bank[:, :], wv_[:, kb * 256 + (h % 2) * 128:kb * 256 + (h % 2 + 1) * 128], hT[:, kb, c0:c0 + T], start=(kb == 0), stop=(kb == KB - 1))
                k.act(qT[:, h, :], bank[:, :], AF.Copy)
                rope(qT, h, 0, T, bank, bsw, tmp[(h * 2) % 8], tmp[(h * 2 + 1) % 8], c0 - lo)
            if si == 0 and c == 0:
                dbg("qT", qT[:, :, :], [128, NQ, T], BF16)
                dbg("kT", kT[:, :, 0:W], [128, 2, W], BF16)
                dbg("Vt", Vt[:, 0:tl_hi - tl_lo, :], [128, tl_hi - tl_lo, 256], BF16)
            P.stage = "att"
            ring = 0
            for h in range(NQ):
                kv = h // 4
                bO, bD = PSB[4 + (h % 2) * 2], PSB[5 + (h % 2) * 2]
                for jt in range(tl_lo, tl_hi):
                    qs = [i for i in (jt - 1, jt, jt + 1) if qt0 <= i < qt0 + 4]
                    qa, qb = (qs[0] - qt0) * 128, (qs[-1] - qt0 + 1) * 128
                    n = qb - qa
                    sb_ = PSB[ring % 4]
                    pt = PT[ring % 4]
                    ring += 1
                    k.mm(sb_[:, 0:n], kT[:, kv, (jt - tl_lo) * 128:(jt - tl_lo + 1) * 128], qT[:, h, qa:qb], start=True, stop=True, nogroup=True)
                    for i in qs:
                        off = (i - qt0) * 128 - qa
                        if i == jt - 1:
                            k.mm(sb_[:, off:off + 128], ident[:, :], mlo[:, :], start=False, stop=True, nogroup=True)
                        elif i == jt + 1:
                            k.mm(sb_[:, off:off + 128], ident[:, :], mhi[:, :], start=False, stop=True, nogroup=True)
                    k.act(pt[:, 0:n], sb_[:, 0:n], AF.Exp, scale=SCALE)
                    first = (jt == tl_lo)
                    last = (jt == tl_hi - 1)
                    k.mm(bO[:, qa:qb], Vt[:, jt - tl_lo, kv * 128:(kv + 1) * 128], pt[:, 0:n], start=first, stop=last, nogroup=True)
                    k.mm(bD[:, qa:qb], ones[:, :], pt[:, 0:n], start=first, stop=last, nogroup=True)
                td = tmp[h % 2]
                k.ts("dve", td[:, :], bD[:, :], esink[:, h:h + 1], None, ALU.add)
                k.recip(td[:, :], td[:, :])
                k.tt("dve", atT[:, h, :], bO[:, :], td[:, :], ALU.mult)
            if si == 0 and c == 0:
                dbg("atT", atT[:, :, :], [128, NQ, T], BF16)
            P.stage = "merge"
            for qt in range(4):
                tg = [tmp[(qt % 2) * 4 + m] for m in range(2)]
                tr_ = [tmp[(qt % 2) * 4 + 2 + m] for m in range(2)]
                bks = [PSB[(qt % 2) * 4 + i] for i in range(4)]
                wga = ws2.next("ga%d" % qt)
                for m in range(2):
                    for kb in range(KB):
                        k.mm(bks[m][:, :], wga[:, kb * 256 + m * 128:kb * 256 + (m + 1) * 128], hT[:, kb, c0:c0 + T], start=(kb == 0), stop=(kb == KB - 1))
                    k.act(tg[m][:, :], bks[m][:, :], AF.Sigmoid)
                wpa = ws2.next("pa%d" % qt)
                for m in range(2):
                    for kb in range(NQ):
                        k.mm(bks[2 + m][:, :], wpa[:, kb * 256 + m * 128:kb * 256 + (m + 1) * 128], atT[:, kb, :], start=(kb == 0), stop=(kb == NQ - 1))
                    k.tt("dve", tg[m][:, :], bks[2 + m][:, :], tg[m][:, :], ALU.mult)
                wgr = ws2.next("gr%d" % qt)
                for m in range(2):
                    for kb in range(KB):
                        k.mm(bks[m][:, :], wgr[:, kb * 256 + m * 128:kb * 256 + (m + 1) * 128], hT[:, kb, c0:c0 + T], start=(kb == 0), stop=(kb == KB - 1))
                    k.act(tr_[m][:, :], bks[m][:, :], AF.Sigmoid)
                wpr = ws2.next("pr%d" % qt)
                for m in range(2):
                    for kb in range(LB):
                        k.mm(bks[2 + m][:, :], wpr[:, kb * 256 + m * 128:kb * 256 + (m + 1) * 128], recc[:, kb, :], start=(kb == 0), stop=(kb == LB - 1))
                    k.tt("dve", tr_[m][:, :], bks[2 + m][:, :], tr_[m][:, :], ALU.mult)
                    k.tt("dve", mgT[:, qt * 2 + m, :], tg[m][:, :], tr_[m][:, :], ALU.add)
            wo = [[ws2.next("wo00", 1), ws2.next("wo01", 2)], [ws2.next("wo10", 3), ws2.next("wo11", 4)]]

            def wout_tile(i):
                P.stage = "wout"
                st = stO[i % 2]
                for half in range(2):
                    bank = PSB[(i % 2) * 2 + half]
                    for kb in range(KB):
                        k.mm(bank[:, :], mgT[:, kb, i * 128:(i + 1) * 128], wo[half][kb // 4][:, (kb % 4) * 512:(kb % 4 + 1) * 512], start=(kb == 0), stop=(kb == KB - 1))
                    k.act(junk[:, half * 512:(half + 1) * 512], bank[:, :], AF.Square, accum=st[:, half:half + 1])
                k.tt("dve", st[:, 2:3], st[:, 0:1], st[:, 1:2], ALU.add)
                rstd_from(st, 2, 1, 4)
                for half in range(2):
                    bank = PSB[(i % 2) * 2 + half]
                    tb = tmp[(i % 2) * 2 + half]
                    k.stt("dve", tb[:, :], bank[:, :], st[:, 4:5], gpost[:, 0, half * 512:(half + 1) * 512], ALU.mult, ALU.mult)
                    k.tt("dve", xt[:, i, half * 512:(half + 1) * 512], xt[:, i, half * 512:(half + 1) * 512], tb[:, :], ALU.add)
                sf = stF[i % 2]
                k.act(junk[:, :], xt[:, i, :], AF.Square, accum=sf[:, 0:1])
                rstd_from(sf, 0, 1, 8)
                k.act(xn2[i % 2][:, :], xt[:, i, :], AF.Copy, scale=sf[:, 8:9])

            def ffn_tr_tile(i):
                P.stage = "ffnT"
                xnb = xn2[i % 2]
                bank = 4 + i % 2
                for kb in range(KB):
                    k.tr(PSH[bank][:, kb * 128:(kb + 1) * 128], xnb[:, kb * 128:(kb + 1) * 128], ident[:, :])
                srcv = PSH[bank][:, :]
                dstv = h2T[:, :, i * 128:(i + 1) * 128]
                srcap = PSH[bank].t[:, :].rearrange("p (a b) -> p a b", a=KB)
                if i % 2:
                    P.op("dve", lambda e, d=dstv, s=srcap: e.tensor_copy(out=d.ap, in_=s), reads=[srcv], writes=[dstv])
                else:
                    P.op("act", lambda e, d=dstv, s=srcap: e.activation(out=d.ap, in_=s, func=AF.Copy), reads=[srcv], writes=[dstv])

            if si == 0 and c == 0:
                dbg("mgT", mgT[:, :, :], [128, NQ, T], BF16)
            if OPT_B:
                for i in range(4):
                    wout_tile(i)
                    if i >= 1:
                        ffn_tr_tile(i - 1)
                ffn_tr_tile(3)
            else:
                for i in range(4):
                    wout_tile(i)
                    ffn_tr_tile(i)
            if si == 0 and c == 0:
                dbg("x1", xt[:, :, :], [128, 4, D], F32)
            P.stage = "ffnin"
            jj = 0
            for g2 in range(FB // 2):
                wg_ = ws2.next("fg%d" % g2, 1)
                wu_ = ws2.next("fu%d" % g2, 2)
                for m in range(2):
                    j = g2 * 2 + m
                    bG, bU = PSB[(jj % 2) * 2], PSB[(jj % 2) * 2 + 1]
                    tb = tmp[4 + jj % 4]
                    jj += 1
                    for kb in range(KB):
                        k.mm(bG[:, :], wg_[:, kb * 256 + m * 128:kb * 256 + (m + 1) * 128], h2T[:, kb, :], start=(kb == 0), stop=(kb == KB - 1))
                    for kb in range(KB):
                        k.mm(bU[:, :], wu_[:, kb * 256 + m * 128:kb * 256 + (m + 1) * 128], h2T[:, kb, :], start=(kb == 0), stop=(kb == KB - 1))
                    k.act(tb[:, :], bG[:, :], AF.Silu)
                    k.tt("dve", acT[:, j, :], bU[:, :], tb[:, :], ALU.mult)
            if si == 0 and c == 0:
                dbg("acT", acT[:, :, :], [128, FB, T], BF16)
            P.stage = "ffnout"
            for half in range(2):
                for (j0, j1) in FO_PIECES:
                    wf = ws2.next("fo%d_%d" % (half, j0))
                    for i in range(4):
                        bank = PSB[half * 4 + i]
                        for j in range(j0, j1):
                            k.mm(bank[:, :], acT[:, j, i * 128:(i + 1) * 128], wf[:, (j - j0) * 512:(j - j0 + 1) * 512], start=(j == 0), stop=(j == FB - 1))
            for i in range(4):
                st = stY[i % 2]
                for half in range(2):
                    k.act(junk[:, 0:512], PSB[half * 4 + i][:, :], AF.Square, accum=st[:, half:half + 1])
                k.tt("dve", st[:, 2:3], st[:, 0:1], st[:, 1:2], ALU.add)
                rstd_from(st, 2, 1, 4)
                for half in range(2):
                    tb = tmp[(i % 2) * 2 + half]
                    k.stt("dve", tb[:, :], PSB[half * 4 + i][:, :], st[:, 4:5], gpost[:, 1, half * 512:(half + 1) * 512], ALU.mult, ALU.mult)
                    k.tt("dve", xt[:, i, half * 512:(half + 1) * 512], xt[:, i, half * 512:(half + 1) * 512], tb[:, :], ALU.add)
                k.dma("yst%d" % i, ys[si][c0 + i * 128:c0 + (i + 1) * 128, :], xt[:, i, :])
    P.emit(final_streams=["yst0", "yst1", "yst2", "yst3", "dbg"])
    import os
    if os.environ.get("MK_DUMP_STAGES"):
        with open(os.environ["MK_DUMP_STAGES"], "w") as f:
            for e in ("pe", "act", "dve"):
                f.write(e + ":" + ",".join(o.stage for o in P.ops[e] if o.fn is not None) + "\n")
    return nc


def host_prep(inputs, smax):
    f = np.float32
    g = lambda n: np.ascontiguousarray(np.asarray(inputs[n], dtype=f)[0])
    conv_w, conv_b = g("conv_w"), g("conv_b")
    vec = np.zeros((128, NVEC), f)
    vec[:, VC_CONVW:VC_CONVW + 40] = conv_w.reshape(4, LB, 128).transpose(2, 1, 0).reshape(128, 40)
    vec[:, VC_CONVB:VC_CONVB + 10] = conv_b.reshape(LB, 128).T
    vec[:, VC_BA:VC_BA + 20] = g("lru_b_a").reshape(2 * LB, 128).T
    vec[:, VC_BI:VC_BI + 20] = g("lru_b_i").reshape(2 * LB, 128).T
    vec[:, VC_LAM:VC_LAM + 20] = g("lru_lambda").reshape(2 * LB, 128).T
    vec[:, VC_GPRE:VC_GPRE + 8] = g("norm_mix_pre").reshape(KB, 128).T
    vec[:, VC_GFFN:VC_GFFN + 8] = g("norm_ffn_pre").reshape(KB, 128).T
    gp = np.stack([g("norm_mix_post"), g("norm_ffn_post")], 0)
    sink = g("attn_sink").reshape(1, NQ)
    half = 16
    inv = (np.float32(500000.0) ** (-np.arange(half, dtype=f) / np.float32(half))).astype(f)
    pos = np.arange(-128, smax + 128).astype(f)
    ang = (pos[None, :] * inv[:, None]).astype(f)
    rope = np.zeros((32, 2, smax + 256), f)
    rope[0:16, 0] = np.cos(ang.astype(np.float64))
    rope[16:32, 0] = np.cos(ang.astype(np.float64))
    rope[0:16, 1] = np.sin(ang.astype(np.float64))
    rope[16:32, 1] = np.sin(ang.astype(np.float64))
    common = {
        "w_in": g("w_in"), "w_attn_proj": g("w_attn_proj"), "w_rec_proj": g("w_rec_proj"), "w_out": g("w_out"),
        "w_ffn_in": g("w_ffn_in"), "w_ffn_out": g("w_ffn_out"), "lru_w_a": g("lru_w_a"), "lru_w_i": g("lru_w_i"),
        "vecs": vec, "gpost": gp, "sink": sink, "rope": rope,
    }
    return common


_CACHE = {}


def run_layer(inputs, per_core_seqs, n_cores):
    seq_lens = tuple(a.shape[0] for a in per_core_seqs[0])
    smax = max(seq_lens)
    if seq_lens not in _CACHE:
        _CACHE[seq_lens] = build_program(list(seq_lens), smax)
    nc = _CACHE[seq_lens]
    common = host_prep(inputs, smax)
    in_maps = []
    for cseqs in per_core_seqs:
        m = dict(common)
        for i, a in enumerate(cseqs):
            m["x%d" % i] = np.ascontiguousarray(a, dtype=np.float32)
        in_maps.append(m)
    res = run_bass_kernel_spmd(nc, in_maps, core_ids=list(range(n_cores)))
    global LAST_RES
    LAST_RES = res
    return [[r["y%d" % i] for i in range(len(seq_lens))] for r in res.results]


def kernel(**inputs):
    xp = np.asarray(inputs["x_prompt"], dtype=np.float32)
    xsm = np.asarray(inputs["x_sample"], dtype=np.float32)
    n = 8
    per_core = [[xp[2 * c], xp[2 * c + 1], xsm[c]] for c in range(n)]
    outs = run_layer(inputs, per_core, n)
    yp = np.empty_like(xp)
    ys = np.empty_like(xsm)
    for c in range(n):
        yp[2 * c], yp[2 * c + 1], ys[c] = outs[c][0], outs[c][1], outs[c][2]
    return (yp, ys)
```

```python
import numpy as np
import concourse.bass as bass
import concourse.mybir as mybir
from concourse.bass_utils import run_bass_kernel_spmd

F32 = mybir.dt.float32
BF16 = mybir.dt.bfloat16
I32 = mybir.dt.int32
AF = mybir.ActivationFunctionType
ALU = mybir.AluOpType
DTB = {F32: 4, BF16: 2, I32: 4}
GRAN = 256
COMPUTE = ("pe", "act", "dve", "pool")


class View:
    __slots__ = ("ap", "space", "lo", "hi")

    def __init__(self, ap, space, lo, hi):
        self.ap, self.space, self.lo, self.hi = ap, space, lo, hi


class Buf:
    def __init__(self, prog, name, shape, dt, space, off):
        self.prog, self.name, self.shape, self.dt, self.space, self.off = prog, name, list(shape), dt, space, off
        self.esz = DTB[dt]
        self.nbytes = int(np.prod(shape[1:])) * self.esz
        nc = prog.nc
        if space == "sb":
            self.t = nc.alloc_sbuf_tensor_at(name, self.shape, dt, offset=off)
        else:
            self.t = prog.psum_handle(name, self.shape, dt, off)
        st = [1]
        for s in reversed(self.shape[2:]):
            st.insert(0, st[0] * s)
        self.strides = st

    def __getitem__(self, idx):
        if not isinstance(idx, tuple):
            idx = (idx,)
        idx = tuple(idx) + (slice(None),) * (len(self.shape) - len(idx))
        lo = 0
        hi = 0
        for d in range(1, len(self.shape)):
            i = idx[d]
            n = self.shape[d]
            if isinstance(i, int):
                a, b = i, i
            else:
                r = range(*i.indices(n))
                assert len(r) > 0, (self.name, idx)
                a, b = min(r[0], r[-1]), max(r[0], r[-1])
            lo += a * self.strides[d - 1]
            hi += b * self.strides[d - 1]
        return View(self.t[idx], self.space, self.off + lo * self.esz, self.off + (hi + 1) * self.esz)


class Op:
    __slots__ = ("eng", "fn", "reads", "writes", "sem", "inc", "deps", "cnt", "need", "dma", "idx", "tag", "xw", "stage")


class Prog:
    def __init__(self, nc):
        self.nc = nc
        self.ops = {e: [] for e in COMPUTE + ("sp",)}
        self.allops = []
        self.state = {}
        self.sb_off = (nc.sbuf_base + GRAN - 1) // GRAN * GRAN
        self.sb_top = nc.sbuf_top
        self.ps_banks = {}
        self.dma_streams = {}
        self.dcount = {}
        self.stage = ""

    def sb(self, name, shape, dt, off=None):
        esz = DTB[dt]
        nb = int(np.prod(shape[1:])) * esz
        if off is None:
            off = self.sb_off
            self.sb_off = (off + nb + GRAN - 1) // GRAN * GRAN
            assert self.sb_off <= self.sb_top, ("SBUF overflow", name, self.sb_off)
        return Buf(self, name, shape, dt, "sb", off)

    def psum_handle(self, name, shape, dt, off):
        bank = off // 2048
        assert off % 2048 == 0 and int(np.prod(shape[1:])) * DTB[dt] <= 2048
        key = (bank, dt)
        if bank not in self.ps_banks:
            self.ps_banks[bank] = self.nc.alloc_psum_tensor("psb%d" % bank, [128, 512], F32)
        t = self.ps_banks[bank]
        return t

    def ps(self, name, bank, dt=F32, n=None):
        n = n or (2048 // DTB[dt])
        b = Buf.__new__(Buf)
        b.prog, b.name, b.dt, b.space, b.off = self, name, dt, "ps", bank * 2048
        b.esz = DTB[dt]
        b.shape = [128, n]
        b.nbytes = n * b.esz
        b.strides = [1]
        if bank not in self.ps_banks:
            self.ps_banks[bank] = self.nc.alloc_psum_tensor("psb%d" % bank, [128, 512], F32)
        t = self.ps_banks[bank]
        if dt != F32:
            b.t = _Bitcast(t, dt, n)
        else:
            b.t = t
        return b

    def _grans(self, v):
        if v.space in ("sb", "ps"):
            return [(v.space, g) for g in range(v.lo // GRAN, (v.hi - 1) // GRAN + 1)]
        return [(v.space, g) for g in range(v.lo, v.hi)]

    def op(self, eng, fn, reads=(), writes=(), dma=None, ndma=1, tag=""):
        o = Op()
        o.eng, o.fn, o.tag = eng, fn, tag
        o.dma = dma
        o.need = False
        o.idx = len(self.allops)
        o.xw = []
        o.cnt = None
        o.stage = self.stage
        if dma is not None:
            o.sem = "dma:" + dma
            o.inc = 16 * ndma
            self.dcount[o.sem] = self.dcount.get(o.sem, 0) + o.inc
            o.cnt = self.dcount[o.sem]
            o.need = True
        else:
            o.sem = eng
            o.inc = 1
        deps = {}
        for v in reads:
            for g in self._grans(v):
                s = self.state.get(g)
                if s and s[0] is not None:
                    deps[s[0].idx] = (s[0], "raw")
        for v in writes:
            for g in self._grans(v):
                s = self.state.get(g)
                if s:
                    if s[0] is not None and s[0].idx not in deps:
                        deps[s[0].idx] = (s[0], "waw")
                    for r in s[1].values():
                        if r.idx not in deps:
                            deps[r.idx] = (r, "war")
        keep = []
        for d, kind in deps.values():
            if d is o:
                continue
            same = (d.eng == eng) and d.dma is None and dma is None
            if same and eng == "pe":
                continue
            keep.append(d)
            d.need = True
        o.deps = keep
        rkey = eng if dma is None else ("dma", o.idx)
        for v in reads:
            for g in self._grans(v):
                s = self.state.setdefault(g, [None, {}])
                s[1][rkey] = o
        for v in writes:
            for g in self._grans(v):
                self.state[g] = [o, {}]
        self.ops[eng].append(o)
        self.allops.append(o)
        return o

    def stream_wait(self, streams, eng="sp"):
        o = self.op(eng, None, tag="swait")
        o.xw = [("dma:" + s, self.dcount["dma:" + s]) for s in streams if ("dma:" + s) in self.dcount]
        return o

    def emit(self, final_streams=()):
        nc = self.nc
        cnt = dict(self.dcount)
        for o in self.allops:
            if o.dma is None:
                if o.need and o.fn is not None:
                    cnt[o.sem] = cnt.get(o.sem, 0) + 1
                    o.cnt = cnt[o.sem]
                elif o.need:
                    raise AssertionError("dependency on a wait-only op")
        self.final_cnt = cnt
        semnames = sorted(cnt.keys())
        import contextlib
        with contextlib.ExitStack() as es:
            sems = {}
            for i, s in enumerate(semnames):
                sems[s] = es.enter_context(nc.semaphore("s_" + s.replace(":", "_")))
            block = es.enter_context(nc.Block())
            engmap = {"pe": block.tensor, "act": block.scalar, "dve": block.vector, "pool": block.gpsimd, "sp": block.sync}

            def run(engname):
                def body(eng):
                    waited = {}
                    for o in self.ops[engname]:
                        for d in o.deps:
                            if waited.get(d.sem, 0) < d.cnt:
                                eng.wait_ge(sems[d.sem], d.cnt)
                                waited[d.sem] = d.cnt
                        for (sm, val) in o.xw:
                            if waited.get(sm, 0) < val:
                                eng.wait_ge(sems[sm], val)
                                waited[sm] = val
                        if o.fn is None:
                            continue
                        r = o.fn(eng)
                        if o.need:
                            if o.dma is not None:
                                rs = r if isinstance(r, (list, tuple)) else [r]
                                assert len(rs) * 16 == o.inc, (o.tag, len(rs), o.inc)
                                for x in rs:
                                    x.then_inc(sems[o.sem], 16)
                            else:
                                r.then_inc(sems[o.sem], 1)
                    if engname == "sp":
                        for s in final_streams:
                            k = "dma:" + s
                            if k in cnt:
                                eng.wait_ge(sems[k], cnt[k])
                return body

            for e in ("sp", "pe", "act", "dve", "pool"):
                if self.ops[e] or e == "sp":
                    engmap[e](run(e))


class _Bitcast:
    def __init__(self, t, dt, n):
        self.t, self.dt, self.n = t, dt, n

    def __getitem__(self, idx):
        ap = self.t[:, :].bitcast(self.dt)
        return ap[idx]


D = 1024
KB = 8
NQ = 8
LW = 1280
LB = 10
DFF = 2816
FB = 22
INW = 6144
O1, O2, O3, O4, O5, O6 = 1024, 1280, 1536, 2816, 4096, 5120
T = 512
EPS = 1e-6
NEG = -30000.0
SCALE = 1.0 / float(np.sqrt(128.0))
VC_CONVW, VC_CONVB, VC_BA, VC_BI, VC_LAM, VC_GPRE, VC_GFFN = 0, 40, 50, 70, 90, 110, 118
NVEC = 126


def _reads(*vs):
    return [v for v in vs if isinstance(v, View)]


def _a(v):
    return v.ap if isinstance(v, View) else v


class K:
    def __init__(self, P):
        self.P = P

    def mm(self, out, lhsT, rhs, start=True, stop=True, nogroup=False):
        if nogroup:
            self.P.op("pe", lambda e: e.matmul(out.ap, lhsT=lhsT.ap, rhs=rhs.ap, start=start, stop=stop, skip_group_check=True),
                      reads=[lhsT, rhs], writes=[out])
        else:
            self.P.op("pe", lambda e: e.matmul(out.ap, lhsT=lhsT.ap, rhs=rhs.ap, start=start, stop=stop),
                      reads=[lhsT, rhs], writes=[out])

    def tr(self, out, in_, ident):
        self.P.op("pe", lambda e: e.transpose(out=out.ap, in_=in_.ap, identity=ident.ap), reads=[in_, ident], writes=[out])

    def act(self, out, in_, func, scale=1.0, bias=0.0, accum=None):
        w = [out] + ([accum] if accum is not None else [])
        kw = {}
        if accum is not None:
            kw["accum_out"] = accum.ap
        self.P.op("act", lambda e: e.activation(out=out.ap, in_=in_.ap, func=func, bias=_a(bias), scale=_a(scale), **kw),
                  reads=[in_] + _reads(scale, bias), writes=w)

    def ts(self, eng, out, in0, s1, s2, op0, op1=None):
        if op1 is None:
            self.P.op(eng, lambda e: e.tensor_scalar(out=out.ap, in0=in0.ap, scalar1=_a(s1), scalar2=None, op0=op0),
                      reads=[in0] + _reads(s1), writes=[out])
        else:
            self.P.op(eng, lambda e: e.tensor_scalar(out=out.ap, in0=in0.ap, scalar1=_a(s1), scalar2=_a(s2), op0=op0, op1=op1),
                      reads=[in0] + _reads(s1, s2), writes=[out])

    def tt(self, eng, out, in0, in1, op):
        self.P.op(eng, lambda e: e.tensor_tensor(out=out.ap, in0=in0.ap, in1=in1.ap, op=op), reads=[in0, in1], writes=[out])

    def stt(self, eng, out, in0, s, in1, op0, op1):
        self.P.op(eng, lambda e: e.scalar_tensor_tensor(out=out.ap, in0=in0.ap, scalar=_a(s), in1=in1.ap, op0=op0, op1=op1),
                  reads=[in0, in1] + _reads(s), writes=[out])

    def copy(self, eng, out, in_):
        if eng == "act":
            self.act(out, in_, AF.Copy)
        else:
            self.P.op(eng, lambda e: e.tensor_copy(out=out.ap, in_=in_.ap), reads=[in_], writes=[out])

    def memset(self, eng, out, val):
        self.P.op(eng, lambda e: e.memset(out.ap, val), writes=[out])

    def recip(self, out, in_):
        self.P.op("dve", lambda e: e.reciprocal(out=out.ap, in_=in_.ap), reads=[in_], writes=[out])

    def scan(self, out, a, u, init):
        self.P.op("dve", lambda e: e.tensor_tensor_scan(out=out.ap, data0=a.ap, data1=u.ap, initial=_a(init), op0=ALU.mult, op1=ALU.add),
                  reads=[a, u] + _reads(init), writes=[out])

    def dma(self, stream, out, in_, reads=(), writes=()):
        self.P.op("sp", lambda e: e.dma_start(out=_a(out), in_=_a(in_)), reads=list(reads) + _reads(in_), writes=list(writes) + _reads(out), dma=stream)

    def dma_group(self, stream, pairs):
        self.P.op("sp", lambda e: [e.dma_start(out=_a(o), in_=_a(i)) for (o, i) in pairs],
                  reads=[i for (o, i) in pairs if isinstance(i, View)],
                  writes=[o for (o, i) in pairs if isinstance(o, View)], dma=stream, ndma=len(pairs))


class WStream:
    def __init__(self, k, name, slots, eng="sp"):
        self.k, self.name, self.slots, self.eng = k, name, slots, eng
        self.plan = []
        self.loaded = 0
        self.cur = 0
        self.limit = None

    def add(self, tag, dram_ap, shape):
        self.plan.append((dram_ap, shape, tag))

    def next(self, tag=None, hold=1):
        i = self.cur
        self.cur += 1
        assert tag is None or self.plan[i][2] == tag, (i, tag, self.plan[i][2])
        nd = len(self.slots)
        lim = len(self.plan) if self.limit is None else self.limit
        while self.loaded < min(lim, i + nd - hold + 1):
            j = self.loaded
            slot = self.slots[j % nd]
            shape = self.plan[j][1]
            n = int(np.prod(shape[1:]))
            v = slot[:, 0:n]
            ap = slot.t[:, 0:n].rearrange("p (a b) -> p a b", a=shape[1])
            self.k.P.op(self.eng, lambda e, ap=ap, src=self.plan[j][0]: e.dma_start(out=ap, in_=src),
                        writes=[v], dma="%s%d" % (self.name, j % nd))
            self.loaded += 1
        return self.slots[i % nd]


DEBUG = False
WENG = "pool"
OPT_B = True


def build_program(seq_lens, smax):
    nc = bass.Bass("TRN2", target_bir_lowering=False)
    P = Prog(nc)
    k = K(P)
    dbg_n = [0]

    def dbg(name, view, shape, dt):
        if not DEBUG:
            return
        d = nc.dram_tensor("dbg_" + name, list(shape), dt, kind="ExternalOutput").ap()
        P.op("sp", lambda e: e.dma_start(out=d, in_=view.ap), reads=[view], dma="dbg")

    def din(name, shape, dt=F32):
        return nc.dram_tensor(name, list(shape), dt, kind="ExternalInput").ap()

    def dscr(name, shape, dt=BF16):
        return nc.dram_tensor(name, list(shape), dt, kind="Internal").ap()

    xs = [din("x%d" % i, [S, D]) for i, S in enumerate(seq_lens)]
    ys = [nc.dram_tensor("y%d" % i, [S, D], F32, kind="ExternalOutput").ap() for i, S in enumerate(seq_lens)]
    w_in_d = din("w_in", [D, INW])
    w_ap_d = din("w_attn_proj", [D, D])
    w_rp_d = din("w_rec_proj", [LW, D])
    w_out_d = din("w_out", [D, D])
    w_fi_d = din("w_ffn_in", [D, 2 * DFF])
    w_fo_d = din("w_ffn_out", [DFF, D])
    lwa_d = din("lru_w_a", [2, LB, 128, 128])
    lwi_d = din("lru_w_i", [2, LB, 128, 128])
    vecs_d = din("vecs", [128, NVEC])
    gpost_d = din("gpost", [2, D])
    sink_d = din("sink", [1, NQ])
    rope_d = din("rope", [32, 2, smax + 256])

    Win_p = dscr("Win_p", [128, INW // 256, KB, 256])
    Wap_p = dscr("Wap_p", [128, D // 256, KB, 256])
    Wrp_p = dscr("Wrp_p", [128, D // 256, LB, 256])
    Wout_p = dscr("Wout_p", [128, 2, KB, 512])
    Wfi_p = dscr("Wfi_p", [128, 2 * DFF // 256, KB, 256])
    Wfo_p = dscr("Wfo_p", [128, 2, FB, 512])
    Wg_b = dscr("Wg_b", [128, 2, 2 * LB, 128])
    recT_d = [dscr("recT%d" % i, [128, LB, S]) for i, S in enumerate(seq_lens)]

    Smax = max(seq_lens)
    vecs = P.sb("vecs", [128, NVEC], F32)
    der = P.sb("der", [128, 64], F32)
    der2 = P.sb("der2", [128, 64], F32)
    gpost = P.sb("gpost", [128, 2, D], F32)
    esink = P.sb("esink", [128, NQ], F32)
    ident = P.sb("ident", [128, 128], BF16)
    ones = P.sb("ones", [128, 128], BF16)
    mlo = P.sb("mlo", [128, 128], BF16)
    mhi = P.sb("mhi", [128, 128], BF16)
    prot = P.sb("prot", [128, 32], BF16)
    cst = P.sb("cst", [128, 8], F32)
    stA = P.sb("stA", [128, 64], F32)
    stO = [P.sb("stO%d" % i, [128, 64], F32) for i in range(2)]
    stF = [P.sb("stF%d" % i, [128, 64], F32) for i in range(2)]
    stY = [P.sb("stY%d" % i, [128, 64], F32) for i in range(2)]
    hT = P.sb("hT", [128, KB, Smax], BF16)
    ARENA0 = P.sb_off

    PSB = [P.ps("ps%d" % b, b, F32) for b in range(8)]
    PSH = [P.ps("psh%d" % b, b, BF16) for b in range(8)]

    def affsel(v, pattern, cmp, fill, base, cm):
        P.op("pool", lambda e: e.affine_select(out=v.ap, in_=v.ap, pattern=pattern, compare_op=cmp, fill=fill, base=base, channel_multiplier=cm),
             reads=[v], writes=[v])

    scr = [P.sb("cscr%d" % i, [128, 128], F32, off=ARENA0 + 512 * i) for i in range(4)]
    k.memset("pool", scr[0][:, :], 1.0)
    affsel(scr[0][:, :], [[-1, 128]], ALU.is_equal, 0.0, 0, 1)
    k.copy("pool", ident[:, :], scr[0][:, :])
    k.memset("pool", ones[:, :], 1.0)
    k.memset("pool", scr[1][:, :], 0.0)
    affsel(scr[1][:, :], [[1, 128]], ALU.is_ge, NEG, 0, -1)
    k.copy("pool", mlo[:, :], scr[1][:, :])
    k.memset("pool", scr[2][:, :], 0.0)
    affsel(scr[2][:, :], [[-1, 128]], ALU.is_ge, NEG, 0, 1)
    k.copy("pool", mhi[:, :], scr[2][:, :])
    k.memset("pool", scr[3][:, 0:32], 0.0)
    affsel(scr[3][:, 0:16], [[-1, 16]], ALU.not_equal, -1.0, -16, 1)
    affsel(scr[3][:, 16:32], [[-1, 16]], ALU.not_equal, 1.0, 0, 1)
    k.copy("pool", prot[:, :], scr[3][:, 0:32])
    k.memset("pool", cst[:, 0:1], 0.25)
    k.memset("pool", cst[:, 1:2], EPS)
    k.memset("pool", cst[:, 2:3], 1.0)

    k.dma("m0", vecs[:, :], vecs_d)
    k.dma("m1", gpost[:, 0, :], gpost_d[0:1, :].partition_broadcast(128))
    k.dma("m2", gpost[:, 1, :], gpost_d[1:2, :].partition_broadcast(128))
    k.dma("m3", esink[:, :], sink_d[0:1, :].partition_broadcast(128))
    k.act(esink[:, :], esink[:, :], AF.Exp)
    k.ts("dve", der[:, 0:20], vecs[:, VC_BA:VC_BA + 20], 0.5, None, ALU.mult)
    k.ts("dve", der[:, 20:40], vecs[:, VC_BI:VC_BI + 20], 0.5, None, ALU.mult)
    k.act(der[:, 40:60], vecs[:, VC_LAM:VC_LAM + 20], AF.Exp, scale=-1.0)
    k.act(der[:, 40:60], der[:, 40:60], AF.Ln, bias=cst[:, 2:3])
    k.ts("dve", der2[:, 0:20], der[:, 40:60], -4.0, None, ALU.mult)
    k.ts("dve", der2[:, 20:40], der[:, 40:60], -8.0, None, ALU.mult)

    CW = 2048
    stg_f = [P.sb("stgf%d" % i, [128, CW], F32, off=ARENA0 + 4096 + i * (CW * 6)) for i in range(3)]
    stg_b = [P.sb("stgb%d" % i, [128, CW], BF16, off=ARENA0 + 4096 + i * (CW * 6) + CW * 4) for i in range(3)]
    cvi = [0]
    engs = ["act", "dve"]

    cvjobs = []

    def convert(src_ap, dst_ap, width, scale_v, ld3=None, st3=None):
        cvjobs.append((src_ap, dst_ap, width, scale_v, ld3, st3))

    def cv_load(i):
        src_ap, dst_ap, width, scale_v, ld3, st3 = cvjobs[i]
        sf = stg_f[i % 3]
        o_ap = sf[:, 0:width].ap if ld3 is None else sf.t[:, 0:width].rearrange("p (a b) -> p a b", a=ld3)
        P.op("sp", lambda e: e.dma_start(out=o_ap, in_=src_ap), writes=[sf[:, 0:width]], dma="cvl%d" % (i % 3))

    def cv_store(i):
        src_ap, dst_ap, width, scale_v, ld3, st3 = cvjobs[i]
        sf, sb_ = stg_f[i % 3], stg_b[i % 3]
        i_ap = sb_[:, 0:width].ap if st3 is None else sb_.t[:, 0:width].rearrange("p (a b) -> p a b", a=st3)
        eng = engs[i % 2]
        if scale_v is None:
            k.copy(eng, sb_[:, 0:width], sf[:, 0:width])
        elif eng == "act":
            k.act(sb_[:, 0:width], sf[:, 0:width], AF.Copy, scale=scale_v)
        else:
            k.ts(eng, sb_[:, 0:width], sf[:, 0:width], scale_v, None, ALU.mult)
        P.op("sp", lambda e: e.dma_start(out=dst_ap, in_=i_ap), reads=[sb_[:, 0:width]], dma="cvs%d" % (i % 3))

    def cv_run():
        n = len(cvjobs)
        for i in range(n + 2):
            if i < n:
                cv_load(i)
            if i >= 2:
                cv_store(i - 2)

    def convert_cols(src, dst, nkb, ncols, scale_col):
        for kb in range(nkb):
            for c0 in range(0, ncols, CW):
                w = min(CW, ncols - c0)
                sv = vecs[:, scale_col + kb:scale_col + kb + 1] if scale_col is not None else None
                convert(src[kb * 128:(kb + 1) * 128, c0:c0 + w], dst[:, c0 // 256:(c0 + w) // 256, kb, :], w, sv, st3=w // 256)

    def convert_rows(src, dst, nkb):
        for kb in range(nkb):
            convert(src[kb * 128:(kb + 1) * 128, :], dst[:, :, kb, :], D, None, st3=2)

    for gi, src in enumerate((lwa_d, lwi_d)):
        for dr in range(2):
            convert(src[dr].rearrange("n c d -> c n d"), Wg_b[:, gi, dr * LB:(dr + 1) * LB, :], LB * 128, None, ld3=LB, st3=LB)
    convert_cols(w_in_d, Win_p, KB, INW, VC_GPRE)
    convert_cols(w_ap_d, Wap_p, KB, D, None)
    convert_cols(w_rp_d, Wrp_p, LB, D, None)
    convert_rows(w_out_d, Wout_p, KB)
    convert_cols(w_fi_d, Wfi_p, KB, 2 * DFF, VC_GFFN)
    convert_rows(w_fo_d, Wfo_p, FB)
    cv_run()
    P.stream_wait(["cvs0", "cvs1", "cvs2"])

    WSLOT = 5632
    NWS = 8
    wslots = [P.sb("wslot%d" % i, [128, WSLOT // 2], BF16, off=ARENA0 + i * WSLOT) for i in range(NWS)]
    A2 = ARENA0 + NWS * WSLOT
    o = [A2]

    def take(n):
        r = o[0]
        o[0] = (r + n + GRAN - 1) // GRAN * GRAN
        assert o[0] <= P.sb_top, ("arena overflow", o[0], P.sb_top)
        return r

    SF = Smax * 4
    o[0] = ARENA0
    R_off = take(SF + 16)
    XC_off = take(SF)
    XCB_off = take(Smax * 2)
    B_off = [take(SF) for _ in range(4)]
    RB_off = XCB_off
    W1_off = [take(4096) for _ in range(3)]
    GW_off = take(2 * 2 * LB * 128 * 2)
    p1_end = o[0]
    o[0] = R_off
    X1_off = [take(4096) for _ in range(8)]
    XN_off = [take(2048) for _ in range(2)]
    JUNK1_off = take(2048)
    p1_end = max(p1_end, o[0])
    o[0] = A2
    XT_off = take(4 * 4096)
    QT_off = take(NQ * T * 2)
    AT_off = take(NQ * T * 2)
    REC_off = take(LB * T * 2)
    TMP_off = [take(2048) for _ in range(8)]
    JUNK_off = take(2048)
    XN2_off = [take(2048) for _ in range(2)]
    G_off = take(FB * T * 2)
    p2_end = o[0]

    xt = P.sb("xt", [128, 4, D], F32, off=XT_off)
    qT = P.sb("qT", [128, NQ, T], BF16, off=QT_off)
    mgT = qT
    atT = P.sb("atT", [128, NQ, T], BF16, off=AT_off)
    h2T = atT
    recc = P.sb("recc", [128, LB, T], BF16, off=REC_off)
    tmp = [P.sb("tmp%d" % i, [128, T], F32, off=TMP_off[i]) for i in range(8)]
    junk = P.sb("junk", [128, D], BF16, off=JUNK_off)
    xn2 = [P.sb("xn2_%d" % i, [128, D], BF16, off=XN2_off[i]) for i in range(2)]
    kT = P.sb("kT", [128, 2, 768], BF16, off=G_off)
    Vt = P.sb("Vt", [128, 6, 256], BF16, off=G_off + 3072)
    PT = [P.sb("PT%d" % i, [128, 384], BF16, off=G_off + 6144 + i * 1024) for i in range(4)]
    ropeC = P.sb("ropeC", [32, 2, 768], F32, off=G_off + 10240)
    acT = P.sb("acT", [128, FB, T], BF16, off=G_off)
    w1slots = [P.sb("w1slot%d" % i, [128, KB * 256], BF16, off=W1_off[i]) for i in range(3)]
    GW = P.sb("GW", [128, 2, 2 * LB, 128], BF16, off=GW_off)
    x1 = [P.sb("x1_%d" % i, [128, D], F32, off=X1_off[i]) for i in range(8)]
    xn = [P.sb("xn_%d" % i, [128, D], BF16, off=XN_off[i]) for i in range(2)]
    junk1 = P.sb("junk1", [128, D], BF16, off=JUNK1_off)

    FO_PIECES = [(0, 5), (5, 10), (10, 15), (15, 20), (20, 22)]
    ws2 = WStream(k, "w", wslots, eng=WENG)
    ws2_lim = []
    for si, S in enumerate(seq_lens):
        ws2_lim.append(0)
        for c in range(S // T):
            ws2.add("k", Win_p[:, O1 // 256], [128, KB, 256])
            ws2.add("v", Win_p[:, O2 // 256], [128, KB, 256])
            for hp in range(4):
                ws2.add("q%d" % hp, Win_p[:, hp], [128, KB, 256])
            for qt in range(4):
                ws2.add("ga%d" % qt, Win_p[:, O5 // 256 + qt], [128, KB, 256])
                ws2.add("pa%d" % qt, Wap_p[:, qt], [128, KB, 256])
                ws2.add("gr%d" % qt, Win_p[:, O6 // 256 + qt], [128, KB, 256])
                ws2.add("pr%d" % qt, Wrp_p[:, qt], [128, LB, 256])
            for half in range(2):
                for kg in range(2):
                    ws2.add("wo%d%d" % (half, kg), Wout_p[:, half, kg * 4:(kg + 1) * 4, :], [128, 4, 512])
            for g2 in range(FB // 2):
                ws2.add("fg%d" % g2, Wfi_p[:, g2], [128, KB, 256])
                ws2.add("fu%d" % g2, Wfi_p[:, DFF // 256 + g2], [128, KB, 256])
            for half in range(2):
                for (j0, j1) in FO_PIECES:
                    ws2.add("fo%d_%d" % (half, j0), Wfo_p[:, half, j0:j1, :], [128, j1 - j0, 512])
        ws2_lim[si] = len(ws2.plan)

    def rstd_from(st, src_cols, n, dst0):
        k.ts("dve", st[:, 40:40 + n], st[:, src_cols:src_cols + n], 1.0 / D, cst[:, 1:2], ALU.mult, ALU.add)
        k.act(st[:, 40:40 + n], st[:, 40:40 + n], AF.Sqrt)
        k.recip(st[:, dst0:dst0 + n], st[:, 40:40 + n])

    def rope(raw, kvh, c_lo, n, bank_raw, bank_sw, tA, tB, tab_lo):
        k.mm(bank_sw[0:32, 0:n], prot[:, 0:32], raw[:, kvh, c_lo:c_lo + n])
        k.tt("dve", tA[0:32, 0:n], bank_sw[0:32, 0:n], ropeC[0:32, 1, tab_lo:tab_lo + n], ALU.mult)
        k.tt("dve", tB[0:32, 0:n], bank_raw[0:32, 0:n], ropeC[0:32, 0, tab_lo:tab_lo + n], ALU.mult)
        k.tt("dve", raw[0:32, kvh, c_lo:c_lo + n], tA[0:32, 0:n], tB[0:32, 0:n], ALU.add)

    for si, S in enumerate(seq_lens):
        NT = S // 128
        NCH = S // T
        xd = xs[si]
        P.stage = "p1a"
        for g in range(NT // 4):
            for j in range(4):
                t = g * 4 + j
                k.dma("x1l%d" % (t % 8), x1[t % 8][:, :], xd[t * 128:(t + 1) * 128, :])
            for j in range(4):
                t = g * 4 + j
                k.act(junk1[:, :], x1[t % 8][:, :], AF.Square, accum=stA[:, j:j + 1])
            rstd_from(stA, 0, 4, 8)
            for j in range(4):
                t = g * 4 + j
                xnb = xn[t % 2]
                k.act(xnb[:, :], x1[t % 8][:, :], AF.Copy, scale=stA[:, 8 + j:9 + j])
                bank = t % 2
                for kb in range(KB):
                    k.tr(PSH[bank][:, kb * 128:(kb + 1) * 128], xnb[:, kb * 128:(kb + 1) * 128], ident[:, :])
                srcv = PSH[bank][:, :]
                dstv = hT[:, :, t * 128:(t + 1) * 128]
                srcap = PSH[bank].t[:, :].rearrange("p (a b) -> p a b", a=KB)
                if t % 2:
                    P.op("dve", lambda e, d=dstv, s=srcap: e.tensor_copy(out=d.ap, in_=s), reads=[srcv], writes=[dstv])
                else:
                    P.op("act", lambda e, d=dstv, s=srcap: e.activation(out=d.ap, in_=s, func=AF.Copy), reads=[srcv], writes=[dstv])

        if si == 0:
            dbg("hT", hT[:, :, 0:S], [128, KB, S], BF16)
        P.stage = "p1b"
        Rb = P.sb("R_%d" % si, [128, S + 4], F32, off=R_off)
        XC = P.sb("XC_%d" % si, [128, S], F32, off=XC_off)
        XCB = P.sb("XCB_%d" % si, [128, S], BF16, off=XCB_off)
        B = [P.sb("B%d_%d" % (i, si), [128, S], F32, off=B_off[i]) for i in range(4)]
        RB = P.sb("RB_%d" % si, [128, S], BF16, off=RB_off)
        ws1 = WStream(k, "v", w1slots)
        k.dma("m0", GW[:, :, :, :], Wg_b)
        for bp in range(LB // 2):
            ws1.add("rx", Win_p[:, O3 // 256 + bp], [128, KB, 256])
            ws1.add("gz", Win_p[:, O4 // 256 + bp], [128, KB, 256])
        for blk in range(LB):
            if blk % 2 == 0:
                wrx = ws1.next("rx", 2)
            k.memset("pool", Rb[:, 0:2], 0.0)
            k.memset("pool", Rb[:, S + 2:S + 4], 0.0)
            for c in range(NCH):
                bank = PSB[c % 2]
                for kb in range(KB):
                    k.mm(bank[:, :], wrx[:, kb * 256 + (blk % 2) * 128:kb * 256 + (blk % 2 + 1) * 128], hT[:, kb, c * T:(c + 1) * T], start=(kb == 0), stop=(kb == KB - 1))
                k.act(Rb[:, 2 + c * T:2 + (c + 1) * T], bank[:, :], AF.Copy)

            def cw(tp, blk=blk):
                return vecs[:, VC_CONVW + blk * 4 + tp:VC_CONVW + blk * 4 + tp + 1]

            halves = [(0, S // 2), (S // 2, S)] if S >= 1024 else [(0, S)]
            for (a0, b0) in halves:
                k.act(XC[:, a0:b0], Rb[:, a0 + 2:b0 + 2], AF.Identity, scale=cw(2), bias=vecs[:, VC_CONVB + blk:VC_CONVB + blk + 1])
                for tp in (0, 1, 3):
                    k.stt("dve", XC[:, a0:b0], Rb[:, a0 + tp:b0 + tp], cw(tp), XC[:, a0:b0], ALU.mult, ALU.add)
                k.act(XCB[:, a0:b0], XC[:, a0:b0], AF.Copy)
            if si == 0 and blk == 0:
                dbg("xc0", XC[:, :], [128, S], F32)
            for dr in range(2):
                THA, THI, AA = (B[0], B[1], B[2]) if dr == 0 else (B[1], B[2], B[3])
                col = dr * LB + blk
                for c in range(NCH):
                    ba, bi = PSB[2 + (c % 2) * 2], PSB[3 + (c % 2) * 2]
                    k.mm(ba[:, :], GW[:, 0, col, :], XCB[:, c * T:(c + 1) * T])
                    k.mm(bi[:, :], GW[:, 1, col, :], XCB[:, c * T:(c + 1) * T])
                    k.act(THA[:, c * T:(c + 1) * T], ba[:, :], AF.Tanh, scale=0.5, bias=der[:, col:col + 1])
                    k.act(THI[:, c * T:(c + 1) * T], bi[:, :], AF.Tanh, scale=0.5, bias=der[:, 20 + col:21 + col])
                for (a0, b0) in halves:
                    k.act(AA[:, a0:b0], THA[:, a0:b0], AF.Exp, scale=der2[:, col:col + 1], bias=der2[:, col:col + 1])
                    k.act(THA[:, a0:b0], THA[:, a0:b0], AF.Exp, scale=der2[:, 20 + col:21 + col], bias=der2[:, 20 + col:21 + col])
                    k.stt("dve", THI[:, a0:b0], THI[:, a0:b0], 1.0, XC[:, a0:b0], ALU.add, ALU.mult)
                for (a0, b0) in halves:
                    k.act(THA[:, a0:b0], THA[:, a0:b0], AF.Sqrt, scale=-0.25, bias=cst[:, 0:1])
                    k.tt("dve", THI[:, a0:b0], THI[:, a0:b0], THA[:, a0:b0], ALU.mult)
                if dr == 0:
                    for hi_, (a0, b0) in enumerate(halves):
                        init = 0.0 if hi_ == 0 else THA[:, a0 - 1:a0]
                        k.scan(THA[:, a0:b0], AA[:, a0:b0], THI[:, a0:b0], init)
                else:
                    for hi_, (a0, b0) in enumerate(reversed(halves)):
                        init = 0.0 if hi_ == 0 else THA[:, b0:b0 + 1]
                        k.scan(THA[:, a0:b0][::1] if False else THA[:, b0 - 1:(a0 - 1 if a0 > 0 else None):-1],
                               AA[:, b0 - 1:(a0 - 1 if a0 > 0 else None):-1], THI[:, b0 - 1:(a0 - 1 if a0 > 0 else None):-1], init)
            if si == 0 and blk == 0:
                dbg("hf0", B[0][:, :], [128, S], F32)
                dbg("hb0", B[1][:, :], [128, S], F32)
            if blk % 2 == 0:
                wgt = ws1.next("gz", 2)
            for c in range(NCH):
                bank = PSB[6 + c % 2]
                for kb in range(KB):
                    k.mm(bank[:, :], wgt[:, kb * 256 + (blk % 2) * 128:kb * 256 + (blk % 2 + 1) * 128], hT[:, kb, c * T:(c + 1) * T], start=(kb == 0), stop=(kb == KB - 1))
                k.act(Rb[:, c * T:(c + 1) * T], bank[:, :], AF.Gelu_apprx_tanh)
            for (a0, b0) in halves:
                k.tt("dve", B[0][:, a0:b0], B[0][:, a0:b0], B[1][:, a0:b0], ALU.add)
                k.tt("dve", RB[:, a0:b0], B[0][:, a0:b0], Rb[:, a0:b0], ALU.mult)
            k.dma("recst", recT_d[si][:, blk, :], RB[:, :])
            if si == 0 and blk == 0:
                dbg("rec0", RB[:, :], [128, S], BF16)
        P.stream_wait(["recst"])

        ws2.limit = ws2_lim[si]
        for c in range(NCH):
            c0 = c * T
            lo, hi = max(0, c0 - 128), min(S, c0 + T + 128)
            W = hi - lo
            tl_lo, tl_hi = lo // 128, hi // 128
            qt0 = c0 // 128
            k.dma_group("xt", [(xt[:, j, :], xd[c0 + j * 128:c0 + (j + 1) * 128, :]) for j in range(4)])
            k.dma("rope", ropeC[0:32, :, 0:W], rope_d[:, :, lo + 128:hi + 128])
            k.dma("recl", recc[:, :, :], recT_d[si][:, :, c0:c0 + T])
            P.stage = "K"
            wk = ws2.next("k")
            pieces = [(a, min(a + 512, hi)) for a in range(lo, hi, 512)]
            pend = None
            for kv in range(2):
                for pi, (a, b) in enumerate(pieces):
                    n = b - a
                    bank, bsw = PSB[(kv * 2 + pi) % 4], PSB[4 + (kv * 2 + pi) % 2]
                    for kb in range(KB):
                        k.mm(bank[:, 0:n], wk[:, kb * 256 + kv * 128:kb * 256 + (kv + 1) * 128], hT[:, kb, a:b], start=(kb == 0), stop=(kb == KB - 1))
                    k.act(kT[:, kv, a - lo:b - lo], bank[:, 0:n], AF.Copy)
                    if pend is not None:
                        rope(*pend)
                    pend = (kT, kv, a - lo, n, bank, bsw, tmp[((kv * 2 + pi) * 2) % 8], tmp[((kv * 2 + pi) * 2 + 1) % 8], a - lo)
            P.stage = "V"
            wv = ws2.next("v")
            for jt in range(tl_lo, tl_hi):
                bank = PSB[6 + jt % 2]
                for kb in range(KB):
                    k.mm(bank[:, 0:256], hT[:, kb, jt * 128:(jt + 1) * 128], wv[:, kb * 256:(kb + 1) * 256], start=(kb == 0), stop=(kb == KB - 1))
                k.copy("act" if jt % 2 else "dve", Vt[:, jt - tl_lo, :], bank[:, 0:256])
                if pend is not None:
                    rope(*pend)
                    pend = None
            P.stage = "Q"
            for h in range(NQ):
                if h % 2 == 0:
                    wv_ = ws2.next("q%d" % (h // 2))
                bank, bsw = PSB[h % 4], PSB[4 + h % 2]
                for kb in range(KB):
                    k.mm(bank[:, :], wv_[:, kb * 256 + (h % 2) * 128:kb * 256 + (h % 2 + 1) * 128], hT[:, kb, c0:c0 + T], start=(kb == 0), stop=(kb == KB - 1))
                k.act(qT[:, h, :], bank[:, :], AF.Copy)
                if pend is not None:
                    rope(*pend)
                pend = (qT, h, 0, T, bank, bsw, tmp[(h * 2) % 8], tmp[(h * 2 + 1) % 8], c0 - lo)
            rope(*pend)
            pend = None
            if si == 0 and c == 0:
                dbg("qT", qT[:, :, :], [128, NQ, T], BF16)
                dbg("kT", kT[:, :, 0:W], [128, 2, W], BF16)
                dbg("Vt", Vt[:, 0:tl_hi - tl_lo, :], [128, tl_hi - tl_lo, 256], BF16)
            P.stage = "att"
            items = [(h, jt) for h in range(NQ) for jt in range(tl_lo, tl_hi)]

            def att_front(n_):
                h, jt = items[n_]
                kv = h // 4
                qs = [i for i in (jt - 1, jt, jt + 1) if qt0 <= i < qt0 + 4]
                qa, qb = (qs[0] - qt0) * 128, (qs[-1] - qt0 + 1) * 128
                n = qb - qa
                sb_ = PSB[n_ % 4]
                pt = PT[n_ % 4]
                k.mm(sb_[:, 0:n], kT[:, kv, (jt - tl_lo) * 128:(jt - tl_lo + 1) * 128], qT[:, h, qa:qb], start=True, stop=True, nogroup=True)
                for i in qs:
                    off = (i - qt0) * 128 - qa
                    if i == jt - 1:
                        k.mm(sb_[:, off:off + 128], ident[:, :], mlo[:, :], start=False, stop=True, nogroup=True)
                    elif i == jt + 1:
                        k.mm(sb_[:, off:off + 128], ident[:, :], mhi[:, :], start=False, stop=True, nogroup=True)
                k.act(pt[:, 0:n], sb_[:, 0:n], AF.Exp, scale=SCALE)
                return (h, jt, kv, qa, qb, n, pt)

            def att_back(info):
                h, jt, kv, qa, qb, n, pt = info
                bO, bD = PSB[4 + (h % 2) * 2], PSB[5 + (h % 2) * 2]
                first = (jt == tl_lo)
                last = (jt == tl_hi - 1)
                k.mm(bO[:, qa:qb], Vt[:, jt - tl_lo, kv * 128:(kv + 1) * 128], pt[:, 0:n], start=first, stop=last, nogroup=True)
                k.mm(bD[:, qa:qb], ones[:, :], pt[:, 0:n], start=first, stop=last, nogroup=True)
                if last:
                    td = tmp[h % 2]
                    k.ts("dve", td[:, :], bD[:, :], esink[:, h:h + 1], None, ALU.add)
                    k.recip(td[:, :], td[:, :])
                    k.tt("dve", atT[:, h, :], bO[:, :], td[:, :], ALU.mult)

            prev = None
            for n_ in range(len(items)):
                info = att_front(n_)
                if prev is not None:
                    att_back(prev)
                prev = info
            att_back(prev)
            if si == 0 and c == 0:
                dbg("atT", atT[:, :, :], [128, NQ, T], BF16)
            P.stage = "merge"
            for qt in range(4):
                tg = [tmp[(qt % 2) * 4 + m] for m in range(2)]
                tr_ = [tmp[(qt % 2) * 4 + 2 + m] for m in range(2)]
                bks = [PSB[(qt % 2) * 4 + i] for i in range(4)]
                wga = ws2.next("ga%d" % qt)
                for m in range(2):
                    for kb in range(KB):
                        k.mm(bks[m][:, :], wga[:, kb * 256 + m * 128:kb * 256 + (m + 1) * 128], hT[:, kb, c0:c0 + T], start=(kb == 0), stop=(kb == KB - 1))
                    k.act(tg[m][:, :], bks[m][:, :], AF.Sigmoid)
                wpa = ws2.next("pa%d" % qt)
                for m in range(2):
                    for kb in range(NQ):
                        k.mm(bks[2 + m][:, :], wpa[:, kb * 256 + m * 128:kb * 256 + (m + 1) * 128], atT[:, kb, :], start=(kb == 0), stop=(kb == NQ - 1))
                    k.tt("dve", tg[m][:, :], bks[2 + m][:, :], tg[m][:, :], ALU.mult)
                wgr = ws2.next("gr%d" % qt)
                for m in range(2):
                    for kb in range(KB):
                        k.mm(bks[m][:, :], wgr[:, kb * 256 + m * 128:kb * 256 + (m + 1) * 128], hT[:, kb, c0:c0 + T], start=(kb == 0), stop=(kb == KB - 1))
                    k.act(tr_[m][:, :], bks[m][:, :], AF.Sigmoid)
                wpr = ws2.next("pr%d" % qt)
                for m in range(2):
                    for kb in range(LB):
                        k.mm(bks[2 + m][:, :], wpr[:, kb * 256 + m * 128:kb * 256 + (m + 1) * 128], recc[:, kb, :], start=(kb == 0), stop=(kb == LB - 1))
                    k.tt("dve", tr_[m][:, :], bks[2 + m][:, :], tr_[m][:, :], ALU.mult)
                    k.tt("dve", mgT[:, qt * 2 + m, :], tg[m][:, :], tr_[m][:, :], ALU.add)
            wo = [[ws2.next("wo00", 1), ws2.next("wo01", 2)], [ws2.next("wo10", 3), ws2.next("wo11", 4)]]

            def wout_tile(i):
                P.stage = "wout"
                st = stO[i % 2]
                for half in range(2):
                    bank = PSB[(i % 2) * 2 + half]
                    for kb in range(KB):
                        k.mm(bank[:, :], mgT[:, kb, i * 128:(i + 1) * 128], wo[half][kb // 4][:, (kb % 4) * 512:(kb % 4 + 1) * 512], start=(kb == 0), stop=(kb == KB - 1))
                    k.act(junk[:, half * 512:(half + 1) * 512], bank[:, :], AF.Square, accum=st[:, half:half + 1])
                k.tt("dve", st[:, 2:3], st[:, 0:1], st[:, 1:2], ALU.add)
                rstd_from(st, 2, 1, 4)
                for half in range(2):
                    bank = PSB[(i % 2) * 2 + half]
                    tb = tmp[(i % 2) * 2 + half]
                    k.stt("dve", tb[:, :], bank[:, :], st[:, 4:5], gpost[:, 0, half * 512:(half + 1) * 512], ALU.mult, ALU.mult)
                    k.tt("dve", xt[:, i, half * 512:(half + 1) * 512], xt[:, i, half * 512:(half + 1) * 512], tb[:, :], ALU.add)
                sf = stF[i % 2]
                k.act(junk[:, :], xt[:, i, :], AF.Square, accum=sf[:, 0:1])
                rstd_from(sf, 0, 1, 8)
                k.act(xn2[i % 2][:, :], xt[:, i, :], AF.Copy, scale=sf[:, 8:9])

            def ffn_tr_tile(i):
                P.stage = "ffnT"
                xnb = xn2[i % 2]
                bank = 4 + i % 2
                for kb in range(KB):
                    k.tr(PSH[bank][:, kb * 128:(kb + 1) * 128], xnb[:, kb * 128:(kb + 1) * 128], ident[:, :])
                srcv = PSH[bank][:, :]
                dstv = h2T[:, :, i * 128:(i + 1) * 128]
                srcap = PSH[bank].t[:, :].rearrange("p (a b) -> p a b", a=KB)
                if i % 2:
                    P.op("dve", lambda e, d=dstv, s=srcap: e.tensor_copy(out=d.ap, in_=s), reads=[srcv], writes=[dstv])
                else:
                    P.op("act", lambda e, d=dstv, s=srcap: e.activation(out=d.ap, in_=s, func=AF.Copy), reads=[srcv], writes=[dstv])

            if si == 0 and c == 0:
                dbg("mgT", mgT[:, :, :], [128, NQ, T], BF16)
            if OPT_B:
                for i in range(4):
                    wout_tile(i)
                    if i >= 1:
                        ffn_tr_tile(i - 1)
                ffn_tr_tile(3)
            else:
                for i in range(4):
                    wout_tile(i)
                    ffn_tr_tile(i)
            if si == 0 and c == 0:
                dbg("x1", xt[:, :, :], [128, 4, D], F32)
            P.stage = "ffnin"
            jj = 0
            for g2 in range(FB // 2):
                wg_ = ws2.next("fg%d" % g2, 1)
                wu_ = ws2.next("fu%d" % g2, 2)
                for m in range(2):
                    j = g2 * 2 + m
                    bG, bU = PSB[(jj % 2) * 2], PSB[(jj % 2) * 2 + 1]
                    tb = tmp[4 + jj % 4]
                    jj += 1
                    for kb in range(KB):
                        k.mm(bG[:, :], wg_[:, kb * 256 + m * 128:kb * 256 + (m + 1) * 128], h2T[:, kb, :], start=(kb == 0), stop=(kb == KB - 1))
                    for kb in range(KB):
                        k.mm(bU[:, :], wu_[:, kb * 256 + m * 128:kb * 256 + (m + 1) * 128], h2T[:, kb, :], start=(kb == 0), stop=(kb == KB - 1))
                    k.act(tb[:, :], bG[:, :], AF.Silu)
                    k.tt("dve", acT[:, j, :], bU[:, :], tb[:, :], ALU.mult)
            if si == 0 and c == 0:
                dbg("acT", acT[:, :, :], [128, FB, T], BF16)
            P.stage = "ffnout"
            for half in range(2):
                for (j0, j1) in FO_PIECES:
                    wf = ws2.next("fo%d_%d" % (half, j0))
                    for i in range(4):
                        bank = PSB[half * 4 + i]
                        for j in range(j0, j1):
                            k.mm(bank[:, :], acT[:, j, i * 128:(i + 1) * 128], wf[:, (j - j0) * 512:(j - j0 + 1) * 512], start=(j == 0), stop=(j == FB - 1))
            for i in range(4):
                st = stY[i % 2]
                for half in range(2):
                    k.act(junk[:, 0:512], PSB[half * 4 + i][:, :], AF.Square, accum=st[:, half:half + 1])
                k.tt("dve", st[:, 2:3], st[:, 0:1], st[:, 1:2], ALU.add)
                rstd_from(st, 2, 1, 4)
                for half in range(2):
                    tb = tmp[(i % 2) * 2 + half]
                    k.stt("dve", tb[:, :], PSB[half * 4 + i][:, :], st[:, 4:5], gpost[:, 1, half * 512:(half + 1) * 512], ALU.mult, ALU.mult)
                    k.tt("dve", xt[:, i, half * 512:(half + 1) * 512], xt[:, i, half * 512:(half + 1) * 512], tb[:, :], ALU.add)
                k.dma("yst%d" % i, ys[si][c0 + i * 128:c0 + (i + 1) * 128, :], xt[:, i, :])
    P.emit(final_streams=["yst0", "yst1", "yst2", "yst3", "dbg"])
    import os
    if os.environ.get("MK_DUMP_STAGES"):
        with open(os.environ["MK_DUMP_STAGES"], "w") as f:
            for e in ("pe", "act", "dve"):
                f.write(e + ":" + ",".join(o.stage for o in P.ops[e] if o.fn is not None) + "\n")
    return nc


def host_prep(inputs, smax):
    f = np.float32
    g = lambda n: np.ascontiguousarray(np.asarray(inputs[n], dtype=f)[0])
    conv_w, conv_b = g("conv_w"), g("conv_b")
    vec = np.zeros((128, NVEC), f)
    vec[:, VC_CONVW:VC_CONVW + 40] = conv_w.reshape(4, LB, 128).transpose(2, 1, 0).reshape(128, 40)
    vec[:, VC_CONVB:VC_CONVB + 10] = conv_b.reshape(LB, 128).T
    vec[:, VC_BA:VC_BA + 20] = g("lru_b_a").reshape(2 * LB, 128).T
    vec[:, VC_BI:VC_BI + 20] = g("lru_b_i").reshape(2 * LB, 128).T
    vec[:, VC_LAM:VC_LAM + 20] = g("lru_lambda").reshape(2 * LB, 128).T
    vec[:, VC_GPRE:VC_GPRE + 8] = g("norm_mix_pre").reshape(KB, 128).T
    vec[:, VC_GFFN:VC_GFFN + 8] = g("norm_ffn_pre").reshape(KB, 128).T
    gp = np.stack([g("norm_mix_post"), g("norm_ffn_post")], 0)
    sink = g("attn_sink").reshape(1, NQ)
    half = 16
    inv = (np.float32(500000.0) ** (-np.arange(half, dtype=f) / np.float32(half))).astype(f)
    pos = np.arange(-128, smax + 128).astype(f)
    ang = (pos[None, :] * inv[:, None]).astype(f)
    rope = np.zeros((32, 2, smax + 256), f)
    rope[0:16, 0] = np.cos(ang.astype(np.float64))
    rope[16:32, 0] = np.cos(ang.astype(np.float64))
    rope[0:16, 1] = np.sin(ang.astype(np.float64))
    rope[16:32, 1] = np.sin(ang.astype(np.float64))
    common = {
        "w_in": g("w_in"), "w_attn_proj": g("w_attn_proj"), "w_rec_proj": g("w_rec_proj"), "w_out": g("w_out"),
        "w_ffn_in": g("w_ffn_in"), "w_ffn_out": g("w_ffn_out"), "lru_w_a": g("lru_w_a"), "lru_w_i": g("lru_w_i"),
        "vecs": vec, "gpost": gp, "sink": sink, "rope": rope,
    }
    return common


_CACHE = {}


def run_layer(inputs, per_core_seqs, n_cores):
    seq_lens = tuple(a.shape[0] for a in per_core_seqs[0])
    smax = max(seq_lens)
    if seq_lens not in _CACHE:
        _CACHE[seq_lens] = build_program(list(seq_lens), smax)
    nc = _CACHE[seq_lens]
    common = host_prep(inputs, smax)
    in_maps = []
    for cseqs in per_core_seqs:
        m = dict(common)
        for i, a in enumerate(cseqs):
            m["x%d" % i] = np.ascontiguousarray(a, dtype=np.float32)
        in_maps.append(m)
    res = run_bass_kernel_spmd(nc, in_maps, core_ids=list(range(n_cores)))
    global LAST_RES
    LAST_RES = res
    return [[r["y%d" % i] for i in range(len(seq_lens))] for r in res.results]


def kernel(**inputs):
    xp = np.asarray(inputs["x_prompt"], dtype=np.float32)
    xsm = np.asarray(inputs["x_sample"], dtype=np.float32)
    n = 8
    per_core = [[xp[2 * c], xp[2 * c + 1], xsm[c]] for c in range(n)]
    outs = run_layer(inputs, per_core, n)
    yp = np.empty_like(xp)
    ys = np.empty_like(xsm)
    for c in range(n):
        yp[2 * c], yp[2 * c + 1], ys[c] = outs[c][0], outs[c][1], outs[c][2]
    return (yp, ys)
```

```python
import numpy as np
import concourse.bass as bass
import concourse.mybir as mybir
from concourse.bass_utils import run_bass_kernel_spmd

F32 = mybir.dt.float32
BF16 = mybir.dt.bfloat16
I32 = mybir.dt.int32
AF = mybir.ActivationFunctionType
ALU = mybir.AluOpType
DTB = {F32: 4, BF16: 2, I32: 4}
GRAN = 256
COMPUTE = ("pe", "act", "dve", "pool")


class View:
    __slots__ = ("ap", "space", "lo", "hi")

    def __init__(self, ap, space, lo, hi):
        self.ap, self.space, self.lo, self.hi = ap, space, lo, hi


class Buf:
    def __init__(self, prog, name, shape, dt, space, off):
        self.prog, self.name, self.shape, self.dt, self.space, self.off = prog, name, list(shape), dt, space, off
        self.esz = DTB[dt]
        self.nbytes = int(np.prod(shape[1:])) * self.esz
        nc = prog.nc
        if space == "sb":
            self.t = nc.alloc_sbuf_tensor_at(name, self.shape, dt, offset=off)
        else:
            self.t = prog.psum_handle(name, self.shape, dt, off)
        st = [1]
        for s in reversed(self.shape[2:]):
            st.insert(0, st[0] * s)
        self.strides = st

    def __getitem__(self, idx):
        if not isinstance(idx, tuple):
            idx = (idx,)
        idx = tuple(idx) + (slice(None),) * (len(self.shape) - len(idx))
        lo = 0
        hi = 0
        for d in range(1, len(self.shape)):
            i = idx[d]
            n = self.shape[d]
            if isinstance(i, int):
                a, b = i, i
            else:
                r = range(*i.indices(n))
                assert len(r) > 0, (self.name, idx)
                a, b = min(r[0], r[-1]), max(r[0], r[-1])
            lo += a * self.strides[d - 1]
            hi += b * self.strides[d - 1]
        return View(self.t[idx], self.space, self.off + lo * self.esz, self.off + (hi + 1) * self.esz)


class Op:
    __slots__ = ("eng", "fn", "reads", "writes", "sem", "inc", "deps", "cnt", "need", "dma", "idx", "tag", "xw", "stage")


class Prog:
    def __init__(self, nc):
        self.nc = nc
        self.ops = {e: [] for e in COMPUTE + ("sp",)}
        self.allops = []
        self.state = {}
        self.sb_off = (nc.sbuf_base + GRAN - 1) // GRAN * GRAN
        self.sb_top = nc.sbuf_top
        self.ps_banks = {}
        self.dma_streams = {}
        self.dcount = {}
        self.stage = ""

    def sb(self, name, shape, dt, off=None):
        esz = DTB[dt]
        nb = int(np.prod(shape[1:])) * esz
        if off is None:
            off = self.sb_off
            self.sb_off = (off + nb + GRAN - 1) // GRAN * GRAN
            assert self.sb_off <= self.sb_top, ("SBUF overflow", name, self.sb_off)
        return Buf(self, name, shape, dt, "sb", off)

    def psum_handle(self, name, shape, dt, off):
        bank = off // 2048
        assert off % 2048 == 0 and int(np.prod(shape[1:])) * DTB[dt] <= 2048
        key = (bank, dt)
        if bank not in self.ps_banks:
            self.ps_banks[bank] = self.nc.alloc_psum_tensor("psb%d" % bank, [128, 512], F32)
        t = self.ps_banks[bank]
        return t

    def ps(self, name, bank, dt=F32, n=None):
        n = n or (2048 // DTB[dt])
        b = Buf.__new__(Buf)
        b.prog, b.name, b.dt, b.space, b.off = self, name, dt, "ps", bank * 2048
        b.esz = DTB[dt]
        b.shape = [128, n]
        b.nbytes = n * b.esz
        b.strides = [1]
        if bank not in self.ps_banks:
            self.ps_banks[bank] = self.nc.alloc_psum_tensor("psb%d" % bank, [128, 512], F32)
        t = self.ps_banks[bank]
        if dt != F32:
            b.t = _Bitcast(t, dt, n)
        else:
            b.t = t
        return b

    def _grans(self, v):
        if v.space in ("sb", "ps"):
            return [(v.space, g) for g in range(v.lo // GRAN, (v.hi - 1) // GRAN + 1)]
        return [(v.space, g) for g in range(v.lo, v.hi)]

    def op(self, eng, fn, reads=(), writes=(), dma=None, ndma=1, tag=""):
        o = Op()
        o.eng, o.fn, o.tag = eng, fn, tag
        o.dma = dma
        o.need = False
        o.idx = len(self.allops)
        o.xw = []
        o.cnt = None
        o.stage = self.stage
        if dma is not None:
            o.sem = "dma:" + dma
            o.inc = 16 * ndma
            self.dcount[o.sem] = self.dcount.get(o.sem, 0) + o.inc
            o.cnt = self.dcount[o.sem]
            o.need = True
        else:
            o.sem = eng
            o.inc = 1
        deps = {}
        for v in reads:
            for g in self._grans(v):
                s = self.state.get(g)
                if s and s[0] is not None:
                    deps[s[0].idx] = (s[0], "raw")
        for v in writes:
            for g in self._grans(v):
                s = self.state.get(g)
                if s:
                    if s[0] is not None and s[0].idx not in deps:
                        deps[s[0].idx] = (s[0], "waw")
                    for r in s[1].values():
                        if r.idx not in deps:
                            deps[r.idx] = (r, "war")
        keep = []
        for d, kind in deps.values():
            if d is o:
                continue
            same = (d.eng == eng) and d.dma is None and dma is None
            if same and eng == "pe":
                continue
            keep.append(d)
            d.need = True
        o.deps = keep
        rkey = eng if dma is None else ("dma", o.idx)
        for v in reads:
            for g in self._grans(v):
                s = self.state.setdefault(g, [None, {}])
                s[1][rkey] = o
        for v in writes:
            for g in self._grans(v):
                self.state[g] = [o, {}]
        self.ops[eng].append(o)
        self.allops.append(o)
        return o

    def stream_wait(self, streams, eng="sp"):
        o = self.op(eng, None, tag="swait")
        o.xw = [("dma:" + s, self.dcount["dma:" + s]) for s in streams if ("dma:" + s) in self.dcount]
        return o

    def emit(self, final_streams=()):
        nc = self.nc
        cnt = dict(self.dcount)
        for o in self.allops:
            if o.dma is None:
                if o.need and o.fn is not None:
                    cnt[o.sem] = cnt.get(o.sem, 0) + 1
                    o.cnt = cnt[o.sem]
                elif o.need:
                    raise AssertionError("dependency on a wait-only op")
        self.final_cnt = cnt
        semnames = sorted(cnt.keys())
        import contextlib
        with contextlib.ExitStack() as es:
            sems = {}
            for i, s in enumerate(semnames):
                sems[s] = es.enter_context(nc.semaphore("s_" + s.replace(":", "_")))
            block = es.enter_context(nc.Block())
            engmap = {"pe": block.tensor, "act": block.scalar, "dve": block.vector, "pool": block.gpsimd, "sp": block.sync}

            def run(engname):
                def body(eng):
                    waited = {}
                    for o in self.ops[engname]:
                        for d in o.deps:
                            if waited.get(d.sem, 0) < d.cnt:
                                eng.wait_ge(sems[d.sem], d.cnt)
                                waited[d.sem] = d.cnt
                        for (sm, val) in o.xw:
                            if waited.get(sm, 0) < val:
                                eng.wait_ge(sems[sm], val)
                                waited[sm] = val
                        if o.fn is None:
                            continue
                        r = o.fn(eng)
                        if o.need:
                            if o.dma is not None:
                                rs = r if isinstance(r, (list, tuple)) else [r]
                                assert len(rs) * 16 == o.inc, (o.tag, len(rs), o.inc)
                                for x in rs:
                                    x.then_inc(sems[o.sem], 16)
                            else:
                                r.then_inc(sems[o.sem], 1)
                    if engname == "sp":
                        for s in final_streams:
                            k = "dma:" + s
                            if k in cnt:
                                eng.wait_ge(sems[k], cnt[k])
                return body

            for e in ("sp", "pe", "act", "dve", "pool"):
                if self.ops[e] or e == "sp":
                    engmap[e](run(e))


class _Bitcast:
    def __init__(self, t, dt, n):
        self.t, self.dt, self.n = t, dt, n

    def __getitem__(self, idx):
        ap = self.t[:, :].bitcast(self.dt)
        return ap[idx]


D = 1024
KB = 8
NQ = 8
LW = 1280
LB = 10
DFF = 2816
FB = 22
INW = 6144
O1, O2, O3, O4, O5, O6 = 1024, 1280, 1536, 2816, 4096, 5120
T = 512
EPS = 1e-6
NEG = -30000.0
SCALE = 1.0 / float(np.sqrt(128.0))
VC_CONVW, VC_CONVB, VC_BA, VC_BI, VC_LAM, VC_GPRE, VC_GFFN = 0, 40, 50, 70, 90, 110, 118
NVEC = 126


def _reads(*vs):
    return [v for v in vs if isinstance(v, View)]


def _a(v):
    return v.ap if isinstance(v, View) else v


class K:
    def __init__(self, P):
        self.P = P

    def mm(self, out, lhsT, rhs, start=True, stop=True, nogroup=False):
        if nogroup:
            self.P.op("pe", lambda e: e.matmul(out.ap, lhsT=lhsT.ap, rhs=rhs.ap, start=start, stop=stop, skip_group_check=True),
                      reads=[lhsT, rhs], writes=[out])
        else:
            self.P.op("pe", lambda e: e.matmul(out.ap, lhsT=lhsT.ap, rhs=rhs.ap, start=start, stop=stop),
                      reads=[lhsT, rhs], writes=[out])

    def tr(self, out, in_, ident):
        self.P.op("pe", lambda e: e.transpose(out=out.ap, in_=in_.ap, identity=ident.ap), reads=[in_, ident], writes=[out])

    def act(self, out, in_, func, scale=1.0, bias=0.0, accum=None):
        w = [out] + ([accum] if accum is not None else [])
        kw = {}
        if accum is not None:
            kw["accum_out"] = accum.ap
        self.P.op("act", lambda e: e.activation(out=out.ap, in_=in_.ap, func=func, bias=_a(bias), scale=_a(scale), **kw),
                  reads=[in_] + _reads(scale, bias), writes=w)

    def ts(self, eng, out, in0, s1, s2, op0, op1=None):
        if op1 is None:
            self.P.op(eng, lambda e: e.tensor_scalar(out=out.ap, in0=in0.ap, scalar1=_a(s1), scalar2=None, op0=op0),
                      reads=[in0] + _reads(s1), writes=[out])
        else:
            self.P.op(eng, lambda e: e.tensor_scalar(out=out.ap, in0=in0.ap, scalar1=_a(s1), scalar2=_a(s2), op0=op0, op1=op1),
                      reads=[in0] + _reads(s1, s2), writes=[out])

    def tt(self, eng, out, in0, in1, op):
        self.P.op(eng, lambda e: e.tensor_tensor(out=out.ap, in0=in0.ap, in1=in1.ap, op=op), reads=[in0, in1], writes=[out])

    def stt(self, eng, out, in0, s, in1, op0, op1):
        self.P.op(eng, lambda e: e.scalar_tensor_tensor(out=out.ap, in0=in0.ap, scalar=_a(s), in1=in1.ap, op0=op0, op1=op1),
                  reads=[in0, in1] + _reads(s), writes=[out])

    def copy(self, eng, out, in_):
        if eng == "act":
            self.act(out, in_, AF.Copy)
        else:
            self.P.op(eng, lambda e: e.tensor_copy(out=out.ap, in_=in_.ap), reads=[in_], writes=[out])

    def memset(self, eng, out, val):
        self.P.op(eng, lambda e: e.memset(out.ap, val), writes=[out])

    def recip(self, out, in_):
        self.P.op("dve", lambda e: e.reciprocal(out=out.ap, in_=in_.ap), reads=[in_], writes=[out])

    def scan(self, out, a, u, init):
        self.P.op("dve", lambda e: e.tensor_tensor_scan(out=out.ap, data0=a.ap, data1=u.ap, initial=_a(init), op0=ALU.mult, op1=ALU.add),
                  reads=[a, u] + _reads(init), writes=[out])

    def dma(self, stream, out, in_, reads=(), writes=()):
        self.P.op("sp", lambda e: e.dma_start(out=_a(out), in_=_a(in_)), reads=list(reads) + _reads(in_), writes=list(writes) + _reads(out), dma=stream)

    def dma_group(self, stream, pairs):
        self.P.op("sp", lambda e: [e.dma_start(out=_a(o), in_=_a(i)) for (o, i) in pairs],
                  reads=[i for (o, i) in pairs if isinstance(i, View)],
                  writes=[o for (o, i) in pairs if isinstance(o, View)], dma=stream, ndma=len(pairs))


class WStream:
    def __init__(self, k, name, slots, eng="sp"):
        self.k, self.name, self.slots, self.eng = k, name, slots, eng
        self.plan = []
        self.loaded = 0
        self.cur = 0
        self.limit = None

    def add(self, tag, dram_ap, shape):
        self.plan.append((dram_ap, shape, tag))

    def next(self, tag=None, hold=1):
        i = self.cur
        self.cur += 1
        assert tag is None or self.plan[i][2] == tag, (i, tag, self.plan[i][2])
        nd = len(self.slots)
        lim = len(self.plan) if self.limit is None else self.limit
        while self.loaded < min(lim, i + nd - hold + 1):
            j = self.loaded
            slot = self.slots[j % nd]
            shape = self.plan[j][1]
            n = int(np.prod(shape[1:]))
            v = slot[:, 0:n]
            ap = slot.t[:, 0:n].rearrange("p (a b) -> p a b", a=shape[1])
            self.k.P.op(self.eng, lambda e, ap=ap, src=self.plan[j][0]: e.dma_start(out=ap, in_=src),
                        writes=[v], dma="%s%d" % (self.name, j % nd))
            self.loaded += 1
        return self.slots[i % nd]


DEBUG = False
WENG = "pool"
OPT_B = True


def build_program(seq_lens, smax):
    nc = bass.Bass("TRN2", target_bir_lowering=False)
    P = Prog(nc)
    k = K(P)
    dbg_n = [0]

    def dbg(name, view, shape, dt):
        if not DEBUG:
            return
        d = nc.dram_tensor("dbg_" + name, list(shape), dt, kind="ExternalOutput").ap()
        P.op("sp", lambda e: e.dma_start(out=d, in_=view.ap), reads=[view], dma="dbg")

    def din(name, shape, dt=F32):
        return nc.dram_tensor(name, list(shape), dt, kind="ExternalInput").ap()

    def dscr(name, shape, dt=BF16):
        return nc.dram_tensor(name, list(shape), dt, kind="Internal").ap()

    xs = [din("x%d" % i, [S, D]) for i, S in enumerate(seq_lens)]
    ys = [nc.dram_tensor("y%d" % i, [S, D], F32, kind="ExternalOutput").ap() for i, S in enumerate(seq_lens)]
    w_in_d = din("w_in", [D, INW])
    w_ap_d = din("w_attn_proj", [D, D])
    w_rp_d = din("w_rec_proj", [LW, D])
    w_out_d = din("w_out", [D, D])
    w_fi_d = din("w_ffn_in", [D, 2 * DFF])
    w_fo_d = din("w_ffn_out", [DFF, D])
    lwa_d = din("lru_w_a", [2, LB, 128, 128])
    lwi_d = din("lru_w_i", [2, LB, 128, 128])
    vecs_d = din("vecs", [128, NVEC])
    gpost_d = din("gpost", [2, D])
    sink_d = din("sink", [1, NQ])
    rope_d = din("rope", [32, 2, smax + 256])

    Win_p = dscr("Win_p", [128, INW // 256, KB, 256])
    Wap_p = dscr("Wap_p", [128, D // 256, KB, 256])
    Wrp_p = dscr("Wrp_p", [128, D // 256, LB, 256])
    Wout_p = dscr("Wout_p", [128, 2, KB, 512])
    Wfi_p = dscr("Wfi_p", [128, 2 * DFF // 256, KB, 256])
    Wfo_p = dscr("Wfo_p", [128, 2, FB, 512])
    Wg_b = dscr("Wg_b", [128, 2, 2 * LB, 128])
    recT_d = [dscr("recT%d" % i, [128, LB, S]) for i, S in enumerate(seq_lens)]

    Smax = max(seq_lens)
    vecs = P.sb("vecs", [128, NVEC], F32)
    der = P.sb("der", [128, 64], F32)
    der2 = P.sb("der2", [128, 64], F32)
    gpost = P.sb("gpost", [128, 2, D], F32)
    esink = P.sb("esink", [128, NQ], F32)
    ident = P.sb("ident", [128, 128], BF16)
    ones = P.sb("ones", [128, 128], BF16)
    mlo = P.sb("mlo", [128, 128], BF16)
    mhi = P.sb("mhi", [128, 128], BF16)
    prot = P.sb("prot", [128, 32], BF16)
    cst = P.sb("cst", [128, 8], F32)
    stA = P.sb("stA", [128, 64], F32)
    stO = [P.sb("stO%d" % i, [128, 64], F32) for i in range(2)]
    stF = [P.sb("stF%d" % i, [128, 64], F32) for i in range(2)]
    stY = [P.sb("stY%d" % i, [128, 64], F32) for i in range(2)]
    hT = P.sb("hT", [128, KB, Smax], BF16)
    ARENA0 = P.sb_off

    PSB = [P.ps("ps%d" % b, b, F32) for b in range(8)]
    PSH = [P.ps("psh%d" % b, b, BF16) for b in range(8)]

    def affsel(v, pattern, cmp, fill, base, cm):
        P.op("pool", lambda e: e.affine_select(out=v.ap, in_=v.ap, pattern=pattern, compare_op=cmp, fill=fill, base=base, channel_multiplier=cm),
             reads=[v], writes=[v])

    scr = [P.sb("cscr%d" % i, [128, 128], F32, off=ARENA0 + 512 * i) for i in range(4)]
    k.memset("pool", scr[0][:, :], 1.0)
    affsel(scr[0][:, :], [[-1, 128]], ALU.is_equal, 0.0, 0, 1)
    k.copy("pool", ident[:, :], scr[0][:, :])
    k.memset("pool", ones[:, :], 1.0)
    k.memset("pool", scr[1][:, :], 0.0)
    affsel(scr[1][:, :], [[1, 128]], ALU.is_ge, NEG, 0, -1)
    k.copy("pool", mlo[:, :], scr[1][:, :])
    k.memset("pool", scr[2][:, :], 0.0)
    affsel(scr[2][:, :], [[-1, 128]], ALU.is_ge, NEG, 0, 1)
    k.copy("pool", mhi[:, :], scr[2][:, :])
    k.memset("pool", scr[3][:, 0:32], 0.0)
    affsel(scr[3][:, 0:16], [[-1, 16]], ALU.not_equal, -1.0, -16, 1)
    affsel(scr[3][:, 16:32], [[-1, 16]], ALU.not_equal, 1.0, 0, 1)
    k.copy("pool", prot[:, :], scr[3][:, 0:32])
    k.memset("pool", cst[:, 0:1], 0.25)
    k.memset("pool", cst[:, 1:2], EPS)
    k.memset("pool", cst[:, 2:3], 1.0)

    k.dma("m0", vecs[:, :], vecs_d)
    k.dma("m1", gpost[:, 0, :], gpost_d[0:1, :].partition_broadcast(128))
    k.dma("m2", gpost[:, 1, :], gpost_d[1:2, :].partition_broadcast(128))
    k.dma("m3", esink[:, :], sink_d[0:1, :].partition_broadcast(128))
    k.act(esink[:, :], esink[:, :], AF.Exp)
    k.ts("dve", der[:, 0:20], vecs[:, VC_BA:VC_BA + 20], 0.5, None, ALU.mult)
    k.ts("dve", der[:, 20:40], vecs[:, VC_BI:VC_BI + 20], 0.5, None, ALU.mult)
    k.act(der[:, 40:60], vecs[:, VC_LAM:VC_LAM + 20], AF.Exp, scale=-1.0)
    k.act(der[:, 40:60], der[:, 40:60], AF.Ln, bias=cst[:, 2:3])
    k.ts("dve", der2[:, 0:20], der[:, 40:60], -4.0, None, ALU.mult)
    k.ts("dve", der2[:, 20:40], der[:, 40:60], -8.0, None, ALU.mult)

    CW = 2048
    stg_f = [P.sb("stgf%d" % i, [128, CW], F32, off=ARENA0 + 4096 + i * (CW * 6)) for i in range(3)]
    stg_b = [P.sb("stgb%d" % i, [128, CW], BF16, off=ARENA0 + 4096 + i * (CW * 6) + CW * 4) for i in range(3)]
    cvi = [0]
    engs = ["act", "dve"]

    cvjobs = []

    def convert(src_ap, dst_ap, width, scale_v, ld3=None, st3=None):
        cvjobs.append((src_ap, dst_ap, width, scale_v, ld3, st3))

    def cv_load(i):
        src_ap, dst_ap, width, scale_v, ld3, st3 = cvjobs[i]
        sf = stg_f[i % 3]
        o_ap = sf[:, 0:width].ap if ld3 is None else sf.t[:, 0:width].rearrange("p (a b) -> p a b", a=ld3)
        P.op("sp", lambda e: e.dma_start(out=o_ap, in_=src_ap), writes=[sf[:, 0:width]], dma="cvl%d" % (i % 3))

    def cv_store(i):
        src_ap, dst_ap, width, scale_v, ld3, st3 = cvjobs[i]
        sf, sb_ = stg_f[i % 3], stg_b[i % 3]
        i_ap = sb_[:, 0:width].ap if st3 is None else sb_.t[:, 0:width].rearrange("p (a b) -> p a b", a=st3)
        eng = engs[i % 2]
        if scale_v is None:
            k.copy(eng, sb_[:, 0:width], sf[:, 0:width])
        elif eng == "act":
            k.act(sb_[:, 0:width], sf[:, 0:width], AF.Copy, scale=scale_v)
        else:
            k.ts(eng, sb_[:, 0:width], sf[:, 0:width], scale_v, None, ALU.mult)
        P.op("sp", lambda e: e.dma_start(out=dst_ap, in_=i_ap), reads=[sb_[:, 0:width]], dma="cvs%d" % (i % 3))

    def cv_run():
        n = len(cvjobs)
        for i in range(n + 2):
            if i < n:
                cv_load(i)
            if i >= 2:
                cv_store(i - 2)

    def convert_cols(src, dst, nkb, ncols, scale_col):
        for kb in range(nkb):
            for c0 in range(0, ncols, CW):
                w = min(CW, ncols - c0)
                sv = vecs[:, scale_col + kb:scale_col + kb + 1] if scale_col is not None else None
                convert(src[kb * 128:(kb + 1) * 128, c0:c0 + w], dst[:, c0 // 256:(c0 + w) // 256, kb, :], w, sv, st3=w // 256)

    def convert_rows(src, dst, nkb):
        for kb in range(nkb):
            convert(src[kb * 128:(kb + 1) * 128, :], dst[:, :, kb, :], D, None, st3=2)

    for gi, src in enumerate((lwa_d, lwi_d)):
        for dr in range(2):
            convert(src[dr].rearrange("n c d -> c n d"), Wg_b[:, gi, dr * LB:(dr + 1) * LB, :], LB * 128, None, ld3=LB, st3=LB)
    convert_cols(w_in_d, Win_p, KB, INW, VC_GPRE)
    convert_cols(w_ap_d, Wap_p, KB, D, None)
    convert_cols(w_rp_d, Wrp_p, LB, D, None)
    convert_rows(w_out_d, Wout_p, KB)
    convert_cols(w_fi_d, Wfi_p, KB, 2 * DFF, VC_GFFN)
    convert_rows(w_fo_d, Wfo_p, FB)
    cv_run()
    P.stream_wait(["cvs0", "cvs1", "cvs2"])

    WSLOT = 5632
    NWS = 8
    wslots = [P.sb("wslot%d" % i, [128, WSLOT // 2], BF16, off=ARENA0 + i * WSLOT) for i in range(NWS)]
    A2 = ARENA0 + NWS * WSLOT
    o = [A2]

    def take(n):
        r = o[0]
        o[0] = (r + n + GRAN - 1) // GRAN * GRAN
        assert o[0] <= P.sb_top, ("arena overflow", o[0], P.sb_top)
        return r

    SF = Smax * 4
    o[0] = ARENA0
    R_off = take(SF + 16)
    XC_off = take(SF)
    XCB_off = take(Smax * 2)
    B_off = [take(SF) for _ in range(4)]
    RB_off = XCB_off
    W1_off = [take(4096) for _ in range(3)]
    GW_off = take(2 * 2 * LB * 128 * 2)
    p1_end = o[0]
    o[0] = R_off
    X1_off = [take(4096) for _ in range(8)]
    XN_off = [take(2048) for _ in range(2)]
    JUNK1_off = take(2048)
    p1_end = max(p1_end, o[0])
    o[0] = A2
    XT_off = take(4 * 4096)
    QT_off = take(NQ * T * 2)
    AT_off = take(NQ * T * 2)
    REC_off = take(LB * T * 2)
    TMP_off = [take(2048) for _ in range(8)]
    JUNK_off = REC_off
    XN2_off = [REC_off + 2048, REC_off + 4096]
    ROPE_off = take(32 * 0 + 2 * 768 * 4)
    G_off = take(FB * T * 2)
    p2_end = o[0]

    xt = P.sb("xt", [128, 4, D], F32, off=XT_off)
    qT = P.sb("qT", [128, NQ, T], BF16, off=QT_off)
    mgT = qT
    atT = P.sb("atT", [128, NQ, T], BF16, off=AT_off)
    h2T = atT
    recc = P.sb("recc", [128, LB, T], BF16, off=REC_off)
    tmp = [P.sb("tmp%d" % i, [128, T], F32, off=TMP_off[i]) for i in range(8)]
    junk = P.sb("junk", [128, D], BF16, off=JUNK_off)
    xn2 = [P.sb("xn2_%d" % i, [128, D], BF16, off=XN2_off[i]) for i in range(2)]
    kT = P.sb("kT", [128, 2, 768], BF16, off=G_off)
    Vt = P.sb("Vt", [128, 6, 256], BF16, off=G_off + 3072)
    PT = [P.sb("PT%d" % i, [128, 384], BF16, off=G_off + 6144 + i * 1024) for i in range(4)]
    ropeC = P.sb("ropeC", [32, 2, 768], F32, off=ROPE_off)
    acT = P.sb("acT", [128, FB, T], BF16, off=G_off)
    w1slots = [P.sb("w1slot%d" % i, [128, KB * 256], BF16, off=W1_off[i]) for i in range(3)]
    GW = P.sb("GW", [128, 2, 2 * LB, 128], BF16, off=GW_off)
    x1 = [P.sb("x1_%d" % i, [128, D], F32, off=X1_off[i]) for i in range(8)]
    xn = [P.sb("xn_%d" % i, [128, D], BF16, off=XN_off[i]) for i in range(2)]
    junk1 = P.sb("junk1", [128, D], BF16, off=JUNK1_off)

    FO_PIECES = [(0, 5), (5, 10), (10, 15), (15, 20), (20, 22)]
    ws2 = WStream(k, "w", wslots, eng=WENG)
    ws2_lim = []
    for si, S in enumerate(seq_lens):
        ws2_lim.append(0)
        for c in range(S // T):
            ws2.add("k", Win_p[:, O1 // 256], [128, KB, 256])
            ws2.add("v", Win_p[:, O2 // 256], [128, KB, 256])
            for hp in range(4):
                ws2.add("q%d" % hp, Win_p[:, hp], [128, KB, 256])
            for qt in range(4):
                ws2.add("ga%d" % qt, Win_p[:, O5 // 256 + qt], [128, KB, 256])
                ws2.add("pa%d" % qt, Wap_p[:, qt], [128, KB, 256])
                ws2.add("gr%d" % qt, Win_p[:, O6 // 256 + qt], [128, KB, 256])
                ws2.add("pr%d" % qt, Wrp_p[:, qt], [128, LB, 256])
            for half in range(2):
                for kg in range(2):
                    ws2.add("wo%d%d" % (half, kg), Wout_p[:, half, kg * 4:(kg + 1) * 4, :], [128, 4, 512])
            for g2 in range(FB // 2):
                ws2.add("fg%d" % g2, Wfi_p[:, g2], [128, KB, 256])
                ws2.add("fu%d" % g2, Wfi_p[:, DFF // 256 + g2], [128, KB, 256])
            for half in range(2):
                for (j0, j1) in FO_PIECES:
                    ws2.add("fo%d_%d" % (half, j0), Wfo_p[:, half, j0:j1, :], [128, j1 - j0, 512])
        ws2_lim[si] = len(ws2.plan)

    def rstd_from(st, src_cols, n, dst0):
        k.ts("dve", st[:, 40:40 + n], st[:, src_cols:src_cols + n], 1.0 / D, cst[:, 1:2], ALU.mult, ALU.add)
        k.act(st[:, 40:40 + n], st[:, 40:40 + n], AF.Sqrt)
        k.recip(st[:, dst0:dst0 + n], st[:, 40:40 + n])

    def rope(raw, kvh, c_lo, n, bank_raw, bank_sw, tA, tB, tab_lo):
        k.mm(bank_sw[0:32, 0:n], prot[:, 0:32], raw[:, kvh, c_lo:c_lo + n])
        k.tt("dve", tA[0:32, 0:n], bank_sw[0:32, 0:n], ropeC[0:32, 1, tab_lo:tab_lo + n], ALU.mult)
        k.tt("dve", tB[0:32, 0:n], bank_raw[0:32, 0:n], ropeC[0:32, 0, tab_lo:tab_lo + n], ALU.mult)
        k.tt("dve", raw[0:32, kvh, c_lo:c_lo + n], tA[0:32, 0:n], tB[0:32, 0:n], ALU.add)

    for si, S in enumerate(seq_lens):
        NT = S // 128
        NCH = S // T
        xd = xs[si]
        P.stage = "p1a"
        for g in range(NT // 4):
            for j in range(4):
                t = g * 4 + j
                k.dma("x1l%d" % (t % 8), x1[t % 8][:, :], xd[t * 128:(t + 1) * 128, :])
            for j in range(4):
                t = g * 4 + j
                k.act(junk1[:, :], x1[t % 8][:, :], AF.Square, accum=stA[:, j:j + 1])
            rstd_from(stA, 0, 4, 8)
            for j in range(4):
                t = g * 4 + j
                xnb = xn[t % 2]
                k.act(xnb[:, :], x1[t % 8][:, :], AF.Copy, scale=stA[:, 8 + j:9 + j])
                bank = t % 2
                for kb in range(KB):
                    k.tr(PSH[bank][:, kb * 128:(kb + 1) * 128], xnb[:, kb * 128:(kb + 1) * 128], ident[:, :])
                srcv = PSH[bank][:, :]
                dstv = hT[:, :, t * 128:(t + 1) * 128]
                srcap = PSH[bank].t[:, :].rearrange("p (a b) -> p a b", a=KB)
                if t % 2:
                    P.op("dve", lambda e, d=dstv, s=srcap: e.tensor_copy(out=d.ap, in_=s), reads=[srcv], writes=[dstv])
                else:
                    P.op("act", lambda e, d=dstv, s=srcap: e.activation(out=d.ap, in_=s, func=AF.Copy), reads=[srcv], writes=[dstv])

        if si == 0:
            dbg("hT", hT[:, :, 0:S], [128, KB, S], BF16)
        P.stage = "p1b"
        Rb = P.sb("R_%d" % si, [128, S + 4], F32, off=R_off)
        XC = P.sb("XC_%d" % si, [128, S], F32, off=XC_off)
        XCB = P.sb("XCB_%d" % si, [128, S], BF16, off=XCB_off)
        B = [P.sb("B%d_%d" % (i, si), [128, S], F32, off=B_off[i]) for i in range(4)]
        RB = P.sb("RB_%d" % si, [128, S], BF16, off=RB_off)
        ws1 = WStream(k, "v", w1slots)
        k.dma("m0", GW[:, :, :, :], Wg_b)
        for bp in range(LB // 2):
            ws1.add("rx", Win_p[:, O3 // 256 + bp], [128, KB, 256])
            ws1.add("gz", Win_p[:, O4 // 256 + bp], [128, KB, 256])
        for blk in range(LB):
            if blk % 2 == 0:
                wrx = ws1.next("rx", 2)
            k.memset("pool", Rb[:, 0:2], 0.0)
            k.memset("pool", Rb[:, S + 2:S + 4], 0.0)
            for c in range(NCH):
                bank = PSB[c % 2]
                for kb in range(KB):
                    k.mm(bank[:, :], wrx[:, kb * 256 + (blk % 2) * 128:kb * 256 + (blk % 2 + 1) * 128], hT[:, kb, c * T:(c + 1) * T], start=(kb == 0), stop=(kb == KB - 1))
                k.act(Rb[:, 2 + c * T:2 + (c + 1) * T], bank[:, :], AF.Copy)

            def cw(tp, blk=blk):
                return vecs[:, VC_CONVW + blk * 4 + tp:VC_CONVW + blk * 4 + tp + 1]

            halves = [(0, S // 2), (S // 2, S)] if S >= 1024 else [(0, S)]
            for (a0, b0) in halves:
                k.act(XC[:, a0:b0], Rb[:, a0 + 2:b0 + 2], AF.Identity, scale=cw(2), bias=vecs[:, VC_CONVB + blk:VC_CONVB + blk + 1])
                for tp in (0, 1, 3):
                    k.stt("dve", XC[:, a0:b0], Rb[:, a0 + tp:b0 + tp], cw(tp), XC[:, a0:b0], ALU.mult, ALU.add)
                k.act(XCB[:, a0:b0], XC[:, a0:b0], AF.Copy)
            if si == 0 and blk == 0:
                dbg("xc0", XC[:, :], [128, S], F32)
            for dr in range(2):
                THA, THI, AA = (B[0], B[1], B[2]) if dr == 0 else (B[1], B[2], B[3])
                col = dr * LB + blk
                for c in range(NCH):
                    ba, bi = PSB[2 + (c % 2) * 2], PSB[3 + (c % 2) * 2]
                    k.mm(ba[:, :], GW[:, 0, col, :], XCB[:, c * T:(c + 1) * T])
                    k.mm(bi[:, :], GW[:, 1, col, :], XCB[:, c * T:(c + 1) * T])
                    k.act(THA[:, c * T:(c + 1) * T], ba[:, :], AF.Tanh, scale=0.5, bias=der[:, col:col + 1])
                    k.act(THI[:, c * T:(c + 1) * T], bi[:, :], AF.Tanh, scale=0.5, bias=der[:, 20 + col:21 + col])
                for (a0, b0) in halves:
                    k.act(AA[:, a0:b0], THA[:, a0:b0], AF.Exp, scale=der2[:, col:col + 1], bias=der2[:, col:col + 1])
                    k.act(THA[:, a0:b0], THA[:, a0:b0], AF.Exp, scale=der2[:, 20 + col:21 + col], bias=der2[:, 20 + col:21 + col])
                    k.stt("dve", THI[:, a0:b0], THI[:, a0:b0], 1.0, XC[:, a0:b0], ALU.add, ALU.mult)
                for (a0, b0) in halves:
                    k.act(THA[:, a0:b0], THA[:, a0:b0], AF.Sqrt, scale=-0.25, bias=cst[:, 0:1])
                    k.tt("dve", THI[:, a0:b0], THI[:, a0:b0], THA[:, a0:b0], ALU.mult)
                if dr == 0:
                    for hi_, (a0, b0) in enumerate(halves):
                        init = 0.0 if hi_ == 0 else THA[:, a0 - 1:a0]
                        k.scan(THA[:, a0:b0], AA[:, a0:b0], THI[:, a0:b0], init)
                else:
                    for hi_, (a0, b0) in enumerate(reversed(halves)):
                        init = 0.0 if hi_ == 0 else THA[:, b0:b0 + 1]
                        k.scan(THA[:, a0:b0][::1] if False else THA[:, b0 - 1:(a0 - 1 if a0 > 0 else None):-1],
                               AA[:, b0 - 1:(a0 - 1 if a0 > 0 else None):-1], THI[:, b0 - 1:(a0 - 1 if a0 > 0 else None):-1], init)
            if si == 0 and blk == 0:
                dbg("hf0", B[0][:, :], [128, S], F32)
                dbg("hb0", B[1][:, :], [128, S], F32)
            if blk % 2 == 0:
                wgt = ws1.next("gz", 2)
            for c in range(NCH):
                bank = PSB[6 + c % 2]
                for kb in range(KB):
                    k.mm(bank[:, :], wgt[:, kb * 256 + (blk % 2) * 128:kb * 256 + (blk % 2 + 1) * 128], hT[:, kb, c * T:(c + 1) * T], start=(kb == 0), stop=(kb == KB - 1))
                k.act(Rb[:, c * T:(c + 1) * T], bank[:, :], AF.Gelu_apprx_tanh)
            for (a0, b0) in halves:
                k.tt("dve", B[0][:, a0:b0], B[0][:, a0:b0], B[1][:, a0:b0], ALU.add)
                k.tt("dve", RB[:, a0:b0], B[0][:, a0:b0], Rb[:, a0:b0], ALU.mult)
            k.dma("recst", recT_d[si][:, blk, :], RB[:, :])
            if si == 0 and blk == 0:
                dbg("rec0", RB[:, :], [128, S], BF16)
        P.stream_wait(["recst"])

        ws2.limit = ws2_lim[si]
        for c in range(NCH):
            c0 = c * T
            lo, hi = max(0, c0 - 128), min(S, c0 + T + 128)
            W = hi - lo
            tl_lo, tl_hi = lo // 128, hi // 128
            qt0 = c0 // 128
            if c == 0:
                k.dma("rope", ropeC[0:32, :, 0:W], rope_d[:, :, lo + 128:hi + 128])
            k.dma("recl", recc[:, :, :], recT_d[si][:, :, c0:c0 + T])
            k.dma_group("xt", [(xt[:, j, :], xd[c0 + j * 128:c0 + (j + 1) * 128, :]) for j in range(4)])
            P.stage = "K"
            wk = ws2.next("k")
            pieces = [(a, min(a + 512, hi)) for a in range(lo, hi, 512)]
            pend = None
            for kv in range(2):
                for pi, (a, b) in enumerate(pieces):
                    n = b - a
                    bank, bsw = PSB[(kv * 2 + pi) % 4], PSB[4 + (kv * 2 + pi) % 2]
                    for kb in range(KB):
                        k.mm(bank[:, 0:n], wk[:, kb * 256 + kv * 128:kb * 256 + (kv + 1) * 128], hT[:, kb, a:b], start=(kb == 0), stop=(kb == KB - 1))
                    k.act(kT[:, kv, a - lo:b - lo], bank[:, 0:n], AF.Copy)
                    if pend is not None:
                        rope(*pend)
                    pend = (kT, kv, a - lo, n, bank, bsw, tmp[((kv * 2 + pi) * 2) % 8], tmp[((kv * 2 + pi) * 2 + 1) % 8], a - lo)
            P.stage = "V"
            wv = ws2.next("v")
            for jt in range(tl_lo, tl_hi):
                bank = PSB[6 + jt % 2]
                for kb in range(KB):
                    k.mm(bank[:, 0:256], hT[:, kb, jt * 128:(jt + 1) * 128], wv[:, kb * 256:(kb + 1) * 256], start=(kb == 0), stop=(kb == KB - 1))
                k.copy("act" if jt % 2 else "dve", Vt[:, jt - tl_lo, :], bank[:, 0:256])
                if pend is not None:
                    rope(*pend)
                    pend = None
            P.stage = "Q"
            for h in range(NQ):
                if h % 2 == 0:
                    wv_ = ws2.next("q%d" % (h // 2))
                bank, bsw = PSB[h % 4], PSB[4 + h % 2]
                for kb in range(KB):
                    k.mm(bank[:, :], wv_[:, kb * 256 + (h % 2) * 128:kb * 256 + (h % 2 + 1) * 128], hT[:, kb, c0:c0 + T], start=(kb == 0), stop=(kb == KB - 1))
                k.act(qT[:, h, :], bank[:, :], AF.Copy)
                if pend is not None:
                    rope(*pend)
                pend = (qT, h, 0, T, bank, bsw, tmp[(h * 2) % 8], tmp[(h * 2 + 1) % 8], c0 - lo)
            rope(*pend)
            pend = None
            if c + 1 < NCH:
                nlo, nhi = max(0, c0 + T - 128), min(S, c0 + 2 * T + 128)
                k.dma("rope", ropeC[0:32, :, 0:nhi - nlo], rope_d[:, :, nlo + 128:nhi + 128])
            if si == 0 and c == 0:
                dbg("qT", qT[:, :, :], [128, NQ, T], BF16)
                dbg("kT", kT[:, :, 0:W], [128, 2, W], BF16)
                dbg("Vt", Vt[:, 0:tl_hi - tl_lo, :], [128, tl_hi - tl_lo, 256], BF16)
            P.stage = "att"
            items = [(h, jt) for h in range(NQ) for jt in range(tl_lo, tl_hi)]

            def att_front(n_):
                h, jt = items[n_]
                kv = h // 4
                qs = [i for i in (jt - 1, jt, jt + 1) if qt0 <= i < qt0 + 4]
                qa, qb = (qs[0] - qt0) * 128, (qs[-1] - qt0 + 1) * 128
                n = qb - qa
                sb_ = PSB[n_ % 4]
                pt = PT[n_ % 4]
                k.mm(sb_[:, 0:n], kT[:, kv, (jt - tl_lo) * 128:(jt - tl_lo + 1) * 128], qT[:, h, qa:qb], start=True, stop=True, nogroup=True)
                for i in qs:
                    off = (i - qt0) * 128 - qa
                    if i == jt - 1:
                        k.mm(sb_[:, off:off + 128], ident[:, :], mlo[:, :], start=False, stop=True, nogroup=True)
                    elif i == jt + 1:
                        k.mm(sb_[:, off:off + 128], ident[:, :], mhi[:, :], start=False, stop=True, nogroup=True)
                k.act(pt[:, 0:n], sb_[:, 0:n], AF.Exp, scale=SCALE)
                return (h, jt, kv, qa, qb, n, pt)

            def att_back(info):
                h, jt, kv, qa, qb, n, pt = info
                bO, bD = PSB[4 + (h % 2) * 2], PSB[5 + (h % 2) * 2]
                first = (jt == tl_lo)
                last = (jt == tl_hi - 1)
                k.mm(bO[:, qa:qb], Vt[:, jt - tl_lo, kv * 128:(kv + 1) * 128], pt[:, 0:n], start=first, stop=last, nogroup=True)
                k.mm(bD[:, qa:qb], ones[:, :], pt[:, 0:n], start=first, stop=last, nogroup=True)
                if last:
                    td = tmp[h % 2]
                    k.ts("dve", td[:, :], bD[:, :], esink[:, h:h + 1], None, ALU.add)
                    k.recip(td[:, :], td[:, :])
                    k.tt("dve", atT[:, h, :], bO[:, :], td[:, :], ALU.mult)

            prev = None
            for n_ in range(len(items)):
                info = att_front(n_)
                if prev is not None:
                    att_back(prev)
                prev = info
            att_back(prev)
            if si == 0 and c == 0:
                dbg("atT", atT[:, :, :], [128, NQ, T], BF16)
            P.stage = "merge"
            for qt in range(4):
                tg = [tmp[(qt % 2) * 4 + m] for m in range(2)]
                tr_ = [tmp[(qt % 2) * 4 + 2 + m] for m in range(2)]
                bks = [PSB[(qt % 2) * 4 + i] for i in range(4)]
                wga = ws2.next("ga%d" % qt)
                for m in range(2):
                    for kb in range(KB):
                        k.mm(bks[m][:, :], wga[:, kb * 256 + m * 128:kb * 256 + (m + 1) * 128], hT[:, kb, c0:c0 + T], start=(kb == 0), stop=(kb == KB - 1))
                    k.act(tg[m][:, :], bks[m][:, :], AF.Sigmoid)
                wpa = ws2.next("pa%d" % qt)
                for m in range(2):
                    for kb in range(NQ):
                        k.mm(bks[2 + m][:, :], wpa[:, kb * 256 + m * 128:kb * 256 + (m + 1) * 128], atT[:, kb, :], start=(kb == 0), stop=(kb == NQ - 1))
                    k.tt("dve", tg[m][:, :], bks[2 + m][:, :], tg[m][:, :], ALU.mult)
                wgr = ws2.next("gr%d" % qt)
                for m in range(2):
                    for kb in range(KB):
                        k.mm(bks[m][:, :], wgr[:, kb * 256 + m * 128:kb * 256 + (m + 1) * 128], hT[:, kb, c0:c0 + T], start=(kb == 0), stop=(kb == KB - 1))
                    k.act(tr_[m][:, :], bks[m][:, :], AF.Sigmoid)
                wpr = ws2.next("pr%d" % qt)
                for m in range(2):
                    for kb in range(LB):
                        k.mm(bks[2 + m][:, :], wpr[:, kb * 256 + m * 128:kb * 256 + (m + 1) * 128], recc[:, kb, :], start=(kb == 0), stop=(kb == LB - 1))
                    k.tt("dve", tr_[m][:, :], bks[2 + m][:, :], tr_[m][:, :], ALU.mult)
                    k.tt("dve", mgT[:, qt * 2 + m, :], tg[m][:, :], tr_[m][:, :], ALU.add)
            wo = [[ws2.next("wo00", 1), ws2.next("wo01", 2)], [ws2.next("wo10", 3), ws2.next("wo11", 4)]]

            def wout_tile(i):
                P.stage = "wout"
                st = stO[i % 2]
                for half in range(2):
                    bank = PSB[(i % 2) * 2 + half]
                    for kb in range(KB):
                        k.mm(bank[:, :], mgT[:, kb, i * 128:(i + 1) * 128], wo[half][kb // 4][:, (kb % 4) * 512:(kb % 4 + 1) * 512], start=(kb == 0), stop=(kb == KB - 1))
                    k.act(junk[:, half * 512:(half + 1) * 512], bank[:, :], AF.Square, accum=st[:, half:half + 1])
                k.tt("dve", st[:, 2:3], st[:, 0:1], st[:, 1:2], ALU.add)
                rstd_from(st, 2, 1, 4)
                for half in range(2):
                    bank = PSB[(i % 2) * 2 + half]
                    tb = tmp[(i % 2) * 2 + half]
                    k.stt("dve", tb[:, :], bank[:, :], st[:, 4:5], gpost[:, 0, half * 512:(half + 1) * 512], ALU.mult, ALU.mult)
                    k.tt("dve", xt[:, i, half * 512:(half + 1) * 512], xt[:, i, half * 512:(half + 1) * 512], tb[:, :], ALU.add)
                sf = stF[i % 2]
                k.act(junk[:, :], xt[:, i, :], AF.Square, accum=sf[:, 0:1])
                rstd_from(sf, 0, 1, 8)
                k.act(xn2[i % 2][:, :], xt[:, i, :], AF.Copy, scale=sf[:, 8:9])

            def ffn_tr_tile(i):
                P.stage = "ffnT"
                xnb = xn2[i % 2]
                bank = 4 + i % 2
                for kb in range(KB):
                    k.tr(PSH[bank][:, kb * 128:(kb + 1) * 128], xnb[:, kb * 128:(kb + 1) * 128], ident[:, :])
                srcv = PSH[bank][:, :]
                dstv = h2T[:, :, i * 128:(i + 1) * 128]
                srcap = PSH[bank].t[:, :].rearrange("p (a b) -> p a b", a=KB)
                if i % 2:
                    P.op("dve", lambda e, d=dstv, s=srcap: e.tensor_copy(out=d.ap, in_=s), reads=[srcv], writes=[dstv])
                else:
                    P.op("act", lambda e, d=dstv, s=srcap: e.activation(out=d.ap, in_=s, func=AF.Copy), reads=[srcv], writes=[dstv])

            if si == 0 and c == 0:
                dbg("mgT", mgT[:, :, :], [128, NQ, T], BF16)
            if OPT_B:
                for i in range(4):
                    wout_tile(i)
                    if i >= 1:
                        ffn_tr_tile(i - 1)
                ffn_tr_tile(3)
            else:
                for i in range(4):
                    wout_tile(i)
                    ffn_tr_tile(i)
            if si == 0 and c == 0:
                dbg("x1", xt[:, :, :], [128, 4, D], F32)
            P.stage = "ffnin"
            jj = 0
            for g2 in range(FB // 2):
                wg_ = ws2.next("fg%d" % g2, 1)
                wu_ = ws2.next("fu%d" % g2, 2)
                for m in range(2):
                    j = g2 * 2 + m
                    bG, bU = PSB[(jj % 2) * 2], PSB[(jj % 2) * 2 + 1]
                    tb = tmp[4 + jj % 4]
                    jj += 1
                    for kb in range(KB):
                        k.mm(bG[:, :], wg_[:, kb * 256 + m * 128:kb * 256 + (m + 1) * 128], h2T[:, kb, :], start=(kb == 0), stop=(kb == KB - 1))
                    for kb in range(KB):
                        k.mm(bU[:, :], wu_[:, kb * 256 + m * 128:kb * 256 + (m + 1) * 128], h2T[:, kb, :], start=(kb == 0), stop=(kb == KB - 1))
                    k.act(tb[:, :], bG[:, :], AF.Silu)
                    k.tt("dve", acT[:, j, :], bU[:, :], tb[:, :], ALU.mult)
            if si == 0 and c == 0:
                dbg("acT", acT[:, :, :], [128, FB, T], BF16)
            P.stage = "ffnout"
            for half in range(2):
                for (j0, j1) in FO_PIECES:
                    wf = ws2.next("fo%d_%d" % (half, j0))
                    for i in range(4):
                        bank = PSB[half * 4 + i]
                        for j in range(j0, j1):
                            k.mm(bank[:, :], acT[:, j, i * 128:(i + 1) * 128], wf[:, (j - j0) * 512:(j - j0 + 1) * 512], start=(j == 0), stop=(j == FB - 1))
            for i in range(4):
                st = stY[i % 2]
                for half in range(2):
                    k.act(junk[:, 0:512], PSB[half * 4 + i][:, :], AF.Square, accum=st[:, half:half + 1])
                k.tt("dve", st[:, 2:3], st[:, 0:1], st[:, 1:2], ALU.add)
                rstd_from(st, 2, 1, 4)
                for half in range(2):
                    tb = tmp[(i % 2) * 2 + half]
                    k.stt("dve", tb[:, :], PSB[half * 4 + i][:, :], st[:, 4:5], gpost[:, 1, half * 512:(half + 1) * 512], ALU.mult, ALU.mult)
                    k.tt("dve", xt[:, i, half * 512:(half + 1) * 512], xt[:, i, half * 512:(half + 1) * 512], tb[:, :], ALU.add)
                k.dma("yst%d" % i, ys[si][c0 + i * 128:c0 + (i + 1) * 128, :], xt[:, i, :])
    P.emit(final_streams=["yst0", "yst1", "yst2", "yst3", "dbg"])
    import os
    if os.environ.get("MK_DUMP_STAGES"):
        with open(os.environ["MK_DUMP_STAGES"], "w") as f:
            for e in ("pe", "act", "dve"):
                f.write(e + ":" + ",".join(o.stage for o in P.ops[e] if o.fn is not None) + "\n")
    return nc


def host_prep(inputs, smax):
    f = np.float32
    g = lambda n: np.ascontiguousarray(np.asarray(inputs[n], dtype=f)[0])
    conv_w, conv_b = g("conv_w"), g("conv_b")
    vec = np.zeros((128, NVEC), f)
    vec[:, VC_CONVW:VC_CONVW + 40] = conv_w.reshape(4, LB, 128).transpose(2, 1, 0).reshape(128, 40)
    vec[:, VC_CONVB:VC_CONVB + 10] = conv_b.reshape(LB, 128).T
    vec[:, VC_BA:VC_BA + 20] = g("lru_b_a").reshape(2 * LB, 128).T
    vec[:, VC_BI:VC_BI + 20] = g("lru_b_i").reshape(2 * LB, 128).T
    vec[:, VC_LAM:VC_LAM + 20] = g("lru_lambda").reshape(2 * LB, 128).T
    vec[:, VC_GPRE:VC_GPRE + 8] = g("norm_mix_pre").reshape(KB, 128).T
    vec[:, VC_GFFN:VC_GFFN + 8] = g("norm_ffn_pre").reshape(KB, 128).T
    gp = np.stack([g("norm_mix_post"), g("norm_ffn_post")], 0)
    sink = g("attn_sink").reshape(1, NQ)
    half = 16
    inv = (np.float32(500000.0) ** (-np.arange(half, dtype=f) / np.float32(half))).astype(f)
    pos = np.arange(-128, smax + 128).astype(f)
    ang = (pos[None, :] * inv[:, None]).astype(f)
    rope = np.zeros((32, 2, smax + 256), f)
    rope[0:16, 0] = np.cos(ang.astype(np.float64))
    rope[16:32, 0] = np.cos(ang.astype(np.float64))
    rope[0:16, 1] = np.sin(ang.astype(np.float64))
    rope[16:32, 1] = np.sin(ang.astype(np.float64))
    common = {
        "w_in": g("w_in"), "w_attn_proj": g("w_attn_proj"), "w_rec_proj": g("w_rec_proj"), "w_out": g("w_out"),
        "w_ffn_in": g("w_ffn_in"), "w_ffn_out": g("w_ffn_out"), "lru_w_a": g("lru_w_a"), "lru_w_i": g("lru_w_i"),
        "vecs": vec, "gpost": gp, "sink": sink, "rope": rope,
    }
    return common


_CACHE = {}


def run_layer(inputs, per_core_seqs, n_cores):
    seq_lens = tuple(a.shape[0] for a in per_core_seqs[0])
    smax = max(seq_lens)
    if seq_lens not in _CACHE:
        _CACHE[seq_lens] = build_program(list(seq_lens), smax)
    nc = _CACHE[seq_lens]
    common = host_prep(inputs, smax)
    in_maps = []
    for cseqs in per_core_seqs:
        m = dict(common)
        for i, a in enumerate(cseqs):
            m["x%d" % i] = np.ascontiguousarray(a, dtype=np.float32)
        in_maps.append(m)
    res = run_bass_kernel_spmd(nc, in_maps, core_ids=list(range(n_cores)))
    global LAST_RES
    LAST_RES = res
    return [[r["y%d" % i] for i in range(len(seq_lens))] for r in res.results]


def kernel(**inputs):
    xp = np.asarray(inputs["x_prompt"], dtype=np.float32)
    xsm = np.asarray(inputs["x_sample"], dtype=np.float32)
    n = 8
    per_core = [[xp[2 * c], xp[2 * c + 1], xsm[c]] for c in range(n)]
    outs = run_layer(inputs, per_core, n)
    yp = np.empty_like(xp)
    ys = np.empty_like(xsm)
    for c in range(n):
        yp[2 * c], yp[2 * c + 1], ys[c] = outs[c][0], outs[c][1], outs[c][2]
    return (yp, ys)
```

```python
import numpy as np
import concourse.bass as bass
import concourse.mybir as mybir
from concourse.bass_utils import run_bass_kernel_spmd

F32 = mybir.dt.float32
BF16 = mybir.dt.bfloat16
I32 = mybir.dt.int32
AF = mybir.ActivationFunctionType
ALU = mybir.AluOpType
DTB = {F32: 4, BF16: 2, I32: 4}
GRAN = 256
COMPUTE = ("pe", "act", "dve", "pool")


class View:
    __slots__ = ("ap", "space", "lo", "hi")

    def __init__(self, ap, space, lo, hi):
        self.ap, self.space, self.lo, self.hi = ap, space, lo, hi


class Buf:
    def __init__(self, prog, name, shape, dt, space, off):
        self.prog, self.name, self.shape, self.dt, self.space, self.off = prog, name, list(shape), dt, space, off
        self.esz = DTB[dt]
        self.nbytes = int(np.prod(shape[1:])) * self.esz
        nc = prog.nc
        if space == "sb":
            self.t = nc.alloc_sbuf_tensor_at(name, self.shape, dt, offset=off)
        else:
            self.t = prog.psum_handle(name, self.shape, dt, off)
        st = [1]
        for s in reversed(self.shape[2:]):
            st.insert(0, st[0] * s)
        self.strides = st

    def __getitem__(self, idx):
        if not isinstance(idx, tuple):
            idx = (idx,)
        idx = tuple(idx) + (slice(None),) * (len(self.shape) - len(idx))
        lo = 0
        hi = 0
        for d in range(1, len(self.shape)):
            i = idx[d]
            n = self.shape[d]
            if isinstance(i, int):
                a, b = i, i
            else:
                r = range(*i.indices(n))
                assert len(r) > 0, (self.name, idx)
                a, b = min(r[0], r[-1]), max(r[0], r[-1])
            lo += a * self.strides[d - 1]
            hi += b * self.strides[d - 1]
        return View(self.t[idx], self.space, self.off + lo * self.esz, self.off + (hi + 1) * self.esz)


class Op:
    __slots__ = ("eng", "fn", "reads", "writes", "sem", "inc", "deps", "cnt", "need", "dma", "idx", "tag", "xw", "stage")


class Prog:
    def __init__(self, nc):
        self.nc = nc
        self.ops = {e: [] for e in COMPUTE + ("sp",)}
        self.allops = []
        self.state = {}
        self.sb_off = (nc.sbuf_base + GRAN - 1) // GRAN * GRAN
        self.sb_top = nc.sbuf_top
        self.ps_banks = {}
        self.dma_streams = {}
        self.dcount = {}
        self.stage = ""

    def sb(self, name, shape, dt, off=None):
        esz = DTB[dt]
        nb = int(np.prod(shape[1:])) * esz
        if off is None:
            off = self.sb_off
            self.sb_off = (off + nb + GRAN - 1) // GRAN * GRAN
            assert self.sb_off <= self.sb_top, ("SBUF overflow", name, self.sb_off)
        return Buf(self, name, shape, dt, "sb", off)

    def psum_handle(self, name, shape, dt, off):
        bank = off // 2048
        assert off % 2048 == 0 and int(np.prod(shape[1:])) * DTB[dt] <= 2048
        key = (bank, dt)
        if bank not in self.ps_banks:
            self.ps_banks[bank] = self.nc.alloc_psum_tensor("psb%d" % bank, [128, 512], F32)
        t = self.ps_banks[bank]
        return t

    def ps(self, name, bank, dt=F32, n=None):
        n = n or (2048 // DTB[dt])
        b = Buf.__new__(Buf)
        b.prog, b.name, b.dt, b.space, b.off = self, name, dt, "ps", bank * 2048
        b.esz = DTB[dt]
        b.shape = [128, n]
        b.nbytes = n * b.esz
        b.strides = [1]
        if bank not in self.ps_banks:
            self.ps_banks[bank] = self.nc.alloc_psum_tensor("psb%d" % bank, [128, 512], F32)
        t = self.ps_banks[bank]
        if dt != F32:
            b.t = _Bitcast(t, dt, n)
        else:
            b.t = t
        return b

    def _grans(self, v):
        if v.space in ("sb", "ps"):
            return [(v.space, g) for g in range(v.lo // GRAN, (v.hi - 1) // GRAN + 1)]
        return [(v.space, g) for g in range(v.lo, v.hi)]

    def op(self, eng, fn, reads=(), writes=(), dma=None, ndma=1, tag=""):
        o = Op()
        o.eng, o.fn, o.tag = eng, fn, tag
        o.dma = dma
        o.need = False
        o.idx = len(self.allops)
        o.xw = []
        o.cnt = None
        o.stage = self.stage
        if dma is not None:
            o.sem = "dma:" + dma
            o.inc = 16 * ndma
            self.dcount[o.sem] = self.dcount.get(o.sem, 0) + o.inc
            o.cnt = self.dcount[o.sem]
            o.need = True
        else:
            o.sem = eng
            o.inc = 1
        deps = {}
        for v in reads:
            for g in self._grans(v):
                s = self.state.get(g)
                if s and s[0] is not None:
                    deps[s[0].idx] = (s[0], "raw")
        for v in writes:
            for g in self._grans(v):
                s = self.state.get(g)
                if s:
                    if s[0] is not None and s[0].idx not in deps:
                        deps[s[0].idx] = (s[0], "waw")
                    for r in s[1].values():
                        if r.idx not in deps:
                            deps[r.idx] = (r, "war")
        keep = []
        for d, kind in deps.values():
            if d is o:
                continue
            same = (d.eng == eng) and d.dma is None and dma is None
            if same and eng == "pe":
                continue
            keep.append(d)
            d.need = True
        o.deps = keep
        rkey = eng if dma is None else ("dma", o.idx)
        for v in reads:
            for g in self._grans(v):
                s = self.state.setdefault(g, [None, {}])
                s[1][rkey] = o
        for v in writes:
            for g in self._grans(v):
                self.state[g] = [o, {}]
        self.ops[eng].append(o)
        self.allops.append(o)
        return o

    def stream_wait(self, streams, eng="sp"):
        o = self.op(eng, None, tag="swait")
        o.xw = [("dma:" + s, self.dcount["dma:" + s]) for s in streams if ("dma:" + s) in self.dcount]
        return o

    def emit(self, final_streams=()):
        nc = self.nc
        cnt = dict(self.dcount)
        for o in self.allops:
            if o.dma is None:
                if o.need and o.fn is not None:
                    cnt[o.sem] = cnt.get(o.sem, 0) + 1
                    o.cnt = cnt[o.sem]
                elif o.need:
                    raise AssertionError("dependency on a wait-only op")
        self.final_cnt = cnt
        semnames = sorted(cnt.keys())
        import contextlib
        with contextlib.ExitStack() as es:
            sems = {}
            for i, s in enumerate(semnames):
                sems[s] = es.enter_context(nc.semaphore("s_" + s.replace(":", "_")))
            block = es.enter_context(nc.Block())
            engmap = {"pe": block.tensor, "act": block.scalar, "dve": block.vector, "pool": block.gpsimd, "sp": block.sync}

            def run(engname):
                def body(eng):
                    waited = {}
                    for o in self.ops[engname]:
                        for d in o.deps:
                            if waited.get(d.sem, 0) < d.cnt:
                                eng.wait_ge(sems[d.sem], d.cnt)
                                waited[d.sem] = d.cnt
                        for (sm, val) in o.xw:
                            if waited.get(sm, 0) < val:
                                eng.wait_ge(sems[sm], val)
                                waited[sm] = val
                        if o.fn is None:
                            continue
                        r = o.fn(eng)
                        if o.need:
                            if o.dma is not None:
                                rs = r if isinstance(r, (list, tuple)) else [r]
                                assert len(rs) * 16 == o.inc, (o.tag, len(rs), o.inc)
                                for x in rs:
                                    x.then_inc(sems[o.sem], 16)
                            else:
                                r.then_inc(sems[o.sem], 1)
                    if engname == "sp":
                        for s in final_streams:
                            k = "dma:" + s
                            if k in cnt:
                                eng.wait_ge(sems[k], cnt[k])
                return body

            for e in ("sp", "pe", "act", "dve", "pool"):
                if self.ops[e] or e == "sp":
                    engmap[e](run(e))


class _Bitcast:
    def __init__(self, t, dt, n):
        self.t, self.dt, self.n = t, dt, n

    def __getitem__(self, idx):
        ap = self.t[:, :].bitcast(self.dt)
        return ap[idx]


D = 1024
KB = 8
NQ = 8
LW = 1280
LB = 10
DFF = 2816
FB = 22
INW = 6144
O1, O2, O3, O4, O5, O6 = 1024, 1280, 1536, 2816, 4096, 5120
T = 512
EPS = 1e-6
NEG = -30000.0
SCALE = 1.0 / float(np.sqrt(128.0))
VC_CONVW, VC_CONVB, VC_BA, VC_BI, VC_LAM, VC_GPRE, VC_GFFN = 0, 40, 50, 70, 90, 110, 118
NVEC = 126


def _reads(*vs):
    return [v for v in vs if isinstance(v, View)]


def _a(v):
    return v.ap if isinstance(v, View) else v


class K:
    def __init__(self, P):
        self.P = P

    def mm(self, out, lhsT, rhs, start=True, stop=True, nogroup=False):
        if nogroup:
            self.P.op("pe", lambda e: e.matmul(out.ap, lhsT=lhsT.ap, rhs=rhs.ap, start=start, stop=stop, skip_group_check=True),
                      reads=[lhsT, rhs], writes=[out])
        else:
            self.P.op("pe", lambda e: e.matmul(out.ap, lhsT=lhsT.ap, rhs=rhs.ap, start=start, stop=stop),
                      reads=[lhsT, rhs], writes=[out])

    def tr(self, out, in_, ident):
        self.P.op("pe", lambda e: e.transpose(out=out.ap, in_=in_.ap, identity=ident.ap), reads=[in_, ident], writes=[out])

    def act(self, out, in_, func, scale=1.0, bias=0.0, accum=None):
        w = [out] + ([accum] if accum is not None else [])
        kw = {}
        if accum is not None:
            kw["accum_out"] = accum.ap
        self.P.op("act", lambda e: e.activation(out=out.ap, in_=in_.ap, func=func, bias=_a(bias), scale=_a(scale), **kw),
                  reads=[in_] + _reads(scale, bias), writes=w)

    def ts(self, eng, out, in0, s1, s2, op0, op1=None):
        if op1 is None:
            self.P.op(eng, lambda e: e.tensor_scalar(out=out.ap, in0=in0.ap, scalar1=_a(s1), scalar2=None, op0=op0),
                      reads=[in0] + _reads(s1), writes=[out])
        else:
            self.P.op(eng, lambda e: e.tensor_scalar(out=out.ap, in0=in0.ap, scalar1=_a(s1), scalar2=_a(s2), op0=op0, op1=op1),
                      reads=[in0] + _reads(s1, s2), writes=[out])

    def tt(self, eng, out, in0, in1, op):
        self.P.op(eng, lambda e: e.tensor_tensor(out=out.ap, in0=in0.ap, in1=in1.ap, op=op), reads=[in0, in1], writes=[out])

    def stt(self, eng, out, in0, s, in1, op0, op1):
        self.P.op(eng, lambda e: e.scalar_tensor_tensor(out=out.ap, in0=in0.ap, scalar=_a(s), in1=in1.ap, op0=op0, op1=op1),
                  reads=[in0, in1] + _reads(s), writes=[out])

    def copy(self, eng, out, in_):
        if eng == "act":
            self.act(out, in_, AF.Copy)
        else:
            self.P.op(eng, lambda e: e.tensor_copy(out=out.ap, in_=in_.ap), reads=[in_], writes=[out])

    def memset(self, eng, out, val):
        self.P.op(eng, lambda e: e.memset(out.ap, val), writes=[out])

    def recip(self, out, in_):
        self.P.op("dve", lambda e: e.reciprocal(out=out.ap, in_=in_.ap), reads=[in_], writes=[out])

    def scan(self, out, a, u, init):
        self.P.op("dve", lambda e: e.tensor_tensor_scan(out=out.ap, data0=a.ap, data1=u.ap, initial=_a(init), op0=ALU.mult, op1=ALU.add),
                  reads=[a, u] + _reads(init), writes=[out])

    def dma(self, stream, out, in_, reads=(), writes=()):
        self.P.op("sp", lambda e: e.dma_start(out=_a(out), in_=_a(in_)), reads=list(reads) + _reads(in_), writes=list(writes) + _reads(out), dma=stream)

    def dma_group(self, stream, pairs):
        self.P.op("sp", lambda e: [e.dma_start(out=_a(o), in_=_a(i)) for (o, i) in pairs],
                  reads=[i for (o, i) in pairs if isinstance(i, View)],
                  writes=[o for (o, i) in pairs if isinstance(o, View)], dma=stream, ndma=len(pairs))


class WStream:
    def __init__(self, k, name, slots, eng="sp"):
        self.k, self.name, self.slots, self.eng = k, name, slots, eng
        self.plan = []
        self.loaded = 0
        self.cur = 0
        self.limit = None

    def add(self, tag, dram_ap, shape):
        self.plan.append((dram_ap, shape, tag))

    def next(self, tag=None, hold=1):
        i = self.cur
        self.cur += 1
        assert tag is None or self.plan[i][2] == tag, (i, tag, self.plan[i][2])
        nd = len(self.slots)
        lim = len(self.plan) if self.limit is None else self.limit
        while self.loaded < min(lim, i + nd - hold + 1):
            j = self.loaded
            slot = self.slots[j % nd]
            shape = self.plan[j][1]
            n = int(np.prod(shape[1:]))
            v = slot[:, 0:n]
            ap = slot.t[:, 0:n].rearrange("p (a b) -> p a b", a=shape[1])
            self.k.P.op(self.eng, lambda e, ap=ap, src=self.plan[j][0]: e.dma_start(out=ap, in_=src),
                        writes=[v], dma="%s%d" % (self.name, j % nd))
            self.loaded += 1
        return self.slots[i % nd]


DEBUG = False
WENG = "pool"
OPT_B = True


def build_program(seq_lens, smax):
    nc = bass.Bass("TRN2", target_bir_lowering=False)
    P = Prog(nc)
    k = K(P)
    dbg_n = [0]

    def dbg(name, view, shape, dt):
        if not DEBUG:
            return
        d = nc.dram_tensor("dbg_" + name, list(shape), dt, kind="ExternalOutput").ap()
        P.op("sp", lambda e: e.dma_start(out=d, in_=view.ap), reads=[view], dma="dbg")

    def din(name, shape, dt=F32):
        return nc.dram_tensor(name, list(shape), dt, kind="ExternalInput").ap()

    def dscr(name, shape, dt=BF16):
        return nc.dram_tensor(name, list(shape), dt, kind="Internal").ap()

    xs = [din("x%d" % i, [S, D]) for i, S in enumerate(seq_lens)]
    ys = [nc.dram_tensor("y%d" % i, [S, D], F32, kind="ExternalOutput").ap() for i, S in enumerate(seq_lens)]
    w_in_d = din("w_in", [D, INW])
    w_ap_d = din("w_attn_proj", [D, D])
    w_rp_d = din("w_rec_proj", [LW, D])
    w_out_d = din("w_out", [D, D])
    w_fi_d = din("w_ffn_in", [D, 2 * DFF])
    w_fo_d = din("w_ffn_out", [DFF, D])
    lwa_d = din("lru_w_a", [2, LB, 128, 128])
    lwi_d = din("lru_w_i", [2, LB, 128, 128])
    vecs_d = din("vecs", [128, NVEC])
    gpost_d = din("gpost", [2, D])
    sink_d = din("sink", [1, NQ])
    rope_d = din("rope", [32, 2, smax + 256])

    Win_p = dscr("Win_p", [128, INW // 256, KB, 256])
    Wap_p = dscr("Wap_p", [128, D // 256, KB, 256])
    Wrp_p = dscr("Wrp_p", [128, D // 256, LB, 256])
    Wout_p = dscr("Wout_p", [128, 2, KB, 512])
    Wfi_p = dscr("Wfi_p", [128, 2 * DFF // 256, KB, 256])
    Wfo_p = dscr("Wfo_p", [128, 2, FB, 512])
    Wg_b = dscr("Wg_b", [128, 2, 2 * LB, 128])
    recT_d = [dscr("recT%d" % i, [128, LB, S]) for i, S in enumerate(seq_lens)]

    Smax = max(seq_lens)
    vecs = P.sb("vecs", [128, NVEC], F32)
    der = P.sb("der", [128, 64], F32)
    der2 = P.sb("der2", [128, 64], F32)
    gpost = P.sb("gpost", [128, 2, D], F32)
    esink = P.sb("esink", [128, NQ], F32)
    ident = P.sb("ident", [128, 128], BF16)
    ones = P.sb("ones", [128, 128], BF16)
    mlo = P.sb("mlo", [128, 128], BF16)
    mhi = P.sb("mhi", [128, 128], BF16)
    prot = P.sb("prot", [128, 32], BF16)
    cst = P.sb("cst", [128, 8], F32)
    stA = P.sb("stA", [128, 64], F32)
    stO = [P.sb("stO%d" % i, [128, 64], F32) for i in range(2)]
    stF = [P.sb("stF%d" % i, [128, 64], F32) for i in range(2)]
    stY = [P.sb("stY%d" % i, [128, 64], F32) for i in range(2)]
    hT = P.sb("hT", [128, KB, Smax], BF16)
    ARENA0 = P.sb_off

    PSB = [P.ps("ps%d" % b, b, F32) for b in range(8)]
    PSH = [P.ps("psh%d" % b, b, BF16) for b in range(8)]

    def affsel(v, pattern, cmp, fill, base, cm):
        P.op("pool", lambda e: e.affine_select(out=v.ap, in_=v.ap, pattern=pattern, compare_op=cmp, fill=fill, base=base, channel_multiplier=cm),
             reads=[v], writes=[v])

    scr = [P.sb("cscr%d" % i, [128, 128], F32, off=ARENA0 + 512 * i) for i in range(4)]
    k.memset("pool", scr[0][:, :], 1.0)
    affsel(scr[0][:, :], [[-1, 128]], ALU.is_equal, 0.0, 0, 1)
    k.copy("pool", ident[:, :], scr[0][:, :])
    k.memset("pool", ones[:, :], 1.0)
    k.memset("pool", scr[1][:, :], 0.0)
    affsel(scr[1][:, :], [[1, 128]], ALU.is_ge, NEG, 0, -1)
    k.copy("pool", mlo[:, :], scr[1][:, :])
    k.memset("pool", scr[2][:, :], 0.0)
    affsel(scr[2][:, :], [[-1, 128]], ALU.is_ge, NEG, 0, 1)
    k.copy("pool", mhi[:, :], scr[2][:, :])
    k.memset("pool", scr[3][:, 0:32], 0.0)
    affsel(scr[3][:, 0:16], [[-1, 16]], ALU.not_equal, -1.0, -16, 1)
    affsel(scr[3][:, 16:32], [[-1, 16]], ALU.not_equal, 1.0, 0, 1)
    k.copy("pool", prot[:, :], scr[3][:, 0:32])
    k.memset("pool", cst[:, 0:1], 0.25)
    k.memset("pool", cst[:, 1:2], EPS)
    k.memset("pool", cst[:, 2:3], 1.0)

    k.dma("m0", vecs[:, :], vecs_d)
    k.dma("m1", gpost[:, 0, :], gpost_d[0:1, :].partition_broadcast(128))
    k.dma("m2", gpost[:, 1, :], gpost_d[1:2, :].partition_broadcast(128))
    k.dma("m3", esink[:, :], sink_d[0:1, :].partition_broadcast(128))
    k.act(esink[:, :], esink[:, :], AF.Exp)
    k.ts("dve", der[:, 0:20], vecs[:, VC_BA:VC_BA + 20], 0.5, None, ALU.mult)
    k.ts("dve", der[:, 20:40], vecs[:, VC_BI:VC_BI + 20], 0.5, None, ALU.mult)
    k.act(der[:, 40:60], vecs[:, VC_LAM:VC_LAM + 20], AF.Exp, scale=-1.0)
    k.act(der[:, 40:60], der[:, 40:60], AF.Ln, bias=cst[:, 2:3])
    k.ts("dve", der2[:, 0:20], der[:, 40:60], -4.0, None, ALU.mult)
    k.ts("dve", der2[:, 20:40], der[:, 40:60], -8.0, None, ALU.mult)

    CW = 2048
    stg_f = [P.sb("stgf%d" % i, [128, CW], F32, off=ARENA0 + 4096 + i * (CW * 6)) for i in range(3)]
    stg_b = [P.sb("stgb%d" % i, [128, CW], BF16, off=ARENA0 + 4096 + i * (CW * 6) + CW * 4) for i in range(3)]
    cvi = [0]
    engs = ["act", "dve"]

    cvjobs = []

    def convert(src_ap, dst_ap, width, scale_v, ld3=None, st3=None):
        cvjobs.append((src_ap, dst_ap, width, scale_v, ld3, st3))

    def cv_load(i):
        src_ap, dst_ap, width, scale_v, ld3, st3 = cvjobs[i]
        sf = stg_f[i % 3]
        o_ap = sf[:, 0:width].ap if ld3 is None else sf.t[:, 0:width].rearrange("p (a b) -> p a b", a=ld3)
        P.op("sp", lambda e: e.dma_start(out=o_ap, in_=src_ap), writes=[sf[:, 0:width]], dma="cvl%d" % (i % 3))

    def cv_store(i):
        src_ap, dst_ap, width, scale_v, ld3, st3 = cvjobs[i]
        sf, sb_ = stg_f[i % 3], stg_b[i % 3]
        i_ap = sb_[:, 0:width].ap if st3 is None else sb_.t[:, 0:width].rearrange("p (a b) -> p a b", a=st3)
        eng = engs[i % 2]
        if scale_v is None:
            k.copy(eng, sb_[:, 0:width], sf[:, 0:width])
        elif eng == "act":
            k.act(sb_[:, 0:width], sf[:, 0:width], AF.Copy, scale=scale_v)
        else:
            k.ts(eng, sb_[:, 0:width], sf[:, 0:width], scale_v, None, ALU.mult)
        P.op("sp", lambda e: e.dma_start(out=dst_ap, in_=i_ap), reads=[sb_[:, 0:width]], dma="cvs%d" % (i % 3))

    def cv_run():
        n = len(cvjobs)
        for i in range(n + 2):
            if i < n:
                cv_load(i)
            if i >= 2:
                cv_store(i - 2)

    def convert_cols(src, dst, nkb, ncols, scale_col):
        for kb in range(nkb):
            for c0 in range(0, ncols, CW):
                w = min(CW, ncols - c0)
                sv = vecs[:, scale_col + kb:scale_col + kb + 1] if scale_col is not None else None
                convert(src[kb * 128:(kb + 1) * 128, c0:c0 + w], dst[:, c0 // 256:(c0 + w) // 256, kb, :], w, sv, st3=w // 256)

    def convert_rows(src, dst, nkb):
        for kb in range(nkb):
            convert(src[kb * 128:(kb + 1) * 128, :], dst[:, :, kb, :], D, None, st3=2)

    for gi, src in enumerate((lwa_d, lwi_d)):
        for dr in range(2):
            convert(src[dr].rearrange("n c d -> c n d"), Wg_b[:, gi, dr * LB:(dr + 1) * LB, :], LB * 128, None, ld3=LB, st3=LB)
    convert_cols(w_in_d, Win_p, KB, INW, VC_GPRE)
    convert_cols(w_ap_d, Wap_p, KB, D, None)
    convert_cols(w_rp_d, Wrp_p, LB, D, None)
    convert_rows(w_out_d, Wout_p, KB)
    convert_cols(w_fi_d, Wfi_p, KB, 2 * DFF, VC_GFFN)
    convert_rows(w_fo_d, Wfo_p, FB)
    cv_run()
    P.stream_wait(["cvs0", "cvs1", "cvs2"])

    WSLOT = 5632
    NWS = 8
    wslots = [P.sb("wslot%d" % i, [128, WSLOT // 2], BF16, off=ARENA0 + i * WSLOT) for i in range(NWS)]
    A2 = ARENA0 + NWS * WSLOT
    o = [A2]

    def take(n):
        r = o[0]
        o[0] = (r + n + GRAN - 1) // GRAN * GRAN
        assert o[0] <= P.sb_top, ("arena overflow", o[0], P.sb_top)
        return r

    SF = Smax * 4
    o[0] = ARENA0
    R_off = take(SF + 16)
    XC_off = take(SF)
    XCB_off = take(Smax * 2)
    B_off = [take(SF) for _ in range(4)]
    RB_off = XCB_off
    W1_off = [take(4096) for _ in range(3)]
    GW_off = take(2 * 2 * LB * 128 * 2)
    p1_end = o[0]
    o[0] = R_off
    X1_off = [take(4096) for _ in range(8)]
    XN_off = [take(2048) for _ in range(2)]
    JUNK1_off = take(2048)
    p1_end = max(p1_end, o[0])
    o[0] = A2
    XT_off = take(4 * 4096)
    QT_off = take(NQ * T * 2)
    AT_off = take(NQ * T * 2)
    REC_off = take(LB * T * 2)
    TMP_off = [take(2048) for _ in range(8)]
    JUNK_off = REC_off
    XN2_off = [REC_off + 2048, REC_off + 4096]
    ROPE_off = take(32 * 0 + 2 * 768 * 4)
    G_off = take(FB * T * 2)
    p2_end = o[0]

    xt = P.sb("xt", [128, 4, D], F32, off=XT_off)
    qT = P.sb("qT", [128, NQ, T], BF16, off=QT_off)
    mgT = qT
    atT = P.sb("atT", [128, NQ, T], BF16, off=AT_off)
    h2T = atT
    recc = P.sb("recc", [128, LB, T], BF16, off=REC_off)
    tmp = [P.sb("tmp%d" % i, [128, T], F32, off=TMP_off[i]) for i in range(8)]
    junk = P.sb("junk", [128, D], BF16, off=JUNK_off)
    xn2 = [P.sb("xn2_%d" % i, [128, D], BF16, off=XN2_off[i]) for i in range(2)]
    kT = P.sb("kT", [128, 2, 768], BF16, off=G_off)
    Vt = P.sb("Vt", [128, 6, 256], BF16, off=G_off + 3072)
    PT = [P.sb("PT%d" % i, [128, 384], BF16, off=G_off + 6144 + i * 1024) for i in range(4)]
    ropeC = P.sb("ropeC", [32, 2, 768], F32, off=ROPE_off)
    acT = P.sb("acT", [128, FB, T], BF16, off=G_off)
    w1slots = [P.sb("w1slot%d" % i, [128, KB * 256], BF16, off=W1_off[i]) for i in range(3)]
    GW = P.sb("GW", [128, 2, 2 * LB, 128], BF16, off=GW_off)
    x1 = [P.sb("x1_%d" % i, [128, D], F32, off=X1_off[i]) for i in range(8)]
    xn = [P.sb("xn_%d" % i, [128, D], BF16, off=XN_off[i]) for i in range(2)]
    junk1 = P.sb("junk1", [128, D], BF16, off=JUNK1_off)

    FO_PIECES = [(0, 5), (5, 10), (10, 15), (15, 20), (20, 22)]
    ws2 = WStream(k, "w", wslots, eng=WENG)
    ws2_lim = []
    for si, S in enumerate(seq_lens):
        ws2_lim.append(0)
        for c in range(S // T):
            ws2.add("k", Win_p[:, O1 // 256], [128, KB, 256])
            ws2.add("v", Win_p[:, O2 // 256], [128, KB, 256])
            for hp in range(4):
                ws2.add("q%d" % hp, Win_p[:, hp], [128, KB, 256])
            for qt in range(4):
                ws2.add("ga%d" % qt, Win_p[:, O5 // 256 + qt], [128, KB, 256])
                ws2.add("pa%d" % qt, Wap_p[:, qt], [128, KB, 256])
                ws2.add("gr%d" % qt, Win_p[:, O6 // 256 + qt], [128, KB, 256])
                ws2.add("pr%d" % qt, Wrp_p[:, qt], [128, LB, 256])
            for half in range(2):
                for kg in range(2):
                    ws2.add("wo%d%d" % (half, kg), Wout_p[:, half, kg * 4:(kg + 1) * 4, :], [128, 4, 512])
            for g2 in range(FB // 2):
                ws2.add("fg%d" % g2, Wfi_p[:, g2], [128, KB, 256])
                ws2.add("fu%d" % g2, Wfi_p[:, DFF // 256 + g2], [128, KB, 256])
            for half in range(2):
                for (j0, j1) in FO_PIECES:
                    ws2.add("fo%d_%d" % (half, j0), Wfo_p[:, half, j0:j1, :], [128, j1 - j0, 512])
        ws2_lim[si] = len(ws2.plan)

    def rstd_from(st, src_cols, n, dst0):
        k.ts("dve", st[:, 40:40 + n], st[:, src_cols:src_cols + n], 1.0 / D, cst[:, 1:2], ALU.mult, ALU.add)
        k.act(st[:, 40:40 + n], st[:, 40:40 + n], AF.Sqrt)
        k.recip(st[:, dst0:dst0 + n], st[:, 40:40 + n])

    def rope(raw, kvh, c_lo, n, bank_raw, bank_sw, tA, tB, tab_lo):
        k.mm(bank_sw[0:32, 0:n], prot[:, 0:32], raw[:, kvh, c_lo:c_lo + n])
        k.tt("dve", tA[0:32, 0:n], bank_sw[0:32, 0:n], ropeC[0:32, 1, tab_lo:tab_lo + n], ALU.mult)
        k.tt("dve", tB[0:32, 0:n], bank_raw[0:32, 0:n], ropeC[0:32, 0, tab_lo:tab_lo + n], ALU.mult)
        k.tt("dve", raw[0:32, kvh, c_lo:c_lo + n], tA[0:32, 0:n], tB[0:32, 0:n], ALU.add)

    for si, S in enumerate(seq_lens):
        NT = S // 128
        NCH = S // T
        xd = xs[si]
        P.stage = "p1a"
        for g in range(NT // 4):
            for j in range(4):
                t = g * 4 + j
                k.dma("x1l%d" % (t % 8), x1[t % 8][:, :], xd[t * 128:(t + 1) * 128, :])
            for j in range(4):
                t = g * 4 + j
                k.act(junk1[:, :], x1[t % 8][:, :], AF.Square, accum=stA[:, j:j + 1])
            rstd_from(stA, 0, 4, 8)
            for j in range(4):
                t = g * 4 + j
                xnb = xn[t % 2]
                k.act(xnb[:, :], x1[t % 8][:, :], AF.Copy, scale=stA[:, 8 + j:9 + j])
                bank = t % 2
                for kb in range(KB):
                    k.tr(PSH[bank][:, kb * 128:(kb + 1) * 128], xnb[:, kb * 128:(kb + 1) * 128], ident[:, :])
                srcv = PSH[bank][:, :]
                dstv = hT[:, :, t * 128:(t + 1) * 128]
                srcap = PSH[bank].t[:, :].rearrange("p (a b) -> p a b", a=KB)
                if t % 2:
                    P.op("dve", lambda e, d=dstv, s=srcap: e.tensor_copy(out=d.ap, in_=s), reads=[srcv], writes=[dstv])
                else:
                    P.op("act", lambda e, d=dstv, s=srcap: e.activation(out=d.ap, in_=s, func=AF.Copy), reads=[srcv], writes=[dstv])

        if si == 0:
            dbg("hT", hT[:, :, 0:S], [128, KB, S], BF16)
        P.stage = "p1b"
        Rb = P.sb("R_%d" % si, [128, S + 4], F32, off=R_off)
        XC = P.sb("XC_%d" % si, [128, S], F32, off=XC_off)
        XCB = P.sb("XCB_%d" % si, [128, S], BF16, off=XCB_off)
        B = [P.sb("B%d_%d" % (i, si), [128, S], F32, off=B_off[i]) for i in range(4)]
        RB = P.sb("RB_%d" % si, [128, S], BF16, off=RB_off)
        ws1 = WStream(k, "v", w1slots)
        k.dma("m0", GW[:, :, :, :], Wg_b)
        for bp in range(LB // 2):
            ws1.add("rx", Win_p[:, O3 // 256 + bp], [128, KB, 256])
            ws1.add("gz", Win_p[:, O4 // 256 + bp], [128, KB, 256])
        for blk in range(LB):
            if blk % 2 == 0:
                wrx = ws1.next("rx", 2)
            k.memset("pool", Rb[:, 0:2], 0.0)
            k.memset("pool", Rb[:, S + 2:S + 4], 0.0)
            for c in range(NCH):
                bank = PSB[c % 2]
                for kb in range(KB):
                    k.mm(bank[:, :], wrx[:, kb * 256 + (blk % 2) * 128:kb * 256 + (blk % 2 + 1) * 128], hT[:, kb, c * T:(c + 1) * T], start=(kb == 0), stop=(kb == KB - 1))
                k.act(Rb[:, 2 + c * T:2 + (c + 1) * T], bank[:, :], AF.Copy)

            def cw(tp, blk=blk):
                return vecs[:, VC_CONVW + blk * 4 + tp:VC_CONVW + blk * 4 + tp + 1]

            halves = [(0, S // 2), (S // 2, S)] if S >= 1024 else [(0, S)]
            for (a0, b0) in halves:
                k.ts("dve", XC[:, a0:b0], Rb[:, a0 + 2:b0 + 2], cw(2), vecs[:, VC_CONVB + blk:VC_CONVB + blk + 1], ALU.mult, ALU.add)
                for tp in (0, 1, 3):
                    k.stt("dve", XC[:, a0:b0], Rb[:, a0 + tp:b0 + tp], cw(tp), XC[:, a0:b0], ALU.mult, ALU.add)
                k.copy("dve", XCB[:, a0:b0], XC[:, a0:b0])
            if si == 0 and blk == 0:
                dbg("xc0", XC[:, :], [128, S], F32)
            for dr in range(2):
                THA, THI, AA = (B[0], B[1], B[2]) if dr == 0 else (Rb, B[3], B[2])
                col = dr * LB + blk
                for c in range(NCH):
                    ba, bi = PSB[2 + (c % 2) * 2], PSB[3 + (c % 2) * 2]
                    k.mm(ba[:, :], GW[:, 0, col, :], XCB[:, c * T:(c + 1) * T])
                    k.mm(bi[:, :], GW[:, 1, col, :], XCB[:, c * T:(c + 1) * T])
                    k.act(THA[:, c * T:(c + 1) * T], ba[:, :], AF.Tanh, scale=0.5, bias=der[:, col:col + 1])
                    k.act(THI[:, c * T:(c + 1) * T], bi[:, :], AF.Tanh, scale=0.5, bias=der[:, 20 + col:21 + col])
                if dr == 1:
                    if blk % 2 == 0:
                        wgt = ws1.next("gz", 2)
                    for c in range(NCH):
                        bank = PSB[6 + c % 2]
                        for kb in range(KB):
                            k.mm(bank[:, :], wgt[:, kb * 256 + (blk % 2) * 128:kb * 256 + (blk % 2 + 1) * 128], hT[:, kb, c * T:(c + 1) * T], start=(kb == 0), stop=(kb == KB - 1))
                        k.act(B[1][:, c * T:(c + 1) * T], bank[:, :], AF.Gelu_apprx_tanh)
                for (a0, b0) in halves:
                    k.act(AA[:, a0:b0], THA[:, a0:b0], AF.Exp, scale=der2[:, col:col + 1], bias=der2[:, col:col + 1])
                    k.tt("dve", THA[:, a0:b0], AA[:, a0:b0], AA[:, a0:b0], ALU.mult)
                    k.stt("dve", THI[:, a0:b0], THI[:, a0:b0], 1.0, XC[:, a0:b0], ALU.add, ALU.mult)
                for (a0, b0) in halves:
                    k.act(THA[:, a0:b0], THA[:, a0:b0], AF.Sqrt, scale=-0.25, bias=cst[:, 0:1])
                    k.tt("dve", THI[:, a0:b0], THI[:, a0:b0], THA[:, a0:b0], ALU.mult)
                if dr == 0:
                    for hi_, (a0, b0) in enumerate(halves):
                        init = 0.0 if hi_ == 0 else THA[:, a0 - 1:a0]
                        k.scan(THA[:, a0:b0], AA[:, a0:b0], THI[:, a0:b0], init)
                else:
                    for hi_, (a0, b0) in enumerate(reversed(halves)):
                        init = 0.0 if hi_ == 0 else THA[:, b0:b0 + 1]
                        k.scan(THA[:, a0:b0][::1] if False else THA[:, b0 - 1:(a0 - 1 if a0 > 0 else None):-1],
                               AA[:, b0 - 1:(a0 - 1 if a0 > 0 else None):-1], THI[:, b0 - 1:(a0 - 1 if a0 > 0 else None):-1], init)
            if si == 0 and blk == 0:
                dbg("hf0", B[0][:, :], [128, S], F32)
                dbg("hb0", Rb[:, 0:S], [128, S], F32)
            for (a0, b0) in halves:
                k.tt("dve", B[0][:, a0:b0], B[0][:, a0:b0], Rb[:, a0:b0], ALU.add)
                k.tt("dve", RB[:, a0:b0], B[0][:, a0:b0], B[1][:, a0:b0], ALU.mult)
            k.dma("recst", recT_d[si][:, blk, :], RB[:, :])
            if si == 0 and blk == 0:
                dbg("rec0", RB[:, :], [128, S], BF16)
        P.stream_wait(["recst"])

        ws2.limit = ws2_lim[si]
        for c in range(NCH):
            c0 = c * T
            lo, hi = max(0, c0 - 128), min(S, c0 + T + 128)
            W = hi - lo
            tl_lo, tl_hi = lo // 128, hi // 128
            qt0 = c0 // 128
            if c == 0:
                k.dma("rope", ropeC[0:32, :, 0:W], rope_d[:, :, lo + 128:hi + 128])
            k.dma("recl", recc[:, :, :], recT_d[si][:, :, c0:c0 + T])
            k.dma_group("xt", [(xt[:, j, :], xd[c0 + j * 128:c0 + (j + 1) * 128, :]) for j in range(4)])
            P.stage = "K"
            wk = ws2.next("k")
            pieces = [(a, min(a + 512, hi)) for a in range(lo, hi, 512)]
            pend = None
            for kv in range(2):
                for pi, (a, b) in enumerate(pieces):
                    n = b - a
                    bank, bsw = PSB[(kv * 2 + pi) % 4], PSB[4 + (kv * 2 + pi) % 2]
                    for kb in range(KB):
                        k.mm(bank[:, 0:n], wk[:, kb * 256 + kv * 128:kb * 256 + (kv + 1) * 128], hT[:, kb, a:b], start=(kb == 0), stop=(kb == KB - 1))
                    k.act(kT[:, kv, a - lo:b - lo], bank[:, 0:n], AF.Copy)
                    if pend is not None:
                        rope(*pend)
                    pend = (kT, kv, a - lo, n, bank, bsw, tmp[((kv * 2 + pi) * 2) % 8], tmp[((kv * 2 + pi) * 2 + 1) % 8], a - lo)
            P.stage = "V"
            wv = ws2.next("v")
            for jt in range(tl_lo, tl_hi):
                bank = PSB[6 + jt % 2]
                for kb in range(KB):
                    k.mm(bank[:, 0:256], hT[:, kb, jt * 128:(jt + 1) * 128], wv[:, kb * 256:(kb + 1) * 256], start=(kb == 0), stop=(kb == KB - 1))
                k.copy("act" if jt % 2 else "dve", Vt[:, jt - tl_lo, :], bank[:, 0:256])
                if pend is not None:
                    rope(*pend)
                    pend = None
            P.stage = "Q"
            for h in range(NQ):
                if h % 2 == 0:
                    wv_ = ws2.next("q%d" % (h // 2))
                bank, bsw = PSB[h % 4], PSB[4 + h % 2]
                for kb in range(KB):
                    k.mm(bank[:, :], wv_[:, kb * 256 + (h % 2) * 128:kb * 256 + (h % 2 + 1) * 128], hT[:, kb, c0:c0 + T], start=(kb == 0), stop=(kb == KB - 1))
                k.act(qT[:, h, :], bank[:, :], AF.Copy)
                if pend is not None:
                    rope(*pend)
                pend = (qT, h, 0, T, bank, bsw, tmp[(h * 2) % 8], tmp[(h * 2 + 1) % 8], c0 - lo)
            rope(*pend)
            pend = None
            if c + 1 < NCH:
                nlo, nhi = max(0, c0 + T - 128), min(S, c0 + 2 * T + 128)
                k.dma("rope", ropeC[0:32, :, 0:nhi - nlo], rope_d[:, :, nlo + 128:nhi + 128])
            if si == 0 and c == 0:
                dbg("qT", qT[:, :, :], [128, NQ, T], BF16)
                dbg("kT", kT[:, :, 0:W], [128, 2, W], BF16)
                dbg("Vt", Vt[:, 0:tl_hi - tl_lo, :], [128, tl_hi - tl_lo, 256], BF16)
            P.stage = "att"
            items = [(h, jt) for h in range(NQ) for jt in range(tl_lo, tl_hi)]

            def att_front(n_):
                h, jt = items[n_]
                kv = h // 4
                qs = [i for i in (jt - 1, jt, jt + 1) if qt0 <= i < qt0 + 4]
                qa, qb = (qs[0] - qt0) * 128, (qs[-1] - qt0 + 1) * 128
                n = qb - qa
                sb_ = PSB[n_ % 4]
                pt = PT[n_ % 4]
                k.mm(sb_[:, 0:n], kT[:, kv, (jt - tl_lo) * 128:(jt - tl_lo + 1) * 128], qT[:, h, qa:qb], start=True, stop=True, nogroup=True)
                for i in qs:
                    off = (i - qt0) * 128 - qa
                    if i == jt - 1:
                        k.mm(sb_[:, off:off + 128], ident[:, :], mlo[:, :], start=False, stop=True, nogroup=True)
                    elif i == jt + 1:
                        k.mm(sb_[:, off:off + 128], ident[:, :], mhi[:, :], start=False, stop=True, nogroup=True)
                k.act(pt[:, 0:n], sb_[:, 0:n], AF.Exp, scale=SCALE)
                return (h, jt, kv, qa, qb, n, pt)

            def att_back(info):
                h, jt, kv, qa, qb, n, pt = info
                bO, bD = PSB[4 + (h % 2) * 2], PSB[5 + (h % 2) * 2]
                first = (jt == tl_lo)
                last = (jt == tl_hi - 1)
                k.mm(bO[:, qa:qb], Vt[:, jt - tl_lo, kv * 128:(kv + 1) * 128], pt[:, 0:n], start=first, stop=last, nogroup=True)
                k.mm(bD[:, qa:qb], ones[:, :], pt[:, 0:n], start=first, stop=last, nogroup=True)
                if last:
                    td = tmp[h % 2]
                    k.ts("dve", td[:, :], bD[:, :], esink[:, h:h + 1], None, ALU.add)
                    k.recip(td[:, :], td[:, :])
                    k.tt("dve", atT[:, h, :], bO[:, :], td[:, :], ALU.mult)

            prev = None
            for n_ in range(len(items)):
                info = att_front(n_)
                if prev is not None:
                    att_back(prev)
                prev = info
            att_back(prev)
            if si == 0 and c == 0:
                dbg("atT", atT[:, :, :], [128, NQ, T], BF16)
            P.stage = "merge"
            for qt in range(4):
                tg = [tmp[(qt % 2) * 4 + m] for m in range(2)]
                tr_ = [tmp[(qt % 2) * 4 + 2 + m] for m in range(2)]
                bks = [PSB[(qt % 2) * 4 + i] for i in range(4)]
                wga = ws2.next("ga%d" % qt)
                for m in range(2):
                    for kb in range(KB):
                        k.mm(bks[m][:, :], wga[:, kb * 256 + m * 128:kb * 256 + (m + 1) * 128], hT[:, kb, c0:c0 + T], start=(kb == 0), stop=(kb == KB - 1))
                    k.act(tg[m][:, :], bks[m][:, :], AF.Sigmoid)
                wpa = ws2.next("pa%d" % qt)
                for m in range(2):
                    for kb in range(NQ):
                        k.mm(bks[2 + m][:, :], wpa[:, kb * 256 + m * 128:kb * 256 + (m + 1) * 128], atT[:, kb, :], start=(kb == 0), stop=(kb == NQ - 1))
                    k.tt("dve", tg[m][:, :], bks[2 + m][:, :], tg[m][:, :], ALU.mult)
                wgr = ws2.next("gr%d" % qt)
                for m in range(2):
                    for kb in range(KB):
                        k.mm(bks[m][:, :], wgr[:, kb * 256 + m * 128:kb * 256 + (m + 1) * 128], hT[:, kb, c0:c0 + T], start=(kb == 0), stop=(kb == KB - 1))
                    k.act(tr_[m][:, :], bks[m][:, :], AF.Sigmoid)
                wpr = ws2.next("pr%d" % qt)
                for m in range(2):
                    for kb in range(LB):
                        k.mm(bks[2 + m][:, :], wpr[:, kb * 256 + m * 128:kb * 256 + (m + 1) * 128], recc[:, kb, :], start=(kb == 0), stop=(kb == LB - 1))
                    k.tt("dve", tr_[m][:, :], bks[2 + m][:, :], tr_[m][:, :], ALU.mult)
                    k.tt("dve", mgT[:, qt * 2 + m, :], tg[m][:, :], tr_[m][:, :], ALU.add)
            wo = [[ws2.next("wo00", 1), ws2.next("wo01", 2)], [ws2.next("wo10", 3), ws2.next("wo11", 4)]]

            def wout_tile(i):
                P.stage = "wout"
                st = stO[i % 2]
                for half in range(2):
                    bank = PSB[(i % 2) * 2 + half]
                    for kb in range(KB):
                        k.mm(bank[:, :], mgT[:, kb, i * 128:(i + 1) * 128], wo[half][kb // 4][:, (kb % 4) * 512:(kb % 4 + 1) * 512], start=(kb == 0), stop=(kb == KB - 1))
                    k.act(junk[:, half * 512:(half + 1) * 512], bank[:, :], AF.Square, accum=st[:, half:half + 1])
                k.tt("dve", st[:, 2:3], st[:, 0:1], st[:, 1:2], ALU.add)
                rstd_from(st, 2, 1, 4)
                for half in range(2):
                    bank = PSB[(i % 2) * 2 + half]
                    tb = tmp[(i % 2) * 2 + half]
                    k.stt("dve", tb[:, :], bank[:, :], st[:, 4:5], gpost[:, 0, half * 512:(half + 1) * 512], ALU.mult, ALU.mult)
                    k.tt("dve", xt[:, i, half * 512:(half + 1) * 512], xt[:, i, half * 512:(half + 1) * 512], tb[:, :], ALU.add)
                sf = stF[i % 2]
                k.act(junk[:, :], xt[:, i, :], AF.Square, accum=sf[:, 0:1])
                rstd_from(sf, 0, 1, 8)
                k.act(xn2[i % 2][:, :], xt[:, i, :], AF.Copy, scale=sf[:, 8:9])

            def ffn_tr_tile(i):
                P.stage = "ffnT"
                xnb = xn2[i % 2]
                bank = 4 + i % 2
                for kb in range(KB):
                    k.tr(PSH[bank][:, kb * 128:(kb + 1) * 128], xnb[:, kb * 128:(kb + 1) * 128], ident[:, :])
                srcv = PSH[bank][:, :]
                dstv = h2T[:, :, i * 128:(i + 1) * 128]
                srcap = PSH[bank].t[:, :].rearrange("p (a b) -> p a b", a=KB)
                if i % 2:
                    P.op("dve", lambda e, d=dstv, s=srcap: e.tensor_copy(out=d.ap, in_=s), reads=[srcv], writes=[dstv])
                else:
                    P.op("act", lambda e, d=dstv, s=srcap: e.activation(out=d.ap, in_=s, func=AF.Copy), reads=[srcv], writes=[dstv])

            if si == 0 and c == 0:
                dbg("mgT", mgT[:, :, :], [128, NQ, T], BF16)
            if OPT_B:
                for i in range(4):
                    wout_tile(i)
                    if i >= 1:
                        ffn_tr_tile(i - 1)
                ffn_tr_tile(3)
            else:
                for i in range(4):
                    wout_tile(i)
                    ffn_tr_tile(i)
            if si == 0 and c == 0:
                dbg("x1", xt[:, :, :], [128, 4, D], F32)
            P.stage = "ffnin"
            jj = 0
            for g2 in range(FB // 2):
                wg_ = ws2.next("fg%d" % g2, 1)
                wu_ = ws2.next("fu%d" % g2, 2)
                for m in range(2):
                    j = g2 * 2 + m
                    bG, bU = PSB[(jj % 2) * 2], PSB[(jj % 2) * 2 + 1]
                    tb = tmp[4 + jj % 4]
                    jj += 1
                    for kb in range(KB):
                        k.mm(bG[:, :], wg_[:, kb * 256 + m * 128:kb * 256 + (m + 1) * 128], h2T[:, kb, :], start=(kb == 0), stop=(kb == KB - 1))
                    for kb in range(KB):
                        k.mm(bU[:, :], wu_[:, kb * 256 + m * 128:kb * 256 + (m + 1) * 128], h2T[:, kb, :], start=(kb == 0), stop=(kb == KB - 1))
                    k.act(tb[:, :], bG[:, :], AF.Silu)
                    k.tt("dve", acT[:, j, :], bU[:, :], tb[:, :], ALU.mult)
            if si == 0 and c == 0:
                dbg("acT", acT[:, :, :], [128, FB, T], BF16)
            P.stage = "ffnout"
            for half in range(2):
                for (j0, j1) in FO_PIECES:
                    wf = ws2.next("fo%d_%d" % (half, j0))
                    for i in range(4):
                        bank = PSB[half * 4 + i]
                        for j in range(j0, j1):
                            k.mm(bank[:, :], acT[:, j, i * 128:(i + 1) * 128], wf[:, (j - j0) * 512:(j - j0 + 1) * 512], start=(j == 0), stop=(j == FB - 1))
            for i in range(4):
                st = stY[i % 2]
                for half in range(2):
                    k.act(junk[:, 0:512], PSB[half * 4 + i][:, :], AF.Square, accum=st[:, half:half + 1])
                k.tt("dve", st[:, 2:3], st[:, 0:1], st[:, 1:2], ALU.add)
                rstd_from(st, 2, 1, 4)
                for half in range(2):
                    tb = tmp[(i % 2) * 2 + half]
                    k.stt("dve", tb[:, :], PSB[half * 4 + i][:, :], st[:, 4:5], gpost[:, 1, half * 512:(half + 1) * 512], ALU.mult, ALU.mult)
                    k.tt("dve", xt[:, i, half * 512:(half + 1) * 512], xt[:, i, half * 512:(half + 1) * 512], tb[:, :], ALU.add)
                k.dma("yst%d" % i, ys[si][c0 + i * 128:c0 + (i + 1) * 128, :], xt[:, i, :])
    P.emit(final_streams=["yst0", "yst1", "yst2", "yst3", "dbg"])
    import os
    if os.environ.get("MK_DUMP_STAGES"):
        with open(os.environ["MK_DUMP_STAGES"], "w") as f:
            for e in ("pe", "act", "dve"):
                f.write(e + ":" + ",".join(o.stage for o in P.ops[e] if o.fn is not None) + "\n")
    return nc


def host_prep(inputs, smax):
    f = np.float32
    g = lambda n: np.ascontiguousarray(np.asarray(inputs[n], dtype=f)[0])
    conv_w, conv_b = g("conv_w"), g("conv_b")
    vec = np.zeros((128, NVEC), f)
    vec[:, VC_CONVW:VC_CONVW + 40] = conv_w.reshape(4, LB, 128).transpose(2, 1, 0).reshape(128, 40)
    vec[:, VC_CONVB:VC_CONVB + 10] = conv_b.reshape(LB, 128).T
    vec[:, VC_BA:VC_BA + 20] = g("lru_b_a").reshape(2 * LB, 128).T
    vec[:, VC_BI:VC_BI + 20] = g("lru_b_i").reshape(2 * LB, 128).T
    vec[:, VC_LAM:VC_LAM + 20] = g("lru_lambda").reshape(2 * LB, 128).T
    vec[:, VC_GPRE:VC_GPRE + 8] = g("norm_mix_pre").reshape(KB, 128).T
    vec[:, VC_GFFN:VC_GFFN + 8] = g("norm_ffn_pre").reshape(KB, 128).T
    gp = np.stack([g("norm_mix_post"), g("norm_ffn_post")], 0)
    sink = g("attn_sink").reshape(1, NQ)
    half = 16
    inv = (np.float32(500000.0) ** (-np.arange(half, dtype=f) / np.float32(half))).astype(f)
    pos = np.arange(-128, smax + 128).astype(f)
    ang = (pos[None, :] * inv[:, None]).astype(f)
    rope = np.zeros((32, 2, smax + 256), f)
    rope[0:16, 0] = np.cos(ang.astype(np.float64))
    rope[16:32, 0] = np.cos(ang.astype(np.float64))
    rope[0:16, 1] = np.sin(ang.astype(np.float64))
    rope[16:32, 1] = np.sin(ang.astype(np.float64))
    common = {
        "w_in": g("w_in"), "w_attn_proj": g("w_attn_proj"), "w_rec_proj": g("w_rec_proj"), "w_out": g("w_out"),
        "w_ffn_in": g("w_ffn_in"), "w_ffn_out": g("w_ffn_out"), "lru_w_a": g("lru_w_a"), "lru_w_i": g("lru_w_i"),
        "vecs": vec, "gpost": gp, "sink": sink, "rope": rope,
    }
    return common


_CACHE = {}


def run_layer(inputs, per_core_seqs, n_cores):
    seq_lens = tuple(a.shape[0] for a in per_core_seqs[0])
    smax = max(seq_lens)
    if seq_lens not in _CACHE:
        _CACHE[seq_lens] = build_program(list(seq_lens), smax)
    nc = _CACHE[seq_lens]
    common = host_prep(inputs, smax)
    in_maps = []
    for cseqs in per_core_seqs:
        m = dict(common)
        for i, a in enumerate(cseqs):
            m["x%d" % i] = np.ascontiguousarray(a, dtype=np.float32)
        in_maps.append(m)
    res = run_bass_kernel_spmd(nc, in_maps, core_ids=list(range(n_cores)))
    global LAST_RES
    LAST_RES = res
    return [[r["y%d" % i] for i in range(len(seq_lens))] for r in res.results]


def kernel(**inputs):
    xp = np.asarray(inputs["x_prompt"], dtype=np.float32)
    xsm = np.asarray(inputs["x_sample"], dtype=np.float32)
    n = 8
    per_core = [[xp[2 * c], xp[2 * c + 1], xsm[c]] for c in range(n)]
    outs = run_layer(inputs, per_core, n)
    yp = np.empty_like(xp)
    ys = np.empty_like(xsm)
    for c in range(n):
        yp[2 * c], yp[2 * c + 1], ys[c] = outs[c][0], outs[c][1], outs[c][2]
    return (yp, ys)
```

```python
import numpy as np
import concourse.bass as bass
import concourse.mybir as mybir
from concourse.bass_utils import run_bass_kernel_spmd

F32 = mybir.dt.float32
BF16 = mybir.dt.bfloat16
I32 = mybir.dt.int32
AF = mybir.ActivationFunctionType
ALU = mybir.AluOpType
DTB = {F32: 4, BF16: 2, I32: 4}
GRAN = 256
COMPUTE = ("pe", "act", "dve", "pool")


class View:
    __slots__ = ("ap", "space", "lo", "hi")

    def __init__(self, ap, space, lo, hi):
        self.ap, self.space, self.lo, self.hi = ap, space, lo, hi


class Buf:
    def __init__(self, prog, name, shape, dt, space, off):
        self.prog, self.name, self.shape, self.dt, self.space, self.off = prog, name, list(shape), dt, space, off
        self.esz = DTB[dt]
        self.nbytes = int(np.prod(shape[1:])) * self.esz
        nc = prog.nc
        if space == "sb":
            self.t = nc.alloc_sbuf_tensor_at(name, self.shape, dt, offset=off)
        else:
            self.t = prog.psum_handle(name, self.shape, dt, off)
        st = [1]
        for s in reversed(self.shape[2:]):
            st.insert(0, st[0] * s)
        self.strides = st

    def __getitem__(self, idx):
        if not isinstance(idx, tuple):
            idx = (idx,)
        idx = tuple(idx) + (slice(None),) * (len(self.shape) - len(idx))
        lo = 0
        hi = 0
        for d in range(1, len(self.shape)):
            i = idx[d]
            n = self.shape[d]
            if isinstance(i, int):
                a, b = i, i
            else:
                r = range(*i.indices(n))
                assert len(r) > 0, (self.name, idx)
                a, b = min(r[0], r[-1]), max(r[0], r[-1])
            lo += a * self.strides[d - 1]
            hi += b * self.strides[d - 1]
        return View(self.t[idx], self.space, self.off + lo * self.esz, self.off + (hi + 1) * self.esz)


class Op:
    __slots__ = ("eng", "fn", "reads", "writes", "sem", "inc", "deps", "cnt", "need", "dma", "idx", "tag", "xw", "stage")


class Prog:
    def __init__(self, nc):
        self.nc = nc
        self.ops = {e: [] for e in COMPUTE + ("sp",)}
        self.allops = []
        self.state = {}
        self.sb_off = (nc.sbuf_base + GRAN - 1) // GRAN * GRAN
        self.sb_top = nc.sbuf_top
        self.ps_banks = {}
        self.dma_streams = {}
        self.dcount = {}
        self.stage = ""

    def sb(self, name, shape, dt, off=None):
        esz = DTB[dt]
        nb = int(np.prod(shape[1:])) * esz
        if off is None:
            off = self.sb_off
            self.sb_off = (off + nb + GRAN - 1) // GRAN * GRAN
            assert self.sb_off <= self.sb_top, ("SBUF overflow", name, self.sb_off)
        return Buf(self, name, shape, dt, "sb", off)

    def psum_handle(self, name, shape, dt, off):
        bank = off // 2048
        assert off % 2048 == 0 and int(np.prod(shape[1:])) * DTB[dt] <= 2048
        key = (bank, dt)
        if bank not in self.ps_banks:
            self.ps_banks[bank] = self.nc.alloc_psum_tensor("psb%d" % bank, [128, 512], F32)
        t = self.ps_banks[bank]
        return t

    def ps(self, name, bank, dt=F32, n=None):
        n = n or (2048 // DTB[dt])
        b = Buf.__new__(Buf)
        b.prog, b.name, b.dt, b.space, b.off = self, name, dt, "ps", bank * 2048
        b.esz = DTB[dt]
        b.shape = [128, n]
        b.nbytes = n * b.esz
        b.strides = [1]
        if bank not in self.ps_banks:
            self.ps_banks[bank] = self.nc.alloc_psum_tensor("psb%d" % bank, [128, 512], F32)
        t = self.ps_banks[bank]
        if dt != F32:
            b.t = _Bitcast(t, dt, n)
        else:
            b.t = t
        return b

    def _grans(self, v):
        if v.space in ("sb", "ps"):
            return [(v.space, g) for g in range(v.lo // GRAN, (v.hi - 1) // GRAN + 1)]
        return [(v.space, g) for g in range(v.lo, v.hi)]

    def op(self, eng, fn, reads=(), writes=(), dma=None, ndma=1, tag=""):
        o = Op()
        o.eng, o.fn, o.tag = eng, fn, tag
        o.dma = dma
        o.need = False
        o.idx = len(self.allops)
        o.xw = []
        o.cnt = None
        o.stage = self.stage
        if dma is not None:
            o.sem = "dma:" + dma
            o.inc = 16 * ndma
            self.dcount[o.sem] = self.dcount.get(o.sem, 0) + o.inc
            o.cnt = self.dcount[o.sem]
            o.need = True
        else:
            o.sem = eng
            o.inc = 1
        deps = {}
        for v in reads:
            for g in self._grans(v):
                s = self.state.get(g)
                if s and s[0] is not None:
                    deps[s[0].idx] = (s[0], "raw")
        for v in writes:
            for g in self._grans(v):
                s = self.state.get(g)
                if s:
                    if s[0] is not None and s[0].idx not in deps:
                        deps[s[0].idx] = (s[0], "waw")
                    for r in s[1].values():
                        if r.idx not in deps:
                            deps[r.idx] = (r, "war")
        keep = []
        for d, kind in deps.values():
            if d is o:
                continue
            same = (d.eng == eng) and d.dma is None and dma is None
            if same and eng == "pe":
                continue
            keep.append(d)
            d.need = True
        o.deps = keep
        rkey = eng if dma is None else ("dma", o.idx)
        for v in reads:
            for g in self._grans(v):
                s = self.state.setdefault(g, [None, {}])
                s[1][rkey] = o
        for v in writes:
            for g in self._grans(v):
                self.state[g] = [o, {}]
        self.ops[eng].append(o)
        self.allops.append(o)
        return o

    def stream_wait(self, streams, eng="sp"):
        o = self.op(eng, None, tag="swait")
        o.xw = [("dma:" + s, self.dcount["dma:" + s]) for s in streams if ("dma:" + s) in self.dcount]
        return o

    def emit(self, final_streams=()):
        nc = self.nc
        cnt = dict(self.dcount)
        for o in self.allops:
            if o.dma is None:
                if o.need and o.fn is not None:
                    cnt[o.sem] = cnt.get(o.sem, 0) + 1
                    o.cnt = cnt[o.sem]
                elif o.need:
                    raise AssertionError("dependency on a wait-only op")
        self.final_cnt = cnt
        semnames = sorted(cnt.keys())
        import contextlib
        with contextlib.ExitStack() as es:
            sems = {}
            for i, s in enumerate(semnames):
                sems[s] = es.enter_context(nc.semaphore("s_" + s.replace(":", "_")))
            block = es.enter_context(nc.Block())
            engmap = {"pe": block.tensor, "act": block.scalar, "dve": block.vector, "pool": block.gpsimd, "sp": block.sync}

            def run(engname):
                def body(eng):
                    waited = {}
                    for o in self.ops[engname]:
                        for d in o.deps:
                            if waited.get(d.sem, 0) < d.cnt:
                                eng.wait_ge(sems[d.sem], d.cnt)
                                waited[d.sem] = d.cnt
                        for (sm, val) in o.xw:
                            if waited.get(sm, 0) < val:
                                eng.wait_ge(sems[sm], val)
                                waited[sm] = val
                        if o.fn is None:
                            continue
                        r = o.fn(eng)
                        if o.need:
                            if o.dma is not None:
                                rs = r if isinstance(r, (list, tuple)) else [r]
                                assert len(rs) * 16 == o.inc, (o.tag, len(rs), o.inc)
                                for x in rs:
                                    x.then_inc(sems[o.sem], 16)
                            else:
                                r.then_inc(sems[o.sem], 1)
                    if engname == "sp":
                        for s in final_streams:
                            k = "dma:" + s
                            if k in cnt:
                                eng.wait_ge(sems[k], cnt[k])
                return body

            for e in ("sp", "pe", "act", "dve", "pool"):
                if self.ops[e] or e == "sp":
                    engmap[e](run(e))


class _Bitcast:
    def __init__(self, t, dt, n):
        self.t, self.dt, self.n = t, dt, n

    def __getitem__(self, idx):
        ap = self.t[:, :].bitcast(self.dt)
        return ap[idx]


D = 1024
KB = 8
NQ = 8
LW = 1280
LB = 10
DFF = 2816
FB = 22
INW = 6144
O1, O2, O3, O4, O5, O6 = 1024, 1280, 1536, 2816, 4096, 5120
T = 512
EPS = 1e-6
NEG = -30000.0
SCALE = 1.0 / float(np.sqrt(128.0))
VC_CONVW, VC_CONVB, VC_BA, VC_BI, VC_LAM, VC_GPRE, VC_GFFN = 0, 40, 50, 70, 90, 110, 118
NVEC = 126


def _reads(*vs):
    return [v for v in vs if isinstance(v, View)]


def _a(v):
    return v.ap if isinstance(v, View) else v


class K:
    def __init__(self, P):
        self.P = P

    def mm(self, out, lhsT, rhs, start=True, stop=True, nogroup=False):
        if nogroup:
            self.P.op("pe", lambda e: e.matmul(out.ap, lhsT=lhsT.ap, rhs=rhs.ap, start=start, stop=stop, skip_group_check=True),
                      reads=[lhsT, rhs], writes=[out])
        else:
            self.P.op("pe", lambda e: e.matmul(out.ap, lhsT=lhsT.ap, rhs=rhs.ap, start=start, stop=stop),
                      reads=[lhsT, rhs], writes=[out])

    def tr(self, out, in_, ident):
        self.P.op("pe", lambda e: e.transpose(out=out.ap, in_=in_.ap, identity=ident.ap), reads=[in_, ident], writes=[out])

    def act(self, out, in_, func, scale=1.0, bias=0.0, accum=None):
        w = [out] + ([accum] if accum is not None else [])
        kw = {}
        if accum is not None:
            kw["accum_out"] = accum.ap
        self.P.op("act", lambda e: e.activation(out=out.ap, in_=in_.ap, func=func, bias=_a(bias), scale=_a(scale), **kw),
                  reads=[in_] + _reads(scale, bias), writes=w)

    def ts(self, eng, out, in0, s1, s2, op0, op1=None):
        if op1 is None:
            self.P.op(eng, lambda e: e.tensor_scalar(out=out.ap, in0=in0.ap, scalar1=_a(s1), scalar2=None, op0=op0),
                      reads=[in0] + _reads(s1), writes=[out])
        else:
            self.P.op(eng, lambda e: e.tensor_scalar(out=out.ap, in0=in0.ap, scalar1=_a(s1), scalar2=_a(s2), op0=op0, op1=op1),
                      reads=[in0] + _reads(s1, s2), writes=[out])

    def tt(self, eng, out, in0, in1, op):
        self.P.op(eng, lambda e: e.tensor_tensor(out=out.ap, in0=in0.ap, in1=in1.ap, op=op), reads=[in0, in1], writes=[out])

    def stt(self, eng, out, in0, s, in1, op0, op1):
        self.P.op(eng, lambda e: e.scalar_tensor_tensor(out=out.ap, in0=in0.ap, scalar=_a(s), in1=in1.ap, op0=op0, op1=op1),
                  reads=[in0, in1] + _reads(s), writes=[out])

    def copy(self, eng, out, in_):
        if eng == "act":
            self.act(out, in_, AF.Copy)
        else:
            self.P.op(eng, lambda e: e.tensor_copy(out=out.ap, in_=in_.ap), reads=[in_], writes=[out])

    def memset(self, eng, out, val):
        self.P.op(eng, lambda e: e.memset(out.ap, val), writes=[out])

    def recip(self, out, in_):
        self.P.op("dve", lambda e: e.reciprocal(out=out.ap, in_=in_.ap), reads=[in_], writes=[out])

    def scan(self, out, a, u, init):
        self.P.op("dve", lambda e: e.tensor_tensor_scan(out=out.ap, data0=a.ap, data1=u.ap, initial=_a(init), op0=ALU.mult, op1=ALU.add),
                  reads=[a, u] + _reads(init), writes=[out])

    def dma(self, stream, out, in_, reads=(), writes=()):
        self.P.op("sp", lambda e: e.dma_start(out=_a(out), in_=_a(in_)), reads=list(reads) + _reads(in_), writes=list(writes) + _reads(out), dma=stream)

    def dma_group(self, stream, pairs):
        self.P.op("sp", lambda e: [e.dma_start(out=_a(o), in_=_a(i)) for (o, i) in pairs],
                  reads=[i for (o, i) in pairs if isinstance(i, View)],
                  writes=[o for (o, i) in pairs if isinstance(o, View)], dma=stream, ndma=len(pairs))


class WStream:
    def __init__(self, k, name, slots, eng="sp"):
        self.k, self.name, self.slots, self.eng = k, name, slots, eng
        self.plan = []
        self.loaded = 0
        self.cur = 0
        self.limit = None

    def add(self, tag, dram_ap, shape):
        self.plan.append((dram_ap, shape, tag))

    def next(self, tag=None, hold=1):
        i = self.cur
        self.cur += 1
        assert tag is None or self.plan[i][2] == tag, (i, tag, self.plan[i][2])
        nd = len(self.slots)
        lim = len(self.plan) if self.limit is None else self.limit
        while self.loaded < min(lim, i + nd - hold + 1):
            j = self.loaded
            slot = self.slots[j % nd]
            shape = self.plan[j][1]
            n = int(np.prod(shape[1:]))
            v = slot[:, 0:n]
            ap = slot.t[:, 0:n].rearrange("p (a b) -> p a b", a=shape[1])
            self.k.P.op(self.eng, lambda e, ap=ap, src=self.plan[j][0]: e.dma_start(out=ap, in_=src),
                        writes=[v], dma="%s%d" % (self.name, j % nd))
            self.loaded += 1
        return self.slots[i % nd]


DEBUG = False
WENG = "pool"
OPT_B = True


def build_program(seq_lens, smax):
    nc = bass.Bass("TRN2", target_bir_lowering=False)
    P = Prog(nc)
    k = K(P)
    dbg_n = [0]

    def dbg(name, view, shape, dt):
        if not DEBUG:
            return
        d = nc.dram_tensor("dbg_" + name, list(shape), dt, kind="ExternalOutput").ap()
        P.op("sp", lambda e: e.dma_start(out=d, in_=view.ap), reads=[view], dma="dbg")

    def din(name, shape, dt=F32):
        return nc.dram_tensor(name, list(shape), dt, kind="ExternalInput").ap()

    def dscr(name, shape, dt=BF16):
        return nc.dram_tensor(name, list(shape), dt, kind="Internal").ap()

    xs = [din("x%d" % i, [S, D]) for i, S in enumerate(seq_lens)]
    ys = [nc.dram_tensor("y%d" % i, [S, D], F32, kind="ExternalOutput").ap() for i, S in enumerate(seq_lens)]
    w_in_d = din("w_in", [D, INW])
    w_ap_d = din("w_attn_proj", [D, D])
    w_rp_d = din("w_rec_proj", [LW, D])
    w_out_d = din("w_out", [D, D])
    w_fi_d = din("w_ffn_in", [D, 2 * DFF])
    w_fo_d = din("w_ffn_out", [DFF, D])
    lwa_d = din("lru_w_a", [2, LB, 128, 128])
    lwi_d = din("lru_w_i", [2, LB, 128, 128])
    vecs_d = din("vecs", [128, NVEC])
    gpost_d = din("gpost", [2, D])
    sink_d = din("sink", [1, NQ])
    rope_d = din("rope", [32, 2, smax + 256])

    Win_p = dscr("Win_p", [128, INW // 256, KB, 256])
    Wap_p = dscr("Wap_p", [128, D // 256, KB, 256])
    Wrp_p = dscr("Wrp_p", [128, D // 256, LB, 256])
    Wout_p = dscr("Wout_p", [128, 2, KB, 512])
    Wfi_p = dscr("Wfi_p", [128, 2 * DFF // 256, KB, 256])
    Wfo_p = dscr("Wfo_p", [128, 2, FB, 512])
    Wg_b = dscr("Wg_b", [128, 2, 2 * LB, 128])
    recT_d = [dscr("recT%d" % i, [128, LB, S]) for i, S in enumerate(seq_lens)]

    Smax = max(seq_lens)
    vecs = P.sb("vecs", [128, NVEC], F32)
    der = P.sb("der", [128, 64], F32)
    der2 = P.sb("der2", [128, 64], F32)
    gpost = P.sb("gpost", [128, 2, D], F32)
    esink = P.sb("esink", [128, NQ], F32)
    ident = P.sb("ident", [128, 128], BF16)
    ones = P.sb("ones", [128, 128], BF16)
    mlo = P.sb("mlo", [128, 128], BF16)
    mhi = P.sb("mhi", [128, 128], BF16)
    prot = P.sb("prot", [128, 32], BF16)
    cst = P.sb("cst", [128, 8], F32)
    stA = P.sb("stA", [128, 64], F32)
    stO = [P.sb("stO%d" % i, [128, 64], F32) for i in range(2)]
    stF = [P.sb("stF%d" % i, [128, 64], F32) for i in range(2)]
    stY = [P.sb("stY%d" % i, [128, 64], F32) for i in range(2)]
    hT = P.sb("hT", [128, KB, Smax], BF16)
    ARENA0 = P.sb_off

    PSB = [P.ps("ps%d" % b, b, F32) for b in range(8)]
    PSH = [P.ps("psh%d" % b, b, BF16) for b in range(8)]

    def affsel(v, pattern, cmp, fill, base, cm):
        P.op("pool", lambda e: e.affine_select(out=v.ap, in_=v.ap, pattern=pattern, compare_op=cmp, fill=fill, base=base, channel_multiplier=cm),
             reads=[v], writes=[v])

    scr = [P.sb("cscr%d" % i, [128, 128], F32, off=ARENA0 + 512 * i) for i in range(4)]
    k.memset("pool", scr[0][:, :], 1.0)
    affsel(scr[0][:, :], [[-1, 128]], ALU.is_equal, 0.0, 0, 1)
    k.copy("pool", ident[:, :], scr[0][:, :])
    k.memset("pool", ones[:, :], 1.0)
    k.memset("pool", scr[1][:, :], 0.0)
    affsel(scr[1][:, :], [[1, 128]], ALU.is_ge, NEG, 0, -1)
    k.copy("pool", mlo[:, :], scr[1][:, :])
    k.memset("pool", scr[2][:, :], 0.0)
    affsel(scr[2][:, :], [[-1, 128]], ALU.is_ge, NEG, 0, 1)
    k.copy("pool", mhi[:, :], scr[2][:, :])
    k.memset("pool", scr[3][:, 0:32], 0.0)
    affsel(scr[3][:, 0:16], [[-1, 16]], ALU.not_equal, -1.0, -16, 1)
    affsel(scr[3][:, 16:32], [[-1, 16]], ALU.not_equal, 1.0, 0, 1)
    k.copy("pool", prot[:, :], scr[3][:, 0:32])
    k.memset("pool", cst[:, 0:1], 0.25)
    k.memset("pool", cst[:, 1:2], EPS)
    k.memset("pool", cst[:, 2:3], 1.0)

    k.dma("m0", vecs[:, :], vecs_d)
    k.dma("m1", gpost[:, 0, :], gpost_d[0:1, :].partition_broadcast(128))
    k.dma("m2", gpost[:, 1, :], gpost_d[1:2, :].partition_broadcast(128))
    k.dma("m3", esink[:, :], sink_d[0:1, :].partition_broadcast(128))
    k.act(esink[:, :], esink[:, :], AF.Exp)
    k.ts("dve", der[:, 0:20], vecs[:, VC_BA:VC_BA + 20], 0.5, None, ALU.mult)
    k.ts("dve", der[:, 20:40], vecs[:, VC_BI:VC_BI + 20], 0.5, None, ALU.mult)
    k.act(der[:, 40:60], vecs[:, VC_LAM:VC_LAM + 20], AF.Exp, scale=-1.0)
    k.act(der[:, 40:60], der[:, 40:60], AF.Ln, bias=cst[:, 2:3])
    k.ts("dve", der2[:, 0:20], der[:, 40:60], -4.0, None, ALU.mult)
    k.ts("dve", der2[:, 20:40], der[:, 40:60], -8.0, None, ALU.mult)

    CW = 2048
    stg_f = [P.sb("stgf%d" % i, [128, CW], F32, off=ARENA0 + 4096 + i * (CW * 6)) for i in range(3)]
    stg_b = [P.sb("stgb%d" % i, [128, CW], BF16, off=ARENA0 + 4096 + i * (CW * 6) + CW * 4) for i in range(3)]
    cvi = [0]
    engs = ["act", "dve"]

    cvjobs = []

    def convert(src_ap, dst_ap, width, scale_v, ld3=None, st3=None):
        cvjobs.append((src_ap, dst_ap, width, scale_v, ld3, st3))

    def cv_load(i):
        src_ap, dst_ap, width, scale_v, ld3, st3 = cvjobs[i]
        sf = stg_f[i % 3]
        o_ap = sf[:, 0:width].ap if ld3 is None else sf.t[:, 0:width].rearrange("p (a b) -> p a b", a=ld3)
        P.op("sp", lambda e: e.dma_start(out=o_ap, in_=src_ap), writes=[sf[:, 0:width]], dma="cvl%d" % (i % 3))

    def cv_store(i):
        src_ap, dst_ap, width, scale_v, ld3, st3 = cvjobs[i]
        sf, sb_ = stg_f[i % 3], stg_b[i % 3]
        i_ap = sb_[:, 0:width].ap if st3 is None else sb_.t[:, 0:width].rearrange("p (a b) -> p a b", a=st3)
        eng = engs[i % 2]
        if scale_v is None:
            k.copy(eng, sb_[:, 0:width], sf[:, 0:width])
        elif eng == "act":
            k.act(sb_[:, 0:width], sf[:, 0:width], AF.Copy, scale=scale_v)
        else:
            k.ts(eng, sb_[:, 0:width], sf[:, 0:width], scale_v, None, ALU.mult)
        P.op("sp", lambda e: e.dma_start(out=dst_ap, in_=i_ap), reads=[sb_[:, 0:width]], dma="cvs%d" % (i % 3))

    def cv_run():
        n = len(cvjobs)
        for i in range(n + 2):
            if i < n:
                cv_load(i)
            if i >= 2:
                cv_store(i - 2)

    def convert_cols(src, dst, nkb, ncols, scale_col):
        for kb in range(nkb):
            for c0 in range(0, ncols, CW):
                w = min(CW, ncols - c0)
                sv = vecs[:, scale_col + kb:scale_col + kb + 1] if scale_col is not None else None
                convert(src[kb * 128:(kb + 1) * 128, c0:c0 + w], dst[:, c0 // 256:(c0 + w) // 256, kb, :], w, sv, st3=w // 256)

    def convert_rows(src, dst, nkb):
        for kb in range(nkb):
            convert(src[kb * 128:(kb + 1) * 128, :], dst[:, :, kb, :], D, None, st3=2)

    for gi, src in enumerate((lwa_d, lwi_d)):
        for dr in range(2):
            convert(src[dr].rearrange("n c d -> c n d"), Wg_b[:, gi, dr * LB:(dr + 1) * LB, :], LB * 128, None, ld3=LB, st3=LB)
    convert_cols(w_in_d, Win_p, KB, INW, VC_GPRE)
    convert_cols(w_ap_d, Wap_p, KB, D, None)
    convert_cols(w_rp_d, Wrp_p, LB, D, None)
    convert_rows(w_out_d, Wout_p, KB)
    convert_cols(w_fi_d, Wfi_p, KB, 2 * DFF, VC_GFFN)
    convert_rows(w_fo_d, Wfo_p, FB)
    cv_run()
    P.stream_wait(["cvs0", "cvs1", "cvs2"])

    WSLOT = 5632
    NWS = 8
    wslots = [P.sb("wslot%d" % i, [128, WSLOT // 2], BF16, off=ARENA0 + i * WSLOT) for i in range(NWS)]
    A2 = ARENA0 + NWS * WSLOT
    o = [A2]

    def take(n):
        r = o[0]
        o[0] = (r + n + GRAN - 1) // GRAN * GRAN
        assert o[0] <= P.sb_top, ("arena overflow", o[0], P.sb_top)
        return r

    SF = Smax * 4
    o[0] = ARENA0
    R_off = take(SF + 16)
    XC_off = take(SF)
    XCB_off = take(Smax * 2)
    B_off = [take(SF) for _ in range(4)]
    RB_off = XCB_off
    W1_off = [take(4096) for _ in range(3)]
    GW_off = take(2 * 2 * LB * 128 * 2)
    p1_end = o[0]
    o[0] = R_off
    X1_off = [take(4096) for _ in range(8)]
    XN_off = [take(2048) for _ in range(2)]
    JUNK1_off = take(2048)
    p1_end = max(p1_end, o[0])
    o[0] = A2
    XT_off = take(4 * 4096)
    QT_off = take(NQ * T * 2)
    AT_off = take(NQ * T * 2)
    REC_off = take(LB * T * 2)
    TMP_off = [take(2048) for _ in range(8)]
    JUNK_off = REC_off
    XN2_off = [REC_off + 2048, REC_off + 4096]
    ROPE_off = take(32 * 0 + 2 * 768 * 4)
    G_off = take(FB * T * 2)
    p2_end = o[0]

    xt = P.sb("xt", [128, 4, D], F32, off=XT_off)
    qT = P.sb("qT", [128, NQ, T], BF16, off=QT_off)
    mgT = qT
    atT = P.sb("atT", [128, NQ, T], BF16, off=AT_off)
    h2T = atT
    recc = P.sb("recc", [128, LB, T], BF16, off=REC_off)
    tmp = [P.sb("tmp%d" % i, [128, T], F32, off=TMP_off[i]) for i in range(8)]
    junk = P.sb("junk", [128, D], BF16, off=JUNK_off)
    xn2 = [P.sb("xn2_%d" % i, [128, D], BF16, off=XN2_off[i]) for i in range(2)]
    kT = P.sb("kT", [128, 2, 768], BF16, off=G_off)
    Vt = P.sb("Vt", [128, 6, 256], BF16, off=G_off + 3072)
    PT = [P.sb("PT%d" % i, [128, 384], BF16, off=G_off + 6144 + i * 1024) for i in range(4)]
    ropeC = P.sb("ropeC", [32, 2, 768], F32, off=ROPE_off)
    acT = P.sb("acT", [128, FB, T], BF16, off=G_off)
    w1slots = [P.sb("w1slot%d" % i, [128, KB * 256], BF16, off=W1_off[i]) for i in range(3)]
    GW = P.sb("GW", [128, 2, 2 * LB, 128], BF16, off=GW_off)
    x1 = [P.sb("x1_%d" % i, [128, D], F32, off=X1_off[i]) for i in range(8)]
    xn = [P.sb("xn_%d" % i, [128, D], BF16, off=XN_off[i]) for i in range(2)]
    junk1 = P.sb("junk1", [128, D], BF16, off=JUNK1_off)

    FO_PIECES = [(0, 5), (5, 10), (10, 15), (15, 20), (20, 22)]
    ws2 = WStream(k, "w", wslots, eng=WENG)
    ws2_lim = []
    for si, S in enumerate(seq_lens):
        ws2_lim.append(0)
        for c in range(S // T):
            ws2.add("k", Win_p[:, O1 // 256], [128, KB, 256])
            ws2.add("v", Win_p[:, O2 // 256], [128, KB, 256])
            for hp in range(4):
                ws2.add("q%d" % hp, Win_p[:, hp], [128, KB, 256])
            for qt in range(4):
                ws2.add("ga%d" % qt, Win_p[:, O5 // 256 + qt], [128, KB, 256])
                ws2.add("pa%d" % qt, Wap_p[:, qt], [128, KB, 256])
                ws2.add("gr%d" % qt, Win_p[:, O6 // 256 + qt], [128, KB, 256])
                ws2.add("pr%d" % qt, Wrp_p[:, qt], [128, LB, 256])
            for half in range(2):
                for kg in range(2):
                    ws2.add("wo%d%d" % (half, kg), Wout_p[:, half, kg * 4:(kg + 1) * 4, :], [128, 4, 512])
            for g2 in range(FB // 2):
                ws2.add("fg%d" % g2, Wfi_p[:, g2], [128, KB, 256])
                ws2.add("fu%d" % g2, Wfi_p[:, DFF // 256 + g2], [128, KB, 256])
            for half in range(2):
                for (j0, j1) in FO_PIECES:
                    ws2.add("fo%d_%d" % (half, j0), Wfo_p[:, half, j0:j1, :], [128, j1 - j0, 512])
        ws2_lim[si] = len(ws2.plan)

    def rstd_from(st, src_cols, n, dst0):
        k.ts("dve", st[:, 40:40 + n], st[:, src_cols:src_cols + n], 1.0 / D, cst[:, 1:2], ALU.mult, ALU.add)
        k.act(st[:, 40:40 + n], st[:, 40:40 + n], AF.Sqrt)
        k.recip(st[:, dst0:dst0 + n], st[:, 40:40 + n])

    def rope(raw, kvh, c_lo, n, bank_raw, bank_sw, tA, tB, tab_lo):
        k.mm(bank_sw[0:32, 0:n], prot[:, 0:32], raw[:, kvh, c_lo:c_lo + n])
        k.tt("dve", tA[0:32, 0:n], bank_sw[0:32, 0:n], ropeC[0:32, 1, tab_lo:tab_lo + n], ALU.mult)
        k.tt("dve", tB[0:32, 0:n], bank_raw[0:32, 0:n], ropeC[0:32, 0, tab_lo:tab_lo + n], ALU.mult)
        k.tt("dve", raw[0:32, kvh, c_lo:c_lo + n], tA[0:32, 0:n], tB[0:32, 0:n], ALU.add)

    for si, S in enumerate(seq_lens):
        NT = S // 128
        NCH = S // T
        xd = xs[si]
        P.stage = "p1a"
        for g in range(NT // 4):
            for j in range(4):
                t = g * 4 + j
                k.dma("x1l%d" % (t % 8), x1[t % 8][:, :], xd[t * 128:(t + 1) * 128, :])
            for j in range(4):
                t = g * 4 + j
                k.act(junk1[:, :], x1[t % 8][:, :], AF.Square, accum=stA[:, j:j + 1])
            rstd_from(stA, 0, 4, 8)
            for j in range(4):
                t = g * 4 + j
                xnb = xn[t % 2]
                k.act(xnb[:, :], x1[t % 8][:, :], AF.Copy, scale=stA[:, 8 + j:9 + j])
                bank = t % 2
                for kb in range(KB):
                    k.tr(PSH[bank][:, kb * 128:(kb + 1) * 128], xnb[:, kb * 128:(kb + 1) * 128], ident[:, :])
                srcv = PSH[bank][:, :]
                dstv = hT[:, :, t * 128:(t + 1) * 128]
                srcap = PSH[bank].t[:, :].rearrange("p (a b) -> p a b", a=KB)
                if t % 2:
                    P.op("dve", lambda e, d=dstv, s=srcap: e.tensor_copy(out=d.ap, in_=s), reads=[srcv], writes=[dstv])
                else:
                    P.op("act", lambda e, d=dstv, s=srcap: e.activation(out=d.ap, in_=s, func=AF.Copy), reads=[srcv], writes=[dstv])

        if si == 0:
            dbg("hT", hT[:, :, 0:S], [128, KB, S], BF16)
        P.stage = "p1b"
        Rb = P.sb("R_%d" % si, [128, S + 4], F32, off=R_off)
        XC = P.sb("XC_%d" % si, [128, S], F32, off=XC_off)
        XCB = P.sb("XCB_%d" % si, [128, S], BF16, off=XCB_off)
        B = [P.sb("B%d_%d" % (i, si), [128, S], F32, off=B_off[i]) for i in range(4)]
        RB = P.sb("RB_%d" % si, [128, S], BF16, off=RB_off)
        ws1 = WStream(k, "v", w1slots)
        k.dma("m0", GW[:, :, :, :], Wg_b)
        for bp in range(LB // 2):
            ws1.add("rx", Win_p[:, O3 // 256 + bp], [128, KB, 256])
            ws1.add("gz", Win_p[:, O4 // 256 + bp], [128, KB, 256])
        for blk in range(LB):
            if blk % 2 == 0:
                wrx = ws1.next("rx", 2)
            k.memset("pool", Rb[:, 0:2], 0.0)
            k.memset("pool", Rb[:, S + 2:S + 4], 0.0)
            for c in range(NCH):
                bank = PSB[c % 2]
                for kb in range(KB):
                    k.mm(bank[:, :], wrx[:, kb * 256 + (blk % 2) * 128:kb * 256 + (blk % 2 + 1) * 128], hT[:, kb, c * T:(c + 1) * T], start=(kb == 0), stop=(kb == KB - 1))
                k.act(Rb[:, 2 + c * T:2 + (c + 1) * T], bank[:, :], AF.Copy)

            def cw(tp, blk=blk):
                return vecs[:, VC_CONVW + blk * 4 + tp:VC_CONVW + blk * 4 + tp + 1]

            halves = [(0, S // 2), (S // 2, S)] if S >= 1024 else [(0, S)]
            for (a0, b0) in halves:
                k.ts("dve", XC[:, a0:b0], Rb[:, a0 + 2:b0 + 2], cw(2), vecs[:, VC_CONVB + blk:VC_CONVB + blk + 1], ALU.mult, ALU.add)
                for tp in (0, 1, 3):
                    k.stt("dve", XC[:, a0:b0], Rb[:, a0 + tp:b0 + tp], cw(tp), XC[:, a0:b0], ALU.mult, ALU.add)
                k.copy("dve", XCB[:, a0:b0], XC[:, a0:b0])
            if si == 0 and blk == 0:
                dbg("xc0", XC[:, :], [128, S], F32)
            for dr in range(2):
                THA, THI, AA = (B[0], B[1], B[2]) if dr == 0 else (Rb, B[3], B[2])
                col = dr * LB + blk
                for c in range(NCH):
                    ba, bi = PSB[2 + (c % 2) * 2], PSB[3 + (c % 2) * 2]
                    k.mm(ba[:, :], GW[:, 0, col, :], XCB[:, c * T:(c + 1) * T])
                    k.mm(bi[:, :], GW[:, 1, col, :], XCB[:, c * T:(c + 1) * T])
                    k.act(THA[:, c * T:(c + 1) * T], ba[:, :], AF.Tanh, scale=0.5, bias=der[:, col:col + 1])
                    k.act(THI[:, c * T:(c + 1) * T], bi[:, :], AF.Tanh, scale=0.5, bias=der[:, 20 + col:21 + col])
                if dr == 1:
                    if blk % 2 == 0:
                        wgt = ws1.next("gz", 2)
                    for c in range(NCH):
                        bank = PSB[6 + c % 2]
                        for kb in range(KB):
                            k.mm(bank[:, :], wgt[:, kb * 256 + (blk % 2) * 128:kb * 256 + (blk % 2 + 1) * 128], hT[:, kb, c * T:(c + 1) * T], start=(kb == 0), stop=(kb == KB - 1))
                        k.act(B[1][:, c * T:(c + 1) * T], bank[:, :], AF.Gelu_apprx_tanh)
                for (a0, b0) in halves:
                    k.act(AA[:, a0:b0], THA[:, a0:b0], AF.Exp, scale=der2[:, col:col + 1], bias=der2[:, col:col + 1])
                    k.tt("dve", THA[:, a0:b0], AA[:, a0:b0], AA[:, a0:b0], ALU.mult)
                    k.stt("dve", THI[:, a0:b0], THI[:, a0:b0], 1.0, XC[:, a0:b0], ALU.add, ALU.mult)
                for (a0, b0) in halves:
                    k.act(THA[:, a0:b0], THA[:, a0:b0], AF.Sqrt, scale=-0.25, bias=cst[:, 0:1])
                    k.tt("dve", THI[:, a0:b0], THI[:, a0:b0], THA[:, a0:b0], ALU.mult)
                if dr == 0:
                    for hi_, (a0, b0) in enumerate(halves):
                        init = 0.0 if hi_ == 0 else THA[:, a0 - 1:a0]
                        k.scan(THA[:, a0:b0], AA[:, a0:b0], THI[:, a0:b0], init)
                else:
                    for hi_, (a0, b0) in enumerate(reversed(halves)):
                        init = 0.0 if hi_ == 0 else THI[:, b0:b0 + 1]
                        rs = slice(b0 - 1, (a0 - 1 if a0 > 0 else None), -1)
                        k.scan(THI[:, rs], AA[:, rs], THI[:, rs], init)
            if si == 0 and blk == 0:
                dbg("hf0", B[0][:, :], [128, S], F32)
                dbg("hb0", B[3][:, :], [128, S], F32)
            for (a0, b0) in halves:
                k.tt("dve", B[0][:, a0:b0], B[0][:, a0:b0], B[3][:, a0:b0], ALU.add)
                k.tt("dve", RB[:, a0:b0], B[0][:, a0:b0], B[1][:, a0:b0], ALU.mult)
            k.dma("recst", recT_d[si][:, blk, :], RB[:, :])
            if si == 0 and blk == 0:
                dbg("rec0", RB[:, :], [128, S], BF16)
        P.stream_wait(["recst"])

        ws2.limit = ws2_lim[si]
        for c in range(NCH):
            c0 = c * T
            lo, hi = max(0, c0 - 128), min(S, c0 + T + 128)
            W = hi - lo
            tl_lo, tl_hi = lo // 128, hi // 128
            qt0 = c0 // 128
            if c == 0:
                k.dma("rope", ropeC[0:32, :, 0:W], rope_d[:, :, lo + 128:hi + 128])
            k.dma("recl", recc[:, :, :], recT_d[si][:, :, c0:c0 + T])
            k.dma_group("xt", [(xt[:, j, :], xd[c0 + j * 128:c0 + (j + 1) * 128, :]) for j in range(4)])
            P.stage = "K"
            wk = ws2.next("k")
            pieces = [(a, min(a + 512, hi)) for a in range(lo, hi, 512)]
            pend = None
            for kv in range(2):
                for pi, (a, b) in enumerate(pieces):
                    n = b - a
                    bank, bsw = PSB[(kv * 2 + pi) % 4], PSB[4 + (kv * 2 + pi) % 2]
                    for kb in range(KB):
                        k.mm(bank[:, 0:n], wk[:, kb * 256 + kv * 128:kb * 256 + (kv + 1) * 128], hT[:, kb, a:b], start=(kb == 0), stop=(kb == KB - 1))
                    k.act(kT[:, kv, a - lo:b - lo], bank[:, 0:n], AF.Copy)
                    if pend is not None:
                        rope(*pend)
                    pend = (kT, kv, a - lo, n, bank, bsw, tmp[((kv * 2 + pi) * 2) % 8], tmp[((kv * 2 + pi) * 2 + 1) % 8], a - lo)
            P.stage = "V"
            wv = ws2.next("v")
            for jt in range(tl_lo, tl_hi):
                bank = PSB[6 + jt % 2]
                for kb in range(KB):
                    k.mm(bank[:, 0:256], hT[:, kb, jt * 128:(jt + 1) * 128], wv[:, kb * 256:(kb + 1) * 256], start=(kb == 0), stop=(kb == KB - 1))
                k.copy("act" if jt % 2 else "dve", Vt[:, jt - tl_lo, :], bank[:, 0:256])
                if pend is not None:
                    rope(*pend)
                    pend = None
            P.stage = "Q"
            for h in range(NQ):
                if h % 2 == 0:
                    wv_ = ws2.next("q%d" % (h // 2))
                bank, bsw = PSB[h % 4], PSB[4 + h % 2]
                for kb in range(KB):
                    k.mm(bank[:, :], wv_[:, kb * 256 + (h % 2) * 128:kb * 256 + (h % 2 + 1) * 128], hT[:, kb, c0:c0 + T], start=(kb == 0), stop=(kb == KB - 1))
                k.act(qT[:, h, :], bank[:, :], AF.Copy)
                if pend is not None:
                    rope(*pend)
                pend = (qT, h, 0, T, bank, bsw, tmp[(h * 2) % 8], tmp[(h * 2 + 1) % 8], c0 - lo)
            rope(*pend)
            pend = None
            if c + 1 < NCH:
                nlo, nhi = max(0, c0 + T - 128), min(S, c0 + 2 * T + 128)
                k.dma("rope", ropeC[0:32, :, 0:nhi - nlo], rope_d[:, :, nlo + 128:nhi + 128])
            if si == 0 and c == 0:
                dbg("qT", qT[:, :, :], [128, NQ, T], BF16)
                dbg("kT", kT[:, :, 0:W], [128, 2, W], BF16)
                dbg("Vt", Vt[:, 0:tl_hi - tl_lo, :], [128, tl_hi - tl_lo, 256], BF16)
            P.stage = "att"
            items = [(h, jt) for h in range(NQ) for jt in range(tl_lo, tl_hi)]

            def att_front(n_):
                h, jt = items[n_]
                kv = h // 4
                qs = [i for i in (jt - 1, jt, jt + 1) if qt0 <= i < qt0 + 4]
                qa, qb = (qs[0] - qt0) * 128, (qs[-1] - qt0 + 1) * 128
                n = qb - qa
                sb_ = PSB[n_ % 4]
                pt = PT[n_ % 4]
                k.mm(sb_[:, 0:n], kT[:, kv, (jt - tl_lo) * 128:(jt - tl_lo + 1) * 128], qT[:, h, qa:qb], start=True, stop=True, nogroup=True)
                for i in qs:
                    off = (i - qt0) * 128 - qa
                    if i == jt - 1:
                        k.mm(sb_[:, off:off + 128], ident[:, :], mlo[:, :], start=False, stop=True, nogroup=True)
                    elif i == jt + 1:
                        k.mm(sb_[:, off:off + 128], ident[:, :], mhi[:, :], start=False, stop=True, nogroup=True)
                k.act(pt[:, 0:n], sb_[:, 0:n], AF.Exp, scale=SCALE)
                return (h, jt, kv, qa, qb, n, pt)

            def att_back(info):
                h, jt, kv, qa, qb, n, pt = info
                bO, bD = PSB[4 + (h % 2) * 2], PSB[5 + (h % 2) * 2]
                first = (jt == tl_lo)
                last = (jt == tl_hi - 1)
                k.mm(bO[:, qa:qb], Vt[:, jt - tl_lo, kv * 128:(kv + 1) * 128], pt[:, 0:n], start=first, stop=last, nogroup=True)
                k.mm(bD[:, qa:qb], ones[:, :], pt[:, 0:n], start=first, stop=last, nogroup=True)
                if last:
                    td = tmp[h % 2]
                    k.act(td[:, :], bD[:, :], AF.Ln, bias=esink[:, h:h + 1])
                    k.act(td[:, :], td[:, :], AF.Exp, scale=-1.0)
                    k.tt("dve", atT[:, h, :], bO[:, :], td[:, :], ALU.mult)

            prev = None
            for n_ in range(len(items)):
                info = att_front(n_)
                if prev is not None:
                    att_back(prev)
                prev = info
            att_back(prev)
            if si == 0 and c == 0:
                dbg("atT", atT[:, :, :], [128, NQ, T], BF16)
            P.stage = "merge"
            for qt in range(4):
                tg = [tmp[(qt % 2) * 4 + m] for m in range(2)]
                tr_ = [tmp[(qt % 2) * 4 + 2 + m] for m in range(2)]
                bks = [PSB[(qt % 2) * 4 + i] for i in range(4)]
                wga = ws2.next("ga%d" % qt)
                for m in range(2):
                    for kb in range(KB):
                        k.mm(bks[m][:, :], wga[:, kb * 256 + m * 128:kb * 256 + (m + 1) * 128], hT[:, kb, c0:c0 + T], start=(kb == 0), stop=(kb == KB - 1))
                    k.act(tg[m][:, :], bks[m][:, :], AF.Sigmoid)
                wpa = ws2.next("pa%d" % qt)
                for m in range(2):
                    for kb in range(NQ):
                        k.mm(bks[2 + m][:, :], wpa[:, kb * 256 + m * 128:kb * 256 + (m + 1) * 128], atT[:, kb, :], start=(kb == 0), stop=(kb == NQ - 1))
                    k.tt("dve", tg[m][:, :], bks[2 + m][:, :], tg[m][:, :], ALU.mult)
                wgr = ws2.next("gr%d" % qt)
                for m in range(2):
                    for kb in range(KB):
                        k.mm(bks[m][:, :], wgr[:, kb * 256 + m * 128:kb * 256 + (m + 1) * 128], hT[:, kb, c0:c0 + T], start=(kb == 0), stop=(kb == KB - 1))
                    k.act(tr_[m][:, :], bks[m][:, :], AF.Sigmoid)
                wpr = ws2.next("pr%d" % qt)
                for m in range(2):
                    for kb in range(LB):
                        k.mm(bks[2 + m][:, :], wpr[:, kb * 256 + m * 128:kb * 256 + (m + 1) * 128], recc[:, kb, :], start=(kb == 0), stop=(kb == LB - 1))
                    k.tt("dve", tr_[m][:, :], bks[2 + m][:, :], tr_[m][:, :], ALU.mult)
                    k.tt("dve", mgT[:, qt * 2 + m, :], tg[m][:, :], tr_[m][:, :], ALU.add)
            wo = [[ws2.next("wo00", 1), ws2.next("wo01", 2)], [ws2.next("wo10", 3), ws2.next("wo11", 4)]]

            def wout_tile(i):
                P.stage = "wout"
                st = stO[i % 2]
                for half in range(2):
                    bank = PSB[(i % 2) * 2 + half]
                    for kb in range(KB):
                        k.mm(bank[:, :], mgT[:, kb, i * 128:(i + 1) * 128], wo[half][kb // 4][:, (kb % 4) * 512:(kb % 4 + 1) * 512], start=(kb == 0), stop=(kb == KB - 1))
                    k.act(junk[:, half * 512:(half + 1) * 512], bank[:, :], AF.Square, accum=st[:, half:half + 1])
                k.tt("dve", st[:, 2:3], st[:, 0:1], st[:, 1:2], ALU.add)
                rstd_from(st, 2, 1, 4)
                for half in range(2):
                    bank = PSB[(i % 2) * 2 + half]
                    tb = tmp[(i % 2) * 2 + half]
                    k.stt("dve", tb[:, :], bank[:, :], st[:, 4:5], gpost[:, 0, half * 512:(half + 1) * 512], ALU.mult, ALU.mult)
                    k.tt("dve", xt[:, i, half * 512:(half + 1) * 512], xt[:, i, half * 512:(half + 1) * 512], tb[:, :], ALU.add)
                sf = stF[i % 2]
                k.act(junk[:, :], xt[:, i, :], AF.Square, accum=sf[:, 0:1])
                rstd_from(sf, 0, 1, 8)
                k.act(xn2[i % 2][:, :], xt[:, i, :], AF.Copy, scale=sf[:, 8:9])

            def ffn_tr_tile(i):
                P.stage = "ffnT"
                xnb = xn2[i % 2]
                bank = 4 + i % 2
                for kb in range(KB):
                    k.tr(PSH[bank][:, kb * 128:(kb + 1) * 128], xnb[:, kb * 128:(kb + 1) * 128], ident[:, :])
                srcv = PSH[bank][:, :]
                dstv = h2T[:, :, i * 128:(i + 1) * 128]
                srcap = PSH[bank].t[:, :].rearrange("p (a b) -> p a b", a=KB)
                if i % 2:
                    P.op("dve", lambda e, d=dstv, s=srcap: e.tensor_copy(out=d.ap, in_=s), reads=[srcv], writes=[dstv])
                else:
                    P.op("act", lambda e, d=dstv, s=srcap: e.activation(out=d.ap, in_=s, func=AF.Copy), reads=[srcv], writes=[dstv])

            if si == 0 and c == 0:
                dbg("mgT", mgT[:, :, :], [128, NQ, T], BF16)
            if OPT_B:
                for i in range(4):
                    wout_tile(i)
                    if i >= 1:
                        ffn_tr_tile(i - 1)
                ffn_tr_tile(3)
            else:
                for i in range(4):
                    wout_tile(i)
                    ffn_tr_tile(i)
            if si == 0 and c == 0:
                dbg("x1", xt[:, :, :], [128, 4, D], F32)
            P.stage = "ffnin"
            jj = 0
            for g2 in range(FB // 2):
                wg_ = ws2.next("fg%d" % g2, 1)
                wu_ = ws2.next("fu%d" % g2, 2)
                for m in range(2):
                    j = g2 * 2 + m
                    bG, bU = PSB[(jj % 2) * 2], PSB[(jj % 2) * 2 + 1]
                    tb = tmp[4 + jj % 4]
                    jj += 1
                    for kb in range(KB):
                        k.mm(bG[:, :], wg_[:, kb * 256 + m * 128:kb * 256 + (m + 1) * 128], h2T[:, kb, :], start=(kb == 0), stop=(kb == KB - 1))
                    for kb in range(KB):
                        k.mm(bU[:, :], wu_[:, kb * 256 + m * 128:kb * 256 + (m + 1) * 128], h2T[:, kb, :], start=(kb == 0), stop=(kb == KB - 1))
                    k.act(tb[:, :], bG[:, :], AF.Silu)
                    k.tt("dve", acT[:, j, :], bU[:, :], tb[:, :], ALU.mult)
            if si == 0 and c == 0:
                dbg("acT", acT[:, :, :], [128, FB, T], BF16)
            P.stage = "ffnout"
            for half in range(2):
                for (j0, j1) in FO_PIECES:
                    wf = ws2.next("fo%d_%d" % (half, j0))
                    for i in range(4):
                        bank = PSB[half * 4 + i]
                        for j in range(j0, j1):
                            k.mm(bank[:, :], acT[:, j, i * 128:(i + 1) * 128], wf[:, (j - j0) * 512:(j - j0 + 1) * 512], start=(j == 0), stop=(j == FB - 1))
            for i in range(4):
                st = stY[i % 2]
                for half in range(2):
                    k.act(junk[:, 0:512], PSB[half * 4 + i][:, :], AF.Square, accum=st[:, half:half + 1])
                k.tt("dve", st[:, 2:3], st[:, 0:1], st[:, 1:2], ALU.add)
                rstd_from(st, 2, 1, 4)
                for half in range(2):
                    tb = tmp[(i % 2) * 2 + half]
                    k.stt("dve", tb[:, :], PSB[half * 4 + i][:, :], st[:, 4:5], gpost[:, 1, half * 512:(half + 1) * 512], ALU.mult, ALU.mult)
                    k.tt("dve", xt[:, i, half * 512:(half + 1) * 512], xt[:, i, half * 512:(half + 1) * 512], tb[:, :], ALU.add)
                k.dma("yst%d" % i, ys[si][c0 + i * 128:c0 + (i + 1) * 128, :], xt[:, i, :])
    P.emit(final_streams=["yst0", "yst1", "yst2", "yst3", "dbg"])
    import os
    if os.environ.get("MK_DUMP_STAGES"):
        with open(os.environ["MK_DUMP_STAGES"], "w") as f:
            for e in ("pe", "act", "dve"):
                f.write(e + ":" + ",".join(o.stage for o in P.ops[e] if o.fn is not None) + "\n")
    return nc


def host_prep(inputs, smax):
    f = np.float32
    g = lambda n: np.ascontiguousarray(np.asarray(inputs[n], dtype=f)[0])
    conv_w, conv_b = g("conv_w"), g("conv_b")
    vec = np.zeros((128, NVEC), f)
    vec[:, VC_CONVW:VC_CONVW + 40] = conv_w.reshape(4, LB, 128).transpose(2, 1, 0).reshape(128, 40)
    vec[:, VC_CONVB:VC_CONVB + 10] = conv_b.reshape(LB, 128).T
    vec[:, VC_BA:VC_BA + 20] = g("lru_b_a").reshape(2 * LB, 128).T
    vec[:, VC_BI:VC_BI + 20] = g("lru_b_i").reshape(2 * LB, 128).T
    vec[:, VC_LAM:VC_LAM + 20] = g("lru_lambda").reshape(2 * LB, 128).T
    vec[:, VC_GPRE:VC_GPRE + 8] = g("norm_mix_pre").reshape(KB, 128).T
    vec[:, VC_GFFN:VC_GFFN + 8] = g("norm_ffn_pre").reshape(KB, 128).T
    gp = np.stack([g("norm_mix_post"), g("norm_ffn_post")], 0)
    sink = g("attn_sink").reshape(1, NQ)
    half = 16
    inv = (np.float32(500000.0) ** (-np.arange(half, dtype=f) / np.float32(half))).astype(f)
    pos = np.arange(-128, smax + 128).astype(f)
    ang = (pos[None, :] * inv[:, None]).astype(f)
    rope = np.zeros((32, 2, smax + 256), f)
    rope[0:16, 0] = np.cos(ang.astype(np.float64))
    rope[16:32, 0] = np.cos(ang.astype(np.float64))
    rope[0:16, 1] = np.sin(ang.astype(np.float64))
    rope[16:32, 1] = np.sin(ang.astype(np.float64))
    common = {
        "w_in": g("w_in"), "w_attn_proj": g("w_attn_proj"), "w_rec_proj": g("w_rec_proj"), "w_out": g("w_out"),
        "w_ffn_in": g("w_ffn_in"), "w_ffn_out": g("w_ffn_out"), "lru_w_a": g("lru_w_a"), "lru_w_i": g("lru_w_i"),
        "vecs": vec, "gpost": gp, "sink": sink, "rope": rope,
    }
    return common


_CACHE = {}


def run_layer(inputs, per_core_seqs, n_cores):
    seq_lens = tuple(a.shape[0] for a in per_core_seqs[0])
    smax = max(seq_lens)
    if seq_lens not in _CACHE:
        _CACHE[seq_lens] = build_program(list(seq_lens), smax)
    nc = _CACHE[seq_lens]
    common = host_prep(inputs, smax)
    in_maps = []
    for cseqs in per_core_seqs:
        m = dict(common)
        for i, a in enumerate(cseqs):
            m["x%d" % i] = np.ascontiguousarray(a, dtype=np.float32)
        in_maps.append(m)
    res = run_bass_kernel_spmd(nc, in_maps, core_ids=list(range(n_cores)))
    global LAST_RES
    LAST_RES = res
    return [[r["y%d" % i] for i in range(len(seq_lens))] for r in res.results]


def kernel(**inputs):
    xp = np.asarray(inputs["x_prompt"], dtype=np.float32)
    xsm = np.asarray(inputs["x_sample"], dtype=np.float32)
    n = 8
    per_core = [[xp[2 * c], xp[2 * c + 1], xsm[c]] for c in range(n)]
    outs = run_layer(inputs, per_core, n)
    yp = np.empty_like(xp)
    ys = np.empty_like(xsm)
    for c in range(n):
        yp[2 * c], yp[2 * c + 1], ys[c] = outs[c][0], outs[c][1], outs[c][2]
    return (yp, ys)
```

```python
import numpy as np
import concourse.bass as bass
import concourse.mybir as mybir
from concourse.bass_utils import run_bass_kernel_spmd

F32 = mybir.dt.float32
BF16 = mybir.dt.bfloat16
I32 = mybir.dt.int32
AF = mybir.ActivationFunctionType
ALU = mybir.AluOpType
DTB = {F32: 4, BF16: 2, I32: 4}
GRAN = 256
COMPUTE = ("pe", "act", "dve", "pool")


class View:
    __slots__ = ("ap", "space", "lo", "hi")

    def __init__(self, ap, space, lo, hi):
        self.ap, self.space, self.lo, self.hi = ap, space, lo, hi


class Buf:
    def __init__(self, prog, name, shape, dt, space, off):
        self.prog, self.name, self.shape, self.dt, self.space, self.off = prog, name, list(shape), dt, space, off
        self.esz = DTB[dt]
        self.nbytes = int(np.prod(shape[1:])) * self.esz
        nc = prog.nc
        if space == "sb":
            self.t = nc.alloc_sbuf_tensor_at(name, self.shape, dt, offset=off)
        else:
            self.t = prog.psum_handle(name, self.shape, dt, off)
        st = [1]
        for s in reversed(self.shape[2:]):
            st.insert(0, st[0] * s)
        self.strides = st

    def __getitem__(self, idx):
        if not isinstance(idx, tuple):
            idx = (idx,)
        idx = tuple(idx) + (slice(None),) * (len(self.shape) - len(idx))
        lo = 0
        hi = 0
        for d in range(1, len(self.shape)):
            i = idx[d]
            n = self.shape[d]
            if isinstance(i, int):
                a, b = i, i
            else:
                r = range(*i.indices(n))
                assert len(r) > 0, (self.name, idx)
                a, b = min(r[0], r[-1]), max(r[0], r[-1])
            lo += a * self.strides[d - 1]
            hi += b * self.strides[d - 1]
        return View(self.t[idx], self.space, self.off + lo * self.esz, self.off + (hi + 1) * self.esz)


class Op:
    __slots__ = ("eng", "fn", "reads", "writes", "sem", "inc", "deps", "cnt", "need", "dma", "idx", "tag", "xw", "stage")


class Prog:
    def __init__(self, nc):
        self.nc = nc
        self.ops = {e: [] for e in COMPUTE + ("sp",)}
        self.allops = []
        self.state = {}
        self.sb_off = (nc.sbuf_base + GRAN - 1) // GRAN * GRAN
        self.sb_top = nc.sbuf_top
        self.ps_banks = {}
        self.dma_streams = {}
        self.dcount = {}
        self.stage = ""

    def sb(self, name, shape, dt, off=None):
        esz = DTB[dt]
        nb = int(np.prod(shape[1:])) * esz
        if off is None:
            off = self.sb_off
            self.sb_off = (off + nb + GRAN - 1) // GRAN * GRAN
            assert self.sb_off <= self.sb_top, ("SBUF overflow", name, self.sb_off)
        return Buf(self, name, shape, dt, "sb", off)

    def psum_handle(self, name, shape, dt, off):
        bank = off // 2048
        assert off % 2048 == 0 and int(np.prod(shape[1:])) * DTB[dt] <= 2048
        key = (bank, dt)
        if bank not in self.ps_banks:
            self.ps_banks[bank] = self.nc.alloc_psum_tensor("psb%d" % bank, [128, 512], F32)
        t = self.ps_banks[bank]
        return t

    def ps(self, name, bank, dt=F32, n=None):
        n = n or (2048 // DTB[dt])
        b = Buf.__new__(Buf)
        b.prog, b.name, b.dt, b.space, b.off = self, name, dt, "ps", bank * 2048
        b.esz = DTB[dt]
        b.shape = [128, n]
        b.nbytes = n * b.esz
        b.strides = [1]
        if bank not in self.ps_banks:
            self.ps_banks[bank] = self.nc.alloc_psum_tensor("psb%d" % bank, [128, 512], F32)
        t = self.ps_banks[bank]
        if dt != F32:
            b.t = _Bitcast(t, dt, n)
        else:
            b.t = t
        return b

    def _grans(self, v):
        if v.space in ("sb", "ps"):
            return [(v.space, g) for g in range(v.lo // GRAN, (v.hi - 1) // GRAN + 1)]
        return [(v.space, g) for g in range(v.lo, v.hi)]

    def op(self, eng, fn, reads=(), writes=(), dma=None, ndma=1, tag=""):
        o = Op()
        o.eng, o.fn, o.tag = eng, fn, tag
        o.dma = dma
        o.need = False
        o.idx = len(self.allops)
        o.xw = []
        o.cnt = None
        o.stage = self.stage
        if dma is not None:
            o.sem = "dma:" + dma
            o.inc = 16 * ndma
            self.dcount[o.sem] = self.dcount.get(o.sem, 0) + o.inc
            o.cnt = self.dcount[o.sem]
            o.need = True
        else:
            o.sem = eng
            o.inc = 1
        deps = {}
        for v in reads:
            for g in self._grans(v):
                s = self.state.get(g)
                if s and s[0] is not None:
                    deps[s[0].idx] = (s[0], "raw")
        for v in writes:
            for g in self._grans(v):
                s = self.state.get(g)
                if s:
                    if s[0] is not None and s[0].idx not in deps:
                        deps[s[0].idx] = (s[0], "waw")
                    for r in s[1].values():
                        if r.idx not in deps:
                            deps[r.idx] = (r, "war")
        keep = []
        for d, kind in deps.values():
            if d is o:
                continue
            same = (d.eng == eng) and d.dma is None and dma is None
            if same and eng == "pe":
                continue
            keep.append(d)
            d.need = True
        o.deps = keep
        rkey = eng if dma is None else ("dma", o.idx)
        for v in reads:
            for g in self._grans(v):
                s = self.state.setdefault(g, [None, {}])
                s[1][rkey] = o
        for v in writes:
            for g in self._grans(v):
                self.state[g] = [o, {}]
        self.ops[eng].append(o)
        self.allops.append(o)
        return o

    def stream_wait(self, streams, eng="sp"):
        o = self.op(eng, None, tag="swait")
        o.xw = [("dma:" + s, self.dcount["dma:" + s]) for s in streams if ("dma:" + s) in self.dcount]
        return o

    def emit(self, final_streams=()):
        nc = self.nc
        cnt = dict(self.dcount)
        for o in self.allops:
            if o.dma is None:
                if o.need and o.fn is not None:
                    cnt[o.sem] = cnt.get(o.sem, 0) + 1
                    o.cnt = cnt[o.sem]
                elif o.need:
                    raise AssertionError("dependency on a wait-only op")
        self.final_cnt = cnt
        semnames = sorted(cnt.keys())
        import contextlib
        with contextlib.ExitStack() as es:
            sems = {}
            for i, s in enumerate(semnames):
                sems[s] = es.enter_context(nc.semaphore("s_" + s.replace(":", "_")))
            block = es.enter_context(nc.Block())
            engmap = {"pe": block.tensor, "act": block.scalar, "dve": block.vector, "pool": block.gpsimd, "sp": block.sync}

            def run(engname):
                def body(eng):
                    waited = {}
                    for o in self.ops[engname]:
                        for d in o.deps:
                            if waited.get(d.sem, 0) < d.cnt:
                                eng.wait_ge(sems[d.sem], d.cnt)
                                waited[d.sem] = d.cnt
                        for (sm, val) in o.xw:
                            if waited.get(sm, 0) < val:
                                eng.wait_ge(sems[sm], val)
                                waited[sm] = val
                        if o.fn is None:
                            continue
                        r = o.fn(eng)
                        if o.need:
                            if o.dma is not None:
                                rs = r if isinstance(r, (list, tuple)) else [r]
                                assert len(rs) * 16 == o.inc, (o.tag, len(rs), o.inc)
                                for x in rs:
                                    x.then_inc(sems[o.sem], 16)
                            else:
                                r.then_inc(sems[o.sem], 1)
                    if engname == "sp":
                        for s in final_streams:
                            k = "dma:" + s
                            if k in cnt:
                                eng.wait_ge(sems[k], cnt[k])
                return body

            for e in ("sp", "pe", "act", "dve", "pool"):
                if self.ops[e] or e == "sp":
                    engmap[e](run(e))


class _Bitcast:
    def __init__(self, t, dt, n):
        self.t, self.dt, self.n = t, dt, n

    def __getitem__(self, idx):
        ap = self.t[:, :].bitcast(self.dt)
        return ap[idx]


D = 1024
KB = 8
NQ = 8
LW = 1280
LB = 10
DFF = 2816
FB = 22
INW = 6144
O1, O2, O3, O4, O5, O6 = 1024, 1280, 1536, 2816, 4096, 5120
T = 512
EPS = 1e-6
NEG = -30000.0
SCALE = 1.0 / float(np.sqrt(128.0))
VC_CONVW, VC_CONVB, VC_BA, VC_BI, VC_LAM, VC_GPRE, VC_GFFN = 0, 40, 50, 70, 90, 110, 118
NVEC = 126


def _reads(*vs):
    return [v for v in vs if isinstance(v, View)]


def _a(v):
    return v.ap if isinstance(v, View) else v


class K:
    def __init__(self, P):
        self.P = P

    def mm(self, out, lhsT, rhs, start=True, stop=True, nogroup=False):
        if nogroup:
            self.P.op("pe", lambda e: e.matmul(out.ap, lhsT=lhsT.ap, rhs=rhs.ap, start=start, stop=stop, skip_group_check=True),
                      reads=[lhsT, rhs], writes=[out])
        else:
            self.P.op("pe", lambda e: e.matmul(out.ap, lhsT=lhsT.ap, rhs=rhs.ap, start=start, stop=stop),
                      reads=[lhsT, rhs], writes=[out])

    def tr(self, out, in_, ident):
        self.P.op("pe", lambda e: e.transpose(out=out.ap, in_=in_.ap, identity=ident.ap), reads=[in_, ident], writes=[out])

    def act(self, out, in_, func, scale=1.0, bias=0.0, accum=None):
        w = [out] + ([accum] if accum is not None else [])
        kw = {}
        if accum is not None:
            kw["accum_out"] = accum.ap
        self.P.op("act", lambda e: e.activation(out=out.ap, in_=in_.ap, func=func, bias=_a(bias), scale=_a(scale), **kw),
                  reads=[in_] + _reads(scale, bias), writes=w)

    def ts(self, eng, out, in0, s1, s2, op0, op1=None):
        if op1 is None:
            self.P.op(eng, lambda e: e.tensor_scalar(out=out.ap, in0=in0.ap, scalar1=_a(s1), scalar2=None, op0=op0),
                      reads=[in0] + _reads(s1), writes=[out])
        else:
            self.P.op(eng, lambda e: e.tensor_scalar(out=out.ap, in0=in0.ap, scalar1=_a(s1), scalar2=_a(s2), op0=op0, op1=op1),
                      reads=[in0] + _reads(s1, s2), writes=[out])

    def tt(self, eng, out, in0, in1, op):
        self.P.op(eng, lambda e: e.tensor_tensor(out=out.ap, in0=in0.ap, in1=in1.ap, op=op), reads=[in0, in1], writes=[out])

    def stt(self, eng, out, in0, s, in1, op0, op1):
        self.P.op(eng, lambda e: e.scalar_tensor_tensor(out=out.ap, in0=in0.ap, scalar=_a(s), in1=in1.ap, op0=op0, op1=op1),
                  reads=[in0, in1] + _reads(s), writes=[out])

    def copy(self, eng, out, in_):
        if eng == "act":
            self.act(out, in_, AF.Copy)
        else:
            self.P.op(eng, lambda e: e.tensor_copy(out=out.ap, in_=in_.ap), reads=[in_], writes=[out])

    def memset(self, eng, out, val):
        self.P.op(eng, lambda e: e.memset(out.ap, val), writes=[out])

    def recip(self, out, in_):
        self.P.op("dve", lambda e: e.reciprocal(out=out.ap, in_=in_.ap), reads=[in_], writes=[out])

    def scan(self, out, a, u, init):
        self.P.op("dve", lambda e: e.tensor_tensor_scan(out=out.ap, data0=a.ap, data1=u.ap, initial=_a(init), op0=ALU.mult, op1=ALU.add),
                  reads=[a, u] + _reads(init), writes=[out])

    def dma(self, stream, out, in_, reads=(), writes=()):
        self.P.op("sp", lambda e: e.dma_start(out=_a(out), in_=_a(in_)), reads=list(reads) + _reads(in_), writes=list(writes) + _reads(out), dma=stream)

    def dma_group(self, stream, pairs):
        self.P.op("sp", lambda e: [e.dma_start(out=_a(o), in_=_a(i)) for (o, i) in pairs],
                  reads=[i for (o, i) in pairs if isinstance(i, View)],
                  writes=[o for (o, i) in pairs if isinstance(o, View)], dma=stream, ndma=len(pairs))


class WStream:
    def __init__(self, k, name, slots, eng="sp"):
        self.k, self.name, self.slots, self.eng = k, name, slots, eng
        self.plan = []
        self.loaded = 0
        self.cur = 0
        self.limit = None

    def add(self, tag, dram_ap, shape):
        self.plan.append((dram_ap, shape, tag))

    def next(self, tag=None, hold=1):
        i = self.cur
        self.cur += 1
        assert tag is None or self.plan[i][2] == tag, (i, tag, self.plan[i][2])
        nd = len(self.slots)
        lim = len(self.plan) if self.limit is None else self.limit
        while self.loaded < min(lim, i + nd - hold + 1):
            j = self.loaded
            slot = self.slots[j % nd]
            shape = self.plan[j][1]
            n = int(np.prod(shape[1:]))
            v = slot[:, 0:n]
            ap = slot.t[:, 0:n].rearrange("p (a b) -> p a b", a=shape[1])
            self.k.P.op(self.eng, lambda e, ap=ap, src=self.plan[j][0]: e.dma_start(out=ap, in_=src),
                        writes=[v], dma="%s%d" % (self.name, j % nd))
            self.loaded += 1
        return self.slots[i % nd]


DEBUG = False
WENG = "pool"
OPT_B = True


def build_program(seq_lens, smax):
    nc = bass.Bass("TRN2", target_bir_lowering=False)
    P = Prog(nc)
    k = K(P)
    dbg_n = [0]

    def dbg(name, view, shape, dt):
        if not DEBUG:
            return
        d = nc.dram_tensor("dbg_" + name, list(shape), dt, kind="ExternalOutput").ap()
        P.op("sp", lambda e: e.dma_start(out=d, in_=view.ap), reads=[view], dma="dbg")

    def din(name, shape, dt=F32):
        return nc.dram_tensor(name, list(shape), dt, kind="ExternalInput").ap()

    def dscr(name, shape, dt=BF16):
        return nc.dram_tensor(name, list(shape), dt, kind="Internal").ap()

    xs = [din("x%d" % i, [S, D]) for i, S in enumerate(seq_lens)]
    ys = [nc.dram_tensor("y%d" % i, [S, D], F32, kind="ExternalOutput").ap() for i, S in enumerate(seq_lens)]
    w_in_d = din("w_in", [D, INW])
    w_ap_d = din("w_attn_proj", [D, D])
    w_rp_d = din("w_rec_proj", [LW, D])
    w_out_d = din("w_out", [D, D])
    w_fi_d = din("w_ffn_in", [D, 2 * DFF])
    w_fo_d = din("w_ffn_out", [DFF, D])
    lwa_d = din("lru_w_a", [2, LB, 128, 128])
    lwi_d = din("lru_w_i", [2, LB, 128, 128])
    vecs_d = din("vecs", [128, NVEC])
    gpost_d = din("gpost", [2, D])
    sink_d = din("sink", [1, NQ])
    rope_d = din("rope", [32, 2, smax + 256])

    Win_p = dscr("Win_p", [128, INW // 256, KB, 256])
    Wap_p = dscr("Wap_p", [128, D // 256, KB, 256])
    Wrp_p = dscr("Wrp_p", [128, D // 256, LB, 256])
    Wout_p = dscr("Wout_p", [128, 2, KB, 512])
    Wfi_p = dscr("Wfi_p", [128, 2 * DFF // 256, KB, 256])
    Wfo_p = dscr("Wfo_p", [128, 2, FB, 512])
    Wg_b = dscr("Wg_b", [128, 2, 2 * LB, 128])
    recT_d = [dscr("recT%d" % i, [128, LB, S]) for i, S in enumerate(seq_lens)]

    Smax = max(seq_lens)
    vecs = P.sb("vecs", [128, NVEC], F32)
    der = P.sb("der", [128, 64], F32)
    der2 = P.sb("der2", [128, 64], F32)
    gpost = P.sb("gpost", [128, 2, D], F32)
    esink = P.sb("esink", [128, NQ], F32)
    ident = P.sb("ident", [128, 128], BF16)
    ones = P.sb("ones", [128, 128], BF16)
    mlo = P.sb("mlo", [128, 128], BF16)
    mhi = P.sb("mhi", [128, 128], BF16)
    prot = P.sb("prot", [128, 32], BF16)
    cst = P.sb("cst", [128, 8], F32)
    stA = P.sb("stA", [128, 64], F32)
    stO = [P.sb("stO%d" % i, [128, 64], F32) for i in range(2)]
    stF = [P.sb("stF%d" % i, [128, 64], F32) for i in range(2)]
    stY = [P.sb("stY%d" % i, [128, 64], F32) for i in range(2)]
    hT = P.sb("hT", [128, KB, Smax], BF16)
    ARENA0 = P.sb_off

    PSB = [P.ps("ps%d" % b, b, F32) for b in range(8)]
    PSH = [P.ps("psh%d" % b, b, BF16) for b in range(8)]

    def affsel(v, pattern, cmp, fill, base, cm):
        P.op("pool", lambda e: e.affine_select(out=v.ap, in_=v.ap, pattern=pattern, compare_op=cmp, fill=fill, base=base, channel_multiplier=cm),
             reads=[v], writes=[v])

    scr = [P.sb("cscr%d" % i, [128, 128], F32, off=ARENA0 + 512 * i) for i in range(4)]
    k.memset("pool", scr[0][:, :], 1.0)
    affsel(scr[0][:, :], [[-1, 128]], ALU.is_equal, 0.0, 0, 1)
    k.copy("pool", ident[:, :], scr[0][:, :])
    k.memset("pool", ones[:, :], 1.0)
    k.memset("pool", scr[1][:, :], 0.0)
    affsel(scr[1][:, :], [[1, 128]], ALU.is_ge, NEG, 0, -1)
    k.copy("pool", mlo[:, :], scr[1][:, :])
    k.memset("pool", scr[2][:, :], 0.0)
    affsel(scr[2][:, :], [[-1, 128]], ALU.is_ge, NEG, 0, 1)
    k.copy("pool", mhi[:, :], scr[2][:, :])
    k.memset("pool", scr[3][:, 0:32], 0.0)
    affsel(scr[3][:, 0:16], [[-1, 16]], ALU.not_equal, -1.0, -16, 1)
    affsel(scr[3][:, 16:32], [[-1, 16]], ALU.not_equal, 1.0, 0, 1)
    k.copy("pool", prot[:, :], scr[3][:, 0:32])
    k.memset("pool", cst[:, 0:1], 0.25)
    k.memset("pool", cst[:, 1:2], EPS)
    k.memset("pool", cst[:, 2:3], 1.0)

    k.dma("m0", vecs[:, :], vecs_d)
    k.dma("m1", gpost[:, 0, :], gpost_d[0:1, :].partition_broadcast(128))
    k.dma("m2", gpost[:, 1, :], gpost_d[1:2, :].partition_broadcast(128))
    k.dma("m3", esink[:, :], sink_d[0:1, :].partition_broadcast(128))
    k.act(esink[:, :], esink[:, :], AF.Exp)
    k.ts("dve", der[:, 0:20], vecs[:, VC_BA:VC_BA + 20], 0.5, None, ALU.mult)
    k.ts("dve", der[:, 20:40], vecs[:, VC_BI:VC_BI + 20], 0.5, None, ALU.mult)
    k.act(der[:, 40:60], vecs[:, VC_LAM:VC_LAM + 20], AF.Exp, scale=-1.0)
    k.act(der[:, 40:60], der[:, 40:60], AF.Ln, bias=cst[:, 2:3])
    k.ts("dve", der2[:, 0:20], der[:, 40:60], -4.0, None, ALU.mult)
    k.ts("dve", der2[:, 20:40], der[:, 40:60], -8.0, None, ALU.mult)

    CW = 2048
    stg_f = [P.sb("stgf%d" % i, [128, CW], F32, off=ARENA0 + 4096 + i * (CW * 6)) for i in range(3)]
    stg_b = [P.sb("stgb%d" % i, [128, CW], BF16, off=ARENA0 + 4096 + i * (CW * 6) + CW * 4) for i in range(3)]
    cvi = [0]
    engs = ["act", "dve"]

    cvjobs = []

    def convert(src_ap, dst_ap, width, scale_v, ld3=None, st3=None):
        cvjobs.append((src_ap, dst_ap, width, scale_v, ld3, st3))

    def cv_load(i):
        src_ap, dst_ap, width, scale_v, ld3, st3 = cvjobs[i]
        sf = stg_f[i % 3]
        o_ap = sf[:, 0:width].ap if ld3 is None else sf.t[:, 0:width].rearrange("p (a b) -> p a b", a=ld3)
        P.op("sp", lambda e: e.dma_start(out=o_ap, in_=src_ap), writes=[sf[:, 0:width]], dma="cvl%d" % (i % 3))

    def cv_store(i):
        src_ap, dst_ap, width, scale_v, ld3, st3 = cvjobs[i]
        sf, sb_ = stg_f[i % 3], stg_b[i % 3]
        i_ap = sb_[:, 0:width].ap if st3 is None else sb_.t[:, 0:width].rearrange("p (a b) -> p a b", a=st3)
        eng = engs[i % 2]
        if scale_v is None:
            k.copy(eng, sb_[:, 0:width], sf[:, 0:width])
        elif eng == "act":
            k.act(sb_[:, 0:width], sf[:, 0:width], AF.Copy, scale=scale_v)
        else:
            k.ts(eng, sb_[:, 0:width], sf[:, 0:width], scale_v, None, ALU.mult)
        P.op("sp", lambda e: e.dma_start(out=dst_ap, in_=i_ap), reads=[sb_[:, 0:width]], dma="cvs%d" % (i % 3))

    def cv_run():
        n = len(cvjobs)
        for i in range(n + 2):
            if i < n:
                cv_load(i)
            if i >= 2:
                cv_store(i - 2)

    def convert_cols(src, dst, nkb, ncols, scale_col):
        for kb in range(nkb):
            for c0 in range(0, ncols, CW):
                w = min(CW, ncols - c0)
                sv = vecs[:, scale_col + kb:scale_col + kb + 1] if scale_col is not None else None
                convert(src[kb * 128:(kb + 1) * 128, c0:c0 + w], dst[:, c0 // 256:(c0 + w) // 256, kb, :], w, sv, st3=w // 256)

    def convert_rows(src, dst, nkb):
        for kb in range(nkb):
            convert(src[kb * 128:(kb + 1) * 128, :], dst[:, :, kb, :], D, None, st3=2)

    for gi, src in enumerate((lwa_d, lwi_d)):
        for dr in range(2):
            convert(src[dr].rearrange("n c d -> c n d"), Wg_b[:, gi, dr * LB:(dr + 1) * LB, :], LB * 128, None, ld3=LB, st3=LB)
    convert_cols(w_in_d, Win_p, KB, INW, VC_GPRE)
    convert_cols(w_ap_d, Wap_p, KB, D, None)
    convert_cols(w_rp_d, Wrp_p, LB, D, None)
    convert_rows(w_out_d, Wout_p, KB)
    convert_cols(w_fi_d, Wfi_p, KB, 2 * DFF, VC_GFFN)
    convert_rows(w_fo_d, Wfo_p, FB)
    cv_run()
    P.stream_wait(["cvs0", "cvs1", "cvs2"])

    WSLOT = 5632
    NWS = 8
    wslots = [P.sb("wslot%d" % i, [128, WSLOT // 2], BF16, off=ARENA0 + i * WSLOT) for i in range(NWS)]
    A2 = ARENA0 + NWS * WSLOT
    o = [A2]

    def take(n):
        r = o[0]
        o[0] = (r + n + GRAN - 1) // GRAN * GRAN
        assert o[0] <= P.sb_top, ("arena overflow", o[0], P.sb_top)
        return r

    SF = Smax * 4
    o[0] = ARENA0
    R_off = take(SF + 16)
    XC_off = take(SF)
    XCB_off = take(Smax * 2)
    B_off = [take(SF) for _ in range(4)]
    RB_off = XCB_off
    W1_off = [take(4096) for _ in range(3)]
    GW_off = take(2 * 2 * LB * 128 * 2)
    p1_end = o[0]
    o[0] = R_off
    X1_off = [take(4096) for _ in range(8)]
    XN_off = [take(2048) for _ in range(2)]
    JUNK1_off = take(2048)
    p1_end = max(p1_end, o[0])
    o[0] = A2
    XT_off = take(4 * 4096)
    QT_off = take(NQ * T * 2)
    AT_off = take(NQ * T * 2)
    REC_off = take(LB * T * 2)
    TMP_off = [take(2048) for _ in range(8)]
    JUNK_off = REC_off
    XN2_off = [REC_off + 2048, REC_off + 4096]
    ROPE_off = take(32 * 0 + 2 * 768 * 4)
    G_off = take(FB * T * 2)
    p2_end = o[0]

    xt = P.sb("xt", [128, 4, D], F32, off=XT_off)
    qT = P.sb("qT", [128, NQ, T], BF16, off=QT_off)
    mgT = qT
    atT = P.sb("atT", [128, NQ, T], BF16, off=AT_off)
    h2T = atT
    recc = P.sb("recc", [128, LB, T], BF16, off=REC_off)
    tmp = [P.sb("tmp%d" % i, [128, T], F32, off=TMP_off[i]) for i in range(8)]
    junk = P.sb("junk", [128, D], BF16, off=JUNK_off)
    xn2 = [P.sb("xn2_%d" % i, [128, D], BF16, off=XN2_off[i]) for i in range(2)]
    kT = P.sb("kT", [128, 2, 768], BF16, off=G_off)
    Vt = P.sb("Vt", [128, 6, 256], BF16, off=G_off + 3072)
    PT = [P.sb("PT%d" % i, [128, 384], BF16, off=G_off + 6144 + i * 1024) for i in range(4)]
    ropeC = P.sb("ropeC", [32, 2, 768], F32, off=ROPE_off)
    acT = P.sb("acT", [128, FB, T], BF16, off=G_off)
    w1slots = [P.sb("w1slot%d" % i, [128, KB * 256], BF16, off=W1_off[i]) for i in range(3)]
    GW = P.sb("GW", [128, 2, 2 * LB, 128], BF16, off=GW_off)
    x1 = [P.sb("x1_%d" % i, [128, D], F32, off=X1_off[i]) for i in range(8)]
    xn = [P.sb("xn_%d" % i, [128, D], BF16, off=XN_off[i]) for i in range(2)]
    junk1 = P.sb("junk1", [128, D], BF16, off=JUNK1_off)

    FO_PIECES = [(0, 5), (5, 10), (10, 15), (15, 20), (20, 22)]
    ws2 = WStream(k, "w", wslots, eng=WENG)
    ws2_lim = []
    for si, S in enumerate(seq_lens):
        ws2_lim.append(0)
        for c in range(S // T):
            ws2.add("k", Win_p[:, O1 // 256], [128, KB, 256])
            ws2.add("v", Win_p[:, O2 // 256], [128, KB, 256])
            for hp in range(4):
                ws2.add("q%d" % hp, Win_p[:, hp], [128, KB, 256])
            for qt in range(4):
                ws2.add("ga%d" % qt, Win_p[:, O5 // 256 + qt], [128, KB, 256])
                ws2.add("pa%d" % qt, Wap_p[:, qt], [128, KB, 256])
                ws2.add("gr%d" % qt, Win_p[:, O6 // 256 + qt], [128, KB, 256])
                ws2.add("pr%d" % qt, Wrp_p[:, qt], [128, LB, 256])
            for half in range(2):
                for kg in range(2):
                    ws2.add("wo%d%d" % (half, kg), Wout_p[:, half, kg * 4:(kg + 1) * 4, :], [128, 4, 512])
            for g2 in range(FB // 2):
                ws2.add("fg%d" % g2, Wfi_p[:, g2], [128, KB, 256])
                ws2.add("fu%d" % g2, Wfi_p[:, DFF // 256 + g2], [128, KB, 256])
            for half in range(2):
                for (j0, j1) in FO_PIECES:
                    ws2.add("fo%d_%d" % (half, j0), Wfo_p[:, half, j0:j1, :], [128, j1 - j0, 512])
        ws2_lim[si] = len(ws2.plan)

    def rstd_from(st, src_cols, n, dst0, add_col=None):
        if add_col is None:
            k.ts("dve", st[:, 40:40 + n], st[:, src_cols:src_cols + n], 1.0 / D, cst[:, 1:2], ALU.mult, ALU.add)
        else:
            k.ts("dve", st[:, 40:40 + n], st[:, src_cols:src_cols + n], st[:, add_col:add_col + 1], 1.0 / D, ALU.add, ALU.mult)
        k.act(st[:, 41 + n:41 + 2 * n], st[:, 40:40 + n], AF.Ln, bias=(0.0 if add_col is None else cst[:, 1:2]))
        k.act(st[:, dst0:dst0 + n], st[:, 41 + n:41 + 2 * n], AF.Exp, scale=-0.5)

    def rope(raw, kvh, c_lo, n, bank_raw, bank_sw, tA, tB, tab_lo):
        k.mm(bank_sw[0:32, 0:n], prot[:, 0:32], raw[:, kvh, c_lo:c_lo + n])
        k.tt("dve", tA[0:32, 0:n], bank_sw[0:32, 0:n], ropeC[0:32, 1, tab_lo:tab_lo + n], ALU.mult)
        k.tt("dve", tB[0:32, 0:n], bank_raw[0:32, 0:n], ropeC[0:32, 0, tab_lo:tab_lo + n], ALU.mult)
        k.tt("dve", raw[0:32, kvh, c_lo:c_lo + n], tA[0:32, 0:n], tB[0:32, 0:n], ALU.add)

    for si, S in enumerate(seq_lens):
        NT = S // 128
        NCH = S // T
        xd = xs[si]
        P.stage = "p1a"
        for g in range(NT // 4):
            for j in range(4):
                t = g * 4 + j
                k.dma("x1l%d" % (t % 8), x1[t % 8][:, :], xd[t * 128:(t + 1) * 128, :])
            for j in range(4):
                t = g * 4 + j
                k.act(junk1[:, :], x1[t % 8][:, :], AF.Square, accum=stA[:, j:j + 1])
            rstd_from(stA, 0, 4, 8)
            for j in range(4):
                t = g * 4 + j
                xnb = xn[t % 2]
                k.ts("dve", xnb[:, :], x1[t % 8][:, :], stA[:, 8 + j:9 + j], None, ALU.mult)
                bank = t % 2
                for kb in range(KB):
                    k.tr(PSH[bank][:, kb * 128:(kb + 1) * 128], xnb[:, kb * 128:(kb + 1) * 128], ident[:, :])
                srcv = PSH[bank][:, :]
                dstv = hT[:, :, t * 128:(t + 1) * 128]
                srcap = PSH[bank].t[:, :].rearrange("p (a b) -> p a b", a=KB)
                if t % 2:
                    P.op("dve", lambda e, d=dstv, s=srcap: e.tensor_copy(out=d.ap, in_=s), reads=[srcv], writes=[dstv])
                else:
                    P.op("act", lambda e, d=dstv, s=srcap: e.activation(out=d.ap, in_=s, func=AF.Copy), reads=[srcv], writes=[dstv])

        if si == 0:
            dbg("hT", hT[:, :, 0:S], [128, KB, S], BF16)
        P.stage = "p1b"
        Rb = P.sb("R_%d" % si, [128, S + 4], F32, off=R_off)
        XC = P.sb("XC_%d" % si, [128, S], F32, off=XC_off)
        XCB = P.sb("XCB_%d" % si, [128, S], BF16, off=XCB_off)
        B = [P.sb("B%d_%d" % (i, si), [128, S], F32, off=B_off[i]) for i in range(4)]
        RB = P.sb("RB_%d" % si, [128, S], BF16, off=RB_off)
        ws1 = WStream(k, "v", w1slots)
        k.dma("m0", GW[:, :, :, :], Wg_b)
        for bp in range(LB // 2):
            ws1.add("rx", Win_p[:, O3 // 256 + bp], [128, KB, 256])
            ws1.add("gz", Win_p[:, O4 // 256 + bp], [128, KB, 256])
        for blk in range(LB):
            if blk % 2 == 0:
                wrx = ws1.next("rx", 2)
            k.memset("pool", Rb[:, 0:2], 0.0)
            k.memset("pool", Rb[:, S + 2:S + 4], 0.0)
            for c in range(NCH):
                bank = PSB[c % 2]
                for kb in range(KB):
                    k.mm(bank[:, :], wrx[:, kb * 256 + (blk % 2) * 128:kb * 256 + (blk % 2 + 1) * 128], hT[:, kb, c * T:(c + 1) * T], start=(kb == 0), stop=(kb == KB - 1))
                k.act(Rb[:, 2 + c * T:2 + (c + 1) * T], bank[:, :], AF.Copy)

            def cw(tp, blk=blk):
                return vecs[:, VC_CONVW + blk * 4 + tp:VC_CONVW + blk * 4 + tp + 1]

            halves = [(0, S // 2), (S // 2, S)] if S >= 1024 else [(0, S)]
            for (a0, b0) in halves:
                k.ts("dve", XC[:, a0:b0], Rb[:, a0 + 2:b0 + 2], cw(2), vecs[:, VC_CONVB + blk:VC_CONVB + blk + 1], ALU.mult, ALU.add)
                for tp in (0, 1, 3):
                    k.stt("dve", XC[:, a0:b0], Rb[:, a0 + tp:b0 + tp], cw(tp), XC[:, a0:b0], ALU.mult, ALU.add)
                k.copy("dve", XCB[:, a0:b0], XC[:, a0:b0])
            if si == 0 and blk == 0:
                dbg("xc0", XC[:, :], [128, S], F32)
            for dr in range(2):
                THA, THI, AA = (B[0], B[1], B[2]) if dr == 0 else (Rb, B[3], B[2])
                col = dr * LB + blk
                for c in range(NCH):
                    ba, bi = PSB[2 + (c % 2) * 2], PSB[3 + (c % 2) * 2]
                    k.mm(ba[:, :], GW[:, 0, col, :], XCB[:, c * T:(c + 1) * T])
                    k.mm(bi[:, :], GW[:, 1, col, :], XCB[:, c * T:(c + 1) * T])
                    k.act(THA[:, c * T:(c + 1) * T], ba[:, :], AF.Tanh, scale=0.5, bias=der[:, col:col + 1])
                    k.act(THI[:, c * T:(c + 1) * T], bi[:, :], AF.Tanh, scale=0.5, bias=der[:, 20 + col:21 + col])
                if dr == 1:
                    if blk % 2 == 0:
                        wgt = ws1.next("gz", 2)
                    for c in range(NCH):
                        bank = PSB[6 + c % 2]
                        for kb in range(KB):
                            k.mm(bank[:, :], wgt[:, kb * 256 + (blk % 2) * 128:kb * 256 + (blk % 2 + 1) * 128], hT[:, kb, c * T:(c + 1) * T], start=(kb == 0), stop=(kb == KB - 1))
                        k.act(B[1][:, c * T:(c + 1) * T], bank[:, :], AF.Gelu_apprx_tanh)
                for (a0, b0) in halves:
                    k.act(AA[:, a0:b0], THA[:, a0:b0], AF.Exp, scale=der2[:, col:col + 1], bias=der2[:, col:col + 1])
                    k.tt("dve", THA[:, a0:b0], AA[:, a0:b0], AA[:, a0:b0], ALU.mult)
                    k.stt("dve", THI[:, a0:b0], THI[:, a0:b0], 1.0, XC[:, a0:b0], ALU.add, ALU.mult)
                for (a0, b0) in halves:
                    k.act(THA[:, a0:b0], THA[:, a0:b0], AF.Sqrt, scale=-0.25, bias=cst[:, 0:1])
                    k.tt("dve", THI[:, a0:b0], THI[:, a0:b0], THA[:, a0:b0], ALU.mult)
                if dr == 0:
                    for hi_, (a0, b0) in enumerate(halves):
                        init = 0.0 if hi_ == 0 else THA[:, a0 - 1:a0]
                        k.scan(THA[:, a0:b0], AA[:, a0:b0], THI[:, a0:b0], init)
                else:
                    for hi_, (a0, b0) in enumerate(reversed(halves)):
                        init = 0.0 if hi_ == 0 else THI[:, b0:b0 + 1]
                        rs = slice(b0 - 1, (a0 - 1 if a0 > 0 else None), -1)
                        k.scan(THI[:, rs], AA[:, rs], THI[:, rs], init)
            if si == 0 and blk == 0:
                dbg("hf0", B[0][:, :], [128, S], F32)
                dbg("hb0", B[3][:, :], [128, S], F32)
            for (a0, b0) in halves:
                k.tt("dve", B[0][:, a0:b0], B[0][:, a0:b0], B[3][:, a0:b0], ALU.add)
                k.tt("dve", RB[:, a0:b0], B[0][:, a0:b0], B[1][:, a0:b0], ALU.mult)
            k.dma("recst", recT_d[si][:, blk, :], RB[:, :])
            if si == 0 and blk == 0:
                dbg("rec0", RB[:, :], [128, S], BF16)
        P.stream_wait(["recst"])

        ws2.limit = ws2_lim[si]
        for c in range(NCH):
            c0 = c * T
            lo, hi = max(0, c0 - 128), min(S, c0 + T + 128)
            W = hi - lo
            tl_lo, tl_hi = lo // 128, hi // 128
            qt0 = c0 // 128
            if c == 0:
                k.dma("rope", ropeC[0:32, :, 0:W], rope_d[:, :, lo + 128:hi + 128])
            k.dma("recl", recc[:, :, :], recT_d[si][:, :, c0:c0 + T])
            k.dma_group("xt", [(xt[:, j, :], xd[c0 + j * 128:c0 + (j + 1) * 128, :]) for j in range(4)])
            P.stage = "K"
            wk = ws2.next("k")
            pieces = [(a, min(a + 512, hi)) for a in range(lo, hi, 512)]
            pend = None
            for kv in range(2):
                for pi, (a, b) in enumerate(pieces):
                    n = b - a
                    bank, bsw = PSB[(kv * 2 + pi) % 4], PSB[4 + (kv * 2 + pi) % 2]
                    for kb in range(KB):
                        k.mm(bank[:, 0:n], wk[:, kb * 256 + kv * 128:kb * 256 + (kv + 1) * 128], hT[:, kb, a:b], start=(kb == 0), stop=(kb == KB - 1))
                    k.act(kT[:, kv, a - lo:b - lo], bank[:, 0:n], AF.Copy)
                    if pend is not None:
                        rope(*pend)
                    pend = (kT, kv, a - lo, n, bank, bsw, tmp[((kv * 2 + pi) * 2) % 8], tmp[((kv * 2 + pi) * 2 + 1) % 8], a - lo)
            P.stage = "V"
            wv = ws2.next("v")
            for jt in range(tl_lo, tl_hi):
                bank = PSB[6 + jt % 2]
                for kb in range(KB):
                    k.mm(bank[:, 0:256], hT[:, kb, jt * 128:(jt + 1) * 128], wv[:, kb * 256:(kb + 1) * 256], start=(kb == 0), stop=(kb == KB - 1))
                k.copy("act" if jt % 2 else "dve", Vt[:, jt - tl_lo, :], bank[:, 0:256])
                if pend is not None:
                    rope(*pend)
                    pend = None
            P.stage = "Q"
            for h in range(NQ):
                if h % 2 == 0:
                    wv_ = ws2.next("q%d" % (h // 2))
                bank, bsw = PSB[h % 4], PSB[4 + h % 2]
                for kb in range(KB):
                    k.mm(bank[:, :], wv_[:, kb * 256 + (h % 2) * 128:kb * 256 + (h % 2 + 1) * 128], hT[:, kb, c0:c0 + T], start=(kb == 0), stop=(kb == KB - 1))
                k.act(qT[:, h, :], bank[:, :], AF.Copy)
                if pend is not None:
                    rope(*pend)
                pend = (qT, h, 0, T, bank, bsw, tmp[(h * 2) % 8], tmp[(h * 2 + 1) % 8], c0 - lo)
            rope(*pend)
            pend = None
            if c + 1 < NCH:
                nlo, nhi = max(0, c0 + T - 128), min(S, c0 + 2 * T + 128)
                k.dma("rope", ropeC[0:32, :, 0:nhi - nlo], rope_d[:, :, nlo + 128:nhi + 128])
            if si == 0 and c == 0:
                dbg("qT", qT[:, :, :], [128, NQ, T], BF16)
                dbg("kT", kT[:, :, 0:W], [128, 2, W], BF16)
                dbg("Vt", Vt[:, 0:tl_hi - tl_lo, :], [128, tl_hi - tl_lo, 256], BF16)
            P.stage = "att"
            items = [(h, jt) for h in range(NQ) for jt in range(tl_lo, tl_hi)]

            def att_front(n_):
                h, jt = items[n_]
                kv = h // 4
                qs = [i for i in (jt - 1, jt, jt + 1) if qt0 <= i < qt0 + 4]
                qa, qb = (qs[0] - qt0) * 128, (qs[-1] - qt0 + 1) * 128
                n = qb - qa
                sb_ = PSB[n_ % 4]
                pt = PT[n_ % 4]
                k.mm(sb_[:, 0:n], kT[:, kv, (jt - tl_lo) * 128:(jt - tl_lo + 1) * 128], qT[:, h, qa:qb], start=True, stop=True, nogroup=True)
                for i in qs:
                    off = (i - qt0) * 128 - qa
                    if i == jt - 1:
                        k.mm(sb_[:, off:off + 128], ident[:, :], mlo[:, :], start=False, stop=True, nogroup=True)
                    elif i == jt + 1:
                        k.mm(sb_[:, off:off + 128], ident[:, :], mhi[:, :], start=False, stop=True, nogroup=True)
                k.act(pt[:, 0:n], sb_[:, 0:n], AF.Exp, scale=SCALE)
                return (h, jt, kv, qa, qb, n, pt)

            def att_back(info):
                h, jt, kv, qa, qb, n, pt = info
                bO, bD = PSB[4 + (h % 2) * 2], PSB[5 + (h % 2) * 2]
                first = (jt == tl_lo)
                last = (jt == tl_hi - 1)
                k.mm(bO[:, qa:qb], Vt[:, jt - tl_lo, kv * 128:(kv + 1) * 128], pt[:, 0:n], start=first, stop=last, nogroup=True)
                k.mm(bD[:, qa:qb], ones[:, :], pt[:, 0:n], start=first, stop=last, nogroup=True)
                if last:
                    td = tmp[h % 2]
                    k.act(td[:, :], bD[:, :], AF.Ln, bias=esink[:, h:h + 1])
                    k.act(td[:, :], td[:, :], AF.Exp, scale=-1.0)
                    k.tt("dve", atT[:, h, :], bO[:, :], td[:, :], ALU.mult)

            prev = None
            for n_ in range(len(items)):
                info = att_front(n_)
                if prev is not None:
                    att_back(prev)
                prev = info
            att_back(prev)
            if si == 0 and c == 0:
                dbg("atT", atT[:, :, :], [128, NQ, T], BF16)
            P.stage = "merge"
            for qt in range(4):
                tg = [tmp[(qt % 2) * 4 + m] for m in range(2)]
                tr_ = [tmp[(qt % 2) * 4 + 2 + m] for m in range(2)]
                bks = [PSB[(qt % 2) * 4 + i] for i in range(4)]
                wga = ws2.next("ga%d" % qt)
                for m in range(2):
                    for kb in range(KB):
                        k.mm(bks[m][:, :], wga[:, kb * 256 + m * 128:kb * 256 + (m + 1) * 128], hT[:, kb, c0:c0 + T], start=(kb == 0), stop=(kb == KB - 1))
                    k.act(tg[m][:, :], bks[m][:, :], AF.Sigmoid)
                wpa = ws2.next("pa%d" % qt)
                for m in range(2):
                    for kb in range(NQ):
                        k.mm(bks[2 + m][:, :], wpa[:, kb * 256 + m * 128:kb * 256 + (m + 1) * 128], atT[:, kb, :], start=(kb == 0), stop=(kb == NQ - 1))
                    k.tt("dve", tg[m][:, :], bks[2 + m][:, :], tg[m][:, :], ALU.mult)
                wgr = ws2.next("gr%d" % qt)
                for m in range(2):
                    for kb in range(KB):
                        k.mm(bks[m][:, :], wgr[:, kb * 256 + m * 128:kb * 256 + (m + 1) * 128], hT[:, kb, c0:c0 + T], start=(kb == 0), stop=(kb == KB - 1))
                    k.act(tr_[m][:, :], bks[m][:, :], AF.Sigmoid)
                wpr = ws2.next("pr%d" % qt)
                for m in range(2):
                    for kb in range(LB):
                        k.mm(bks[2 + m][:, :], wpr[:, kb * 256 + m * 128:kb * 256 + (m + 1) * 128], recc[:, kb, :], start=(kb == 0), stop=(kb == LB - 1))
                    k.tt("dve", tr_[m][:, :], bks[2 + m][:, :], tr_[m][:, :], ALU.mult)
                    k.tt("dve", mgT[:, qt * 2 + m, :], tg[m][:, :], tr_[m][:, :], ALU.add)
            wo = [[ws2.next("wo00", 1), ws2.next("wo01", 2)], [ws2.next("wo10", 3), ws2.next("wo11", 4)]]

            def wout_tile(i):
                P.stage = "wout"
                st = stO[i % 2]
                for half in range(2):
                    bank = PSB[(i % 2) * 2 + half]
                    for kb in range(KB):
                        k.mm(bank[:, :], mgT[:, kb, i * 128:(i + 1) * 128], wo[half][kb // 4][:, (kb % 4) * 512:(kb % 4 + 1) * 512], start=(kb == 0), stop=(kb == KB - 1))
                    k.act(junk[:, half * 512:(half + 1) * 512], bank[:, :], AF.Square, accum=st[:, half:half + 1])
                rstd_from(st, 0, 1, 4, add_col=1)
                for half in range(2):
                    bank = PSB[(i % 2) * 2 + half]
                    tb = tmp[(i % 2) * 2 + half]
                    k.stt("dve", tb[:, :], bank[:, :], st[:, 4:5], gpost[:, 0, half * 512:(half + 1) * 512], ALU.mult, ALU.mult)
                    k.tt("dve", xt[:, i, half * 512:(half + 1) * 512], xt[:, i, half * 512:(half + 1) * 512], tb[:, :], ALU.add)
                sf = stF[i % 2]
                k.act(junk[:, :], xt[:, i, :], AF.Square, accum=sf[:, 0:1])
                rstd_from(sf, 0, 1, 8)
                k.ts("dve", xn2[i % 2][:, :], xt[:, i, :], sf[:, 8:9], None, ALU.mult)

            def ffn_tr_tile(i):
                P.stage = "ffnT"
                xnb = xn2[i % 2]
                bank = 4 + i % 2
                for kb in range(KB):
                    k.tr(PSH[bank][:, kb * 128:(kb + 1) * 128], xnb[:, kb * 128:(kb + 1) * 128], ident[:, :])
                srcv = PSH[bank][:, :]
                dstv = h2T[:, :, i * 128:(i + 1) * 128]
                srcap = PSH[bank].t[:, :].rearrange("p (a b) -> p a b", a=KB)
                if i % 2:
                    P.op("dve", lambda e, d=dstv, s=srcap: e.tensor_copy(out=d.ap, in_=s), reads=[srcv], writes=[dstv])
                else:
                    P.op("act", lambda e, d=dstv, s=srcap: e.activation(out=d.ap, in_=s, func=AF.Copy), reads=[srcv], writes=[dstv])

            if si == 0 and c == 0:
                dbg("mgT", mgT[:, :, :], [128, NQ, T], BF16)
            if OPT_B:
                for i in range(4):
                    wout_tile(i)
                    if i >= 1:
                        ffn_tr_tile(i - 1)
                ffn_tr_tile(3)
            else:
                for i in range(4):
                    wout_tile(i)
                    ffn_tr_tile(i)
            if si == 0 and c == 0:
                dbg("x1", xt[:, :, :], [128, 4, D], F32)
            P.stage = "ffnin"
            jj = 0
            for g2 in range(FB // 2):
                wg_ = ws2.next("fg%d" % g2, 1)
                wu_ = ws2.next("fu%d" % g2, 2)
                for m in range(2):
                    j = g2 * 2 + m
                    bG, bU = PSB[(jj % 2) * 2], PSB[(jj % 2) * 2 + 1]
                    tb = tmp[4 + jj % 4]
                    jj += 1
                    for kb in range(KB):
                        k.mm(bG[:, :], wg_[:, kb * 256 + m * 128:kb * 256 + (m + 1) * 128], h2T[:, kb, :], start=(kb == 0), stop=(kb == KB - 1))
                    for kb in range(KB):
                        k.mm(bU[:, :], wu_[:, kb * 256 + m * 128:kb * 256 + (m + 1) * 128], h2T[:, kb, :], start=(kb == 0), stop=(kb == KB - 1))
                    k.act(tb[:, :], bG[:, :], AF.Silu)
                    k.tt("dve", acT[:, j, :], bU[:, :], tb[:, :], ALU.mult)
            if si == 0 and c == 0:
                dbg("acT", acT[:, :, :], [128, FB, T], BF16)
            P.stage = "ffnout"
            for half in range(2):
                for (j0, j1) in FO_PIECES:
                    wf = ws2.next("fo%d_%d" % (half, j0))
                    for i in range(4):
                        bank = PSB[half * 4 + i]
                        for j in range(j0, j1):
                            k.mm(bank[:, :], acT[:, j, i * 128:(i + 1) * 128], wf[:, (j - j0) * 512:(j - j0 + 1) * 512], start=(j == 0), stop=(j == FB - 1))
            for i in range(4):
                st = stY[i % 2]
                for half in range(2):
                    k.act(junk[:, 0:512], PSB[half * 4 + i][:, :], AF.Square, accum=st[:, half:half + 1])
                rstd_from(st, 0, 1, 4, add_col=1)
                for half in range(2):
                    tb = tmp[(i % 2) * 2 + half]
                    k.stt("dve", tb[:, :], PSB[half * 4 + i][:, :], st[:, 4:5], gpost[:, 1, half * 512:(half + 1) * 512], ALU.mult, ALU.mult)
                    k.tt("dve", xt[:, i, half * 512:(half + 1) * 512], xt[:, i, half * 512:(half + 1) * 512], tb[:, :], ALU.add)
                k.dma("yst%d" % i, ys[si][c0 + i * 128:c0 + (i + 1) * 128, :], xt[:, i, :])
    P.emit(final_streams=["yst0", "yst1", "yst2", "yst3", "dbg"])
    import os
    if os.environ.get("MK_DUMP_STAGES"):
        with open(os.environ["MK_DUMP_STAGES"], "w") as f:
            for e in ("pe", "act", "dve"):
                f.write(e + ":" + ",".join(o.stage for o in P.ops[e] if o.fn is not None) + "\n")
    return nc


def host_prep(inputs, smax):
    f = np.float32
    g = lambda n: np.ascontiguousarray(np.asarray(inputs[n], dtype=f)[0])
    conv_w, conv_b = g("conv_w"), g("conv_b")
    vec = np.zeros((128, NVEC), f)
    vec[:, VC_CONVW:VC_CONVW + 40] = conv_w.reshape(4, LB, 128).transpose(2, 1, 0).reshape(128, 40)
    vec[:, VC_CONVB:VC_CONVB + 10] = conv_b.reshape(LB, 128).T
    vec[:, VC_BA:VC_BA + 20] = g("lru_b_a").reshape(2 * LB, 128).T
    vec[:, VC_BI:VC_BI + 20] = g("lru_b_i").reshape(2 * LB, 128).T
    vec[:, VC_LAM:VC_LAM + 20] = g("lru_lambda").reshape(2 * LB, 128).T
    vec[:, VC_GPRE:VC_GPRE + 8] = g("norm_mix_pre").reshape(KB, 128).T
    vec[:, VC_GFFN:VC_GFFN + 8] = g("norm_ffn_pre").reshape(KB, 128).T
    gp = np.stack([g("norm_mix_post"), g("norm_ffn_post")], 0)
    sink = g("attn_sink").reshape(1, NQ)
    half = 16
    inv = (np.float32(500000.0) ** (-np.arange(half, dtype=f) / np.float32(half))).astype(f)
    pos = np.arange(-128, smax + 128).astype(f)
    ang = (pos[None, :] * inv[:, None]).astype(f)
    rope = np.zeros((32, 2, smax + 256), f)
    rope[0:16, 0] = np.cos(ang.astype(np.float64))
    rope[16:32, 0] = np.cos(ang.astype(np.float64))
    rope[0:16, 1] = np.sin(ang.astype(np.float64))
    rope[16:32, 1] = np.sin(ang.astype(np.float64))
    common = {
        "w_in": g("w_in"), "w_attn_proj": g("w_attn_proj"), "w_rec_proj": g("w_rec_proj"), "w_out": g("w_out"),
        "w_ffn_in": g("w_ffn_in"), "w_ffn_out": g("w_ffn_out"), "lru_w_a": g("lru_w_a"), "lru_w_i": g("lru_w_i"),
        "vecs": vec, "gpost": gp, "sink": sink, "rope": rope,
    }
    return common


_CACHE = {}


def run_layer(inputs, per_core_seqs, n_cores):
    seq_lens = tuple(a.shape[0] for a in per_core_seqs[0])
    smax = max(seq_lens)
    if seq_lens not in _CACHE:
        _CACHE[seq_lens] = build_program(list(seq_lens), smax)
    nc = _CACHE[seq_lens]
    common = host_prep(inputs, smax)
    in_maps = []
    for cseqs in per_core_seqs:
        m = dict(common)
        for i, a in enumerate(cseqs):
            m["x%d" % i] = np.ascontiguousarray(a, dtype=np.float32)
        in_maps.append(m)
    res = run_bass_kernel_spmd(nc, in_maps, core_ids=list(range(n_cores)))
    global LAST_RES
    LAST_RES = res
    return [[r["y%d" % i] for i in range(len(seq_lens))] for r in res.results]


def kernel(**inputs):
    xp = np.asarray(inputs["x_prompt"], dtype=np.float32)
    xsm = np.asarray(inputs["x_sample"], dtype=np.float32)
    n = 8
    per_core = [[xp[2 * c], xp[2 * c + 1], xsm[c]] for c in range(n)]
    outs = run_layer(inputs, per_core, n)
    yp = np.empty_like(xp)
    ys = np.empty_like(xsm)
    for c in range(n):
        yp[2 * c], yp[2 * c + 1], ys[c] = outs[c][0], outs[c][1], outs[c][2]
    return (yp, ys)
```

```python
import numpy as np
import concourse.bass as bass
import concourse.mybir as mybir
from concourse.bass_utils import run_bass_kernel_spmd

F32 = mybir.dt.float32
BF16 = mybir.dt.bfloat16
I32 = mybir.dt.int32
AF = mybir.ActivationFunctionType
ALU = mybir.AluOpType
DTB = {F32: 4, BF16: 2, I32: 4}
GRAN = 256
COMPUTE = ("pe", "act", "dve", "pool")


class View:
    __slots__ = ("ap", "space", "lo", "hi")

    def __init__(self, ap, space, lo, hi):
        self.ap, self.space, self.lo, self.hi = ap, space, lo, hi


class Buf:
    def __init__(self, prog, name, shape, dt, space, off):
        self.prog, self.name, self.shape, self.dt, self.space, self.off = prog, name, list(shape), dt, space, off
        self.esz = DTB[dt]
        self.nbytes = int(np.prod(shape[1:])) * self.esz
        nc = prog.nc
        if space == "sb":
            self.t = nc.alloc_sbuf_tensor_at(name, self.shape, dt, offset=off)
        else:
            self.t = prog.psum_handle(name, self.shape, dt, off)
        st = [1]
        for s in reversed(self.shape[2:]):
            st.insert(0, st[0] * s)
        self.strides = st

    def __getitem__(self, idx):
        if not isinstance(idx, tuple):
            idx = (idx,)
        idx = tuple(idx) + (slice(None),) * (len(self.shape) - len(idx))
        lo = 0
        hi = 0
        for d in range(1, len(self.shape)):
            i = idx[d]
            n = self.shape[d]
            if isinstance(i, int):
                a, b = i, i
            else:
                r = range(*i.indices(n))
                assert len(r) > 0, (self.name, idx)
                a, b = min(r[0], r[-1]), max(r[0], r[-1])
            lo += a * self.strides[d - 1]
            hi += b * self.strides[d - 1]
        return View(self.t[idx], self.space, self.off + lo * self.esz, self.off + (hi + 1) * self.esz)


class Op:
    __slots__ = ("eng", "fn", "reads", "writes", "sem", "inc", "deps", "cnt", "need", "dma", "idx", "tag", "xw", "stage")


class Prog:
    def __init__(self, nc):
        self.nc = nc
        self.ops = {e: [] for e in COMPUTE + ("sp",)}
        self.allops = []
        self.state = {}
        self.sb_off = (nc.sbuf_base + GRAN - 1) // GRAN * GRAN
        self.sb_top = nc.sbuf_top
        self.ps_banks = {}
        self.dma_streams = {}
        self.dcount = {}
        self.stage = ""

    def sb(self, name, shape, dt, off=None):
        esz = DTB[dt]
        nb = int(np.prod(shape[1:])) * esz
        if off is None:
            off = self.sb_off
            self.sb_off = (off + nb + GRAN - 1) // GRAN * GRAN
            assert self.sb_off <= self.sb_top, ("SBUF overflow", name, self.sb_off)
        return Buf(self, name, shape, dt, "sb", off)

    def psum_handle(self, name, shape, dt, off):
        bank = off // 2048
        assert off % 2048 == 0 and int(np.prod(shape[1:])) * DTB[dt] <= 2048
        key = (bank, dt)
        if bank not in self.ps_banks:
            self.ps_banks[bank] = self.nc.alloc_psum_tensor("psb%d" % bank, [128, 512], F32)
        t = self.ps_banks[bank]
        return t

    def ps(self, name, bank, dt=F32, n=None):
        n = n or (2048 // DTB[dt])
        b = Buf.__new__(Buf)
        b.prog, b.name, b.dt, b.space, b.off = self, name, dt, "ps", bank * 2048
        b.esz = DTB[dt]
        b.shape = [128, n]
        b.nbytes = n * b.esz
        b.strides = [1]
        if bank not in self.ps_banks:
            self.ps_banks[bank] = self.nc.alloc_psum_tensor("psb%d" % bank, [128, 512], F32)
        t = self.ps_banks[bank]
        if dt != F32:
            b.t = _Bitcast(t, dt, n)
        else:
            b.t = t
        return b

    def _grans(self, v):
        if v.space in ("sb", "ps"):
            return [(v.space, g) for g in range(v.lo // GRAN, (v.hi - 1) // GRAN + 1)]
        return [(v.space, g) for g in range(v.lo, v.hi)]

    def op(self, eng, fn, reads=(), writes=(), dma=None, ndma=1, tag=""):
        o = Op()
        o.eng, o.fn, o.tag = eng, fn, tag
        o.dma = dma
        o.need = False
        o.idx = len(self.allops)
        o.xw = []
        o.cnt = None
        o.stage = self.stage
        if dma is not None:
            o.sem = "dma:" + dma
            o.inc = 16 * ndma
            self.dcount[o.sem] = self.dcount.get(o.sem, 0) + o.inc
            o.cnt = self.dcount[o.sem]
            o.need = True
        else:
            o.sem = eng
            o.inc = 1
        deps = {}
        for v in reads:
            for g in self._grans(v):
                s = self.state.get(g)
                if s and s[0] is not None:
                    deps[s[0].idx] = (s[0], "raw")
        for v in writes:
            for g in self._grans(v):
                s = self.state.get(g)
                if s:
                    if s[0] is not None and s[0].idx not in deps:
                        deps[s[0].idx] = (s[0], "waw")
                    for r in s[1].values():
                        if r.idx not in deps:
                            deps[r.idx] = (r, "war")
        keep = []
        for d, kind in deps.values():
            if d is o:
                continue
            same = (d.eng == eng) and d.dma is None and dma is None
            if same and eng == "pe":
                continue
            keep.append(d)
            d.need = True
        o.deps = keep
        rkey = eng if dma is None else ("dma", o.idx)
        for v in reads:
            for g in self._grans(v):
                s = self.state.setdefault(g, [None, {}])
                s[1][rkey] = o
        for v in writes:
            for g in self._grans(v):
                self.state[g] = [o, {}]
        self.ops[eng].append(o)
        self.allops.append(o)
        return o

    def stream_wait(self, streams, eng="sp"):
        o = self.op(eng, None, tag="swait")
        o.xw = [("dma:" + s, self.dcount["dma:" + s]) for s in streams if ("dma:" + s) in self.dcount]
        return o

    def emit(self, final_streams=()):
        nc = self.nc
        cnt = dict(self.dcount)
        for o in self.allops:
            if o.dma is None:
                if o.need and o.fn is not None:
                    cnt[o.sem] = cnt.get(o.sem, 0) + 1
                    o.cnt = cnt[o.sem]
                elif o.need:
                    raise AssertionError("dependency on a wait-only op")
        self.final_cnt = cnt
        semnames = sorted(cnt.keys())
        import contextlib
        with contextlib.ExitStack() as es:
            sems = {}
            for i, s in enumerate(semnames):
                sems[s] = es.enter_context(nc.semaphore("s_" + s.replace(":", "_")))
            block = es.enter_context(nc.Block())
            engmap = {"pe": block.tensor, "act": block.scalar, "dve": block.vector, "pool": block.gpsimd, "sp": block.sync}

            def run(engname):
                def body(eng):
                    waited = {}
                    for o in self.ops[engname]:
                        for d in o.deps:
                            if waited.get(d.sem, 0) < d.cnt:
                                eng.wait_ge(sems[d.sem], d.cnt)
                                waited[d.sem] = d.cnt
                        for (sm, val) in o.xw:
                            if waited.get(sm, 0) < val:
                                eng.wait_ge(sems[sm], val)
                                waited[sm] = val
                        if o.fn is None:
                            continue
                        r = o.fn(eng)
                        if o.need:
                            if o.dma is not None:
                                rs = r if isinstance(r, (list, tuple)) else [r]
                                assert len(rs) * 16 == o.inc, (o.tag, len(rs), o.inc)
                                for x in rs:
                                    x.then_inc(sems[o.sem], 16)
                            else:
                                r.then_inc(sems[o.sem], 1)
                    if engname == "sp":
                        for s in final_streams:
                            k = "dma:" + s
                            if k in cnt:
                                eng.wait_ge(sems[k], cnt[k])
                return body

            for e in ("sp", "pe", "act", "dve", "pool"):
                if self.ops[e] or e == "sp":
                    engmap[e](run(e))


class _Bitcast:
    def __init__(self, t, dt, n):
        self.t, self.dt, self.n = t, dt, n

    def __getitem__(self, idx):
        ap = self.t[:, :].bitcast(self.dt)
        return ap[idx]


D = 1024
KB = 8
NQ = 8
LW = 1280
LB = 10
DFF = 2816
FB = 22
INW = 6144
O1, O2, O3, O4, O5, O6 = 1024, 1280, 1536, 2816, 4096, 5120
T = 512
EPS = 1e-6
NEG = -30000.0
SCALE = 1.0 / float(np.sqrt(128.0))
VC_CONVW, VC_CONVB, VC_BA, VC_BI, VC_LAM, VC_GPRE, VC_GFFN = 0, 40, 50, 70, 90, 110, 118
NVEC = 126


def _reads(*vs):
    return [v for v in vs if isinstance(v, View)]


def _a(v):
    return v.ap if isinstance(v, View) else v


class K:
    def __init__(self, P):
        self.P = P

    def mm(self, out, lhsT, rhs, start=True, stop=True, nogroup=False):
        if nogroup:
            self.P.op("pe", lambda e: e.matmul(out.ap, lhsT=lhsT.ap, rhs=rhs.ap, start=start, stop=stop, skip_group_check=True),
                      reads=[lhsT, rhs], writes=[out])
        else:
            self.P.op("pe", lambda e: e.matmul(out.ap, lhsT=lhsT.ap, rhs=rhs.ap, start=start, stop=stop),
                      reads=[lhsT, rhs], writes=[out])

    def tr(self, out, in_, ident):
        self.P.op("pe", lambda e: e.transpose(out=out.ap, in_=in_.ap, identity=ident.ap), reads=[in_, ident], writes=[out])

    def act(self, out, in_, func, scale=1.0, bias=0.0, accum=None):
        w = [out] + ([accum] if accum is not None else [])
        kw = {}
        if accum is not None:
            kw["accum_out"] = accum.ap
        self.P.op("act", lambda e: e.activation(out=out.ap, in_=in_.ap, func=func, bias=_a(bias), scale=_a(scale), **kw),
                  reads=[in_] + _reads(scale, bias), writes=w)

    def ts(self, eng, out, in0, s1, s2, op0, op1=None):
        if op1 is None:
            self.P.op(eng, lambda e: e.tensor_scalar(out=out.ap, in0=in0.ap, scalar1=_a(s1), scalar2=None, op0=op0),
                      reads=[in0] + _reads(s1), writes=[out])
        else:
            self.P.op(eng, lambda e: e.tensor_scalar(out=out.ap, in0=in0.ap, scalar1=_a(s1), scalar2=_a(s2), op0=op0, op1=op1),
                      reads=[in0] + _reads(s1, s2), writes=[out])

    def tt(self, eng, out, in0, in1, op):
        self.P.op(eng, lambda e: e.tensor_tensor(out=out.ap, in0=in0.ap, in1=in1.ap, op=op), reads=[in0, in1], writes=[out])

    def stt(self, eng, out, in0, s, in1, op0, op1):
        self.P.op(eng, lambda e: e.scalar_tensor_tensor(out=out.ap, in0=in0.ap, scalar=_a(s), in1=in1.ap, op0=op0, op1=op1),
                  reads=[in0, in1] + _reads(s), writes=[out])

    def copy(self, eng, out, in_):
        if eng == "act":
            self.act(out, in_, AF.Copy)
        else:
            self.P.op(eng, lambda e: e.tensor_copy(out=out.ap, in_=in_.ap), reads=[in_], writes=[out])

    def memset(self, eng, out, val):
        self.P.op(eng, lambda e: e.memset(out.ap, val), writes=[out])

    def recip(self, out, in_):
        self.P.op("dve", lambda e: e.reciprocal(out=out.ap, in_=in_.ap), reads=[in_], writes=[out])

    def scan(self, out, a, u, init):
        self.P.op("dve", lambda e: e.tensor_tensor_scan(out=out.ap, data0=a.ap, data1=u.ap, initial=_a(init), op0=ALU.mult, op1=ALU.add),
                  reads=[a, u] + _reads(init), writes=[out])

    def dma(self, stream, out, in_, reads=(), writes=()):
        self.P.op("sp", lambda e: e.dma_start(out=_a(out), in_=_a(in_)), reads=list(reads) + _reads(in_), writes=list(writes) + _reads(out), dma=stream)

    def dma_group(self, stream, pairs):
        self.P.op("sp", lambda e: [e.dma_start(out=_a(o), in_=_a(i)) for (o, i) in pairs],
                  reads=[i for (o, i) in pairs if isinstance(i, View)],
                  writes=[o for (o, i) in pairs if isinstance(o, View)], dma=stream, ndma=len(pairs))


class WStream:
    def __init__(self, k, name, slots, eng="sp"):
        self.k, self.name, self.slots, self.eng = k, name, slots, eng
        self.plan = []
        self.loaded = 0
        self.cur = 0
        self.limit = None

    def add(self, tag, dram_ap, shape):
        self.plan.append((dram_ap, shape, tag))

    def next(self, tag=None, hold=1):
        i = self.cur
        self.cur += 1
        assert tag is None or self.plan[i][2] == tag, (i, tag, self.plan[i][2])
        nd = len(self.slots)
        lim = len(self.plan) if self.limit is None else self.limit
        while self.loaded < min(lim, i + nd - hold + 1):
            j = self.loaded
            slot = self.slots[j % nd]
            shape = self.plan[j][1]
            n = int(np.prod(shape[1:]))
            v = slot[:, 0:n]
            ap = slot.t[:, 0:n].rearrange("p (a b) -> p a b", a=shape[1])
            self.k.P.op(self.eng, lambda e, ap=ap, src=self.plan[j][0]: e.dma_start(out=ap, in_=src),
                        writes=[v], dma="%s%d" % (self.name, j % nd))
            self.loaded += 1
        return self.slots[i % nd]


DEBUG = False
WENG = "pool"
OPT_B = True


def build_program(seq_lens, smax):
    nc = bass.Bass("TRN2", target_bir_lowering=False)
    P = Prog(nc)
    k = K(P)
    dbg_n = [0]

    def dbg(name, view, shape, dt):
        if not DEBUG:
            return
        d = nc.dram_tensor("dbg_" + name, list(shape), dt, kind="ExternalOutput").ap()
        P.op("sp", lambda e: e.dma_start(out=d, in_=view.ap), reads=[view], dma="dbg")

    def din(name, shape, dt=F32):
        return nc.dram_tensor(name, list(shape), dt, kind="ExternalInput").ap()

    def dscr(name, shape, dt=BF16):
        return nc.dram_tensor(name, list(shape), dt, kind="Internal").ap()

    xs = [din("x%d" % i, [S, D]) for i, S in enumerate(seq_lens)]
    ys = [nc.dram_tensor("y%d" % i, [S, D], F32, kind="ExternalOutput").ap() for i, S in enumerate(seq_lens)]
    w_in_d = din("w_in", [D, INW])
    w_ap_d = din("w_attn_proj", [D, D])
    w_rp_d = din("w_rec_proj", [LW, D])
    w_out_d = din("w_out", [D, D])
    w_fi_d = din("w_ffn_in", [D, 2 * DFF])
    w_fo_d = din("w_ffn_out", [DFF, D])
    lwa_d = din("lru_w_a", [2, LB, 128, 128])
    lwi_d = din("lru_w_i", [2, LB, 128, 128])
    vecs_d = din("vecs", [128, NVEC])
    gpost_d = din("gpost", [2, D])
    sink_d = din("sink", [1, NQ])
    rope_d = din("rope", [32, 2, smax + 256])

    Win_p = dscr("Win_p", [128, INW // 256, KB, 256])
    Wap_p = dscr("Wap_p", [128, D // 256, KB, 256])
    Wrp_p = dscr("Wrp_p", [128, D // 256, LB, 256])
    Wout_p = dscr("Wout_p", [128, 2, KB, 512])
    Wfi_p = dscr("Wfi_p", [128, 2 * DFF // 256, KB, 256])
    Wfo_p = dscr("Wfo_p", [128, 2, FB, 512])
    Wg_b = dscr("Wg_b", [128, 2, 2 * LB, 128])
    recT_d = [dscr("recT%d" % i, [128, LB, S]) for i, S in enumerate(seq_lens)]

    Smax = max(seq_lens)
    vecs = P.sb("vecs", [128, NVEC], F32)
    der = P.sb("der", [128, 64], F32)
    der2 = P.sb("der2", [128, 64], F32)
    gpost = P.sb("gpost", [128, 2, D], F32)
    esink = P.sb("esink", [128, NQ], F32)
    ident = P.sb("ident", [128, 128], BF16)
    ones = P.sb("ones", [128, 128], BF16)
    mlo = P.sb("mlo", [128, 128], BF16)
    mhi = P.sb("mhi", [128, 128], BF16)
    prot = P.sb("prot", [128, 32], BF16)
    cst = P.sb("cst", [128, 8], F32)
    stA = P.sb("stA", [128, 64], F32)
    stO = [P.sb("stO%d" % i, [128, 64], F32) for i in range(2)]
    stF = [P.sb("stF%d" % i, [128, 64], F32) for i in range(2)]
    stY = [P.sb("stY%d" % i, [128, 64], F32) for i in range(2)]
    hT = P.sb("hT", [128, KB, Smax], BF16)
    ARENA0 = P.sb_off

    PSB = [P.ps("ps%d" % b, b, F32) for b in range(8)]
    PSH = [P.ps("psh%d" % b, b, BF16) for b in range(8)]

    def affsel(v, pattern, cmp, fill, base, cm):
        P.op("pool", lambda e: e.affine_select(out=v.ap, in_=v.ap, pattern=pattern, compare_op=cmp, fill=fill, base=base, channel_multiplier=cm),
             reads=[v], writes=[v])

    scr = [P.sb("cscr%d" % i, [128, 128], F32, off=ARENA0 + 512 * i) for i in range(4)]
    k.memset("pool", scr[0][:, :], 1.0)
    affsel(scr[0][:, :], [[-1, 128]], ALU.is_equal, 0.0, 0, 1)
    k.copy("pool", ident[:, :], scr[0][:, :])
    k.memset("pool", ones[:, :], 1.0)
    k.memset("pool", scr[1][:, :], 0.0)
    affsel(scr[1][:, :], [[1, 128]], ALU.is_ge, NEG, 0, -1)
    k.copy("pool", mlo[:, :], scr[1][:, :])
    k.memset("pool", scr[2][:, :], 0.0)
    affsel(scr[2][:, :], [[-1, 128]], ALU.is_ge, NEG, 0, 1)
    k.copy("pool", mhi[:, :], scr[2][:, :])
    k.memset("pool", scr[3][:, 0:32], 0.0)
    affsel(scr[3][:, 0:16], [[-1, 16]], ALU.not_equal, -1.0, -16, 1)
    affsel(scr[3][:, 16:32], [[-1, 16]], ALU.not_equal, 1.0, 0, 1)
    k.copy("pool", prot[:, :], scr[3][:, 0:32])
    k.memset("pool", cst[:, 0:1], 0.25)
    k.memset("pool", cst[:, 1:2], EPS)
    k.memset("pool", cst[:, 2:3], 1.0)

    k.dma("m0", vecs[:, :], vecs_d)
    k.dma("m1", gpost[:, 0, :], gpost_d[0:1, :].partition_broadcast(128))
    k.dma("m2", gpost[:, 1, :], gpost_d[1:2, :].partition_broadcast(128))
    k.dma("m3", esink[:, :], sink_d[0:1, :].partition_broadcast(128))
    k.act(esink[:, :], esink[:, :], AF.Exp)
    k.ts("dve", der[:, 0:20], vecs[:, VC_BA:VC_BA + 20], 0.5, None, ALU.mult)
    k.ts("dve", der[:, 20:40], vecs[:, VC_BI:VC_BI + 20], 0.5, None, ALU.mult)
    k.act(der[:, 40:60], vecs[:, VC_LAM:VC_LAM + 20], AF.Exp, scale=-1.0)
    k.act(der[:, 40:60], der[:, 40:60], AF.Ln, bias=cst[:, 2:3])
    k.ts("dve", der2[:, 0:20], der[:, 40:60], -4.0, None, ALU.mult)
    k.ts("dve", der2[:, 20:40], der[:, 40:60], -8.0, None, ALU.mult)

    CW = 2048
    stg_f = [P.sb("stgf%d" % i, [128, CW], F32, off=ARENA0 + 4096 + i * (CW * 6)) for i in range(3)]
    stg_b = [P.sb("stgb%d" % i, [128, CW], BF16, off=ARENA0 + 4096 + i * (CW * 6) + CW * 4) for i in range(3)]
    cvi = [0]
    engs = ["act", "dve"]

    cvjobs = []

    def convert(src_ap, dst_ap, width, scale_v, ld3=None, st3=None):
        cvjobs.append((src_ap, dst_ap, width, scale_v, ld3, st3))

    def cv_load(i):
        src_ap, dst_ap, width, scale_v, ld3, st3 = cvjobs[i]
        sf = stg_f[i % 3]
        o_ap = sf[:, 0:width].ap if ld3 is None else sf.t[:, 0:width].rearrange("p (a b) -> p a b", a=ld3)
        P.op("sp", lambda e: e.dma_start(out=o_ap, in_=src_ap), writes=[sf[:, 0:width]], dma="cvl%d" % (i % 3))

    def cv_store(i):
        src_ap, dst_ap, width, scale_v, ld3, st3 = cvjobs[i]
        sf, sb_ = stg_f[i % 3], stg_b[i % 3]
        i_ap = sb_[:, 0:width].ap if st3 is None else sb_.t[:, 0:width].rearrange("p (a b) -> p a b", a=st3)
        eng = engs[i % 2]
        if scale_v is None:
            k.copy(eng, sb_[:, 0:width], sf[:, 0:width])
        elif eng == "act":
            k.act(sb_[:, 0:width], sf[:, 0:width], AF.Copy, scale=scale_v)
        else:
            k.ts(eng, sb_[:, 0:width], sf[:, 0:width], scale_v, None, ALU.mult)
        P.op("pool", lambda e: e.dma_start(out=dst_ap, in_=i_ap), reads=[sb_[:, 0:width]], dma="cvs%d" % (i % 3))

    def cv_run():
        n = len(cvjobs)
        for i in range(n + 2):
            if i < n:
                cv_load(i)
            if i >= 2:
                cv_store(i - 2)

    def convert_cols(src, dst, nkb, ncols, scale_col):
        for kb in range(nkb):
            for c0 in range(0, ncols, CW):
                w = min(CW, ncols - c0)
                sv = vecs[:, scale_col + kb:scale_col + kb + 1] if scale_col is not None else None
                convert(src[kb * 128:(kb + 1) * 128, c0:c0 + w], dst[:, c0 // 256:(c0 + w) // 256, kb, :], w, sv, st3=w // 256)

    def convert_rows(src, dst, nkb):
        for kb in range(nkb):
            convert(src[kb * 128:(kb + 1) * 128, :], dst[:, :, kb, :], D, None, st3=2)

    for gi, src in enumerate((lwa_d, lwi_d)):
        for dr in range(2):
            convert(src[dr].rearrange("n c d -> c n d"), Wg_b[:, gi, dr * LB:(dr + 1) * LB, :], LB * 128, None, ld3=LB, st3=LB)
    convert_cols(w_in_d, Win_p, KB, INW, VC_GPRE)
    convert_cols(w_ap_d, Wap_p, KB, D, None)
    convert_cols(w_rp_d, Wrp_p, LB, D, None)
    convert_rows(w_out_d, Wout_p, KB)
    convert_cols(w_fi_d, Wfi_p, KB, 2 * DFF, VC_GFFN)
    convert_rows(w_fo_d, Wfo_p, FB)
    cv_run()
    P.stream_wait(["cvs0", "cvs1", "cvs2"])

    WSLOT = 5632
    NWS = 8
    wslots = [P.sb("wslot%d" % i, [128, WSLOT // 2], BF16, off=ARENA0 + i * WSLOT) for i in range(NWS)]
    A2 = ARENA0 + NWS * WSLOT
    o = [A2]

    def take(n):
        r = o[0]
        o[0] = (r + n + GRAN - 1) // GRAN * GRAN
        assert o[0] <= P.sb_top, ("arena overflow", o[0], P.sb_top)
        return r

    SF = Smax * 4
    o[0] = ARENA0
    R_off = take(SF + 16)
    XC_off = take(SF)
    XCB_off = take(Smax * 2)
    B_off = [take(SF) for _ in range(4)]
    RB_off = XCB_off
    W1_off = [take(4096) for _ in range(3)]
    GW_off = take(2 * 2 * LB * 128 * 2)
    p1_end = o[0]
    o[0] = R_off
    X1_off = [take(4096) for _ in range(8)]
    XN_off = [take(2048) for _ in range(2)]
    JUNK1_off = take(2048)
    p1_end = max(p1_end, o[0])
    o[0] = A2
    XT_off = take(4 * 4096)
    QT_off = take(NQ * T * 2)
    AT_off = take(NQ * T * 2)
    REC_off = take(LB * T * 2)
    TMP_off = [take(2048) for _ in range(8)]
    JUNK_off = REC_off
    XN2_off = [REC_off + 2048, REC_off + 4096]
    ROPE_off = take(32 * 0 + 2 * 768 * 4)
    G_off = take(FB * T * 2)
    p2_end = o[0]

    xt = P.sb("xt", [128, 4, D], F32, off=XT_off)
    qT = P.sb("qT", [128, NQ, T], BF16, off=QT_off)
    mgT = qT
    atT = P.sb("atT", [128, NQ, T], BF16, off=AT_off)
    h2T = atT
    recc = P.sb("recc", [128, LB, T], BF16, off=REC_off)
    tmp = [P.sb("tmp%d" % i, [128, T], F32, off=TMP_off[i]) for i in range(8)]
    junk = P.sb("junk", [128, D], BF16, off=JUNK_off)
    xn2 = [P.sb("xn2_%d" % i, [128, D], BF16, off=XN2_off[i]) for i in range(2)]
    kT = P.sb("kT", [128, 2, 768], BF16, off=G_off)
    Vt = P.sb("Vt", [128, 6, 256], BF16, off=G_off + 3072)
    PT = [P.sb("PT%d" % i, [128, 384], BF16, off=G_off + 6144 + i * 1024) for i in range(4)]
    ropeC = P.sb("ropeC", [32, 2, 768], F32, off=ROPE_off)
    acT = P.sb("acT", [128, FB, T], BF16, off=G_off)
    w1slots = [P.sb("w1slot%d" % i, [128, KB * 256], BF16, off=W1_off[i]) for i in range(3)]
    GW = P.sb("GW", [128, 2, 2 * LB, 128], BF16, off=GW_off)
    x1 = [P.sb("x1_%d" % i, [128, D], F32, off=X1_off[i]) for i in range(8)]
    xn = [P.sb("xn_%d" % i, [128, D], BF16, off=XN_off[i]) for i in range(2)]
    junk1 = P.sb("junk1", [128, D], BF16, off=JUNK1_off)

    FO_PIECES = [(0, 5), (5, 10), (10, 15), (15, 20), (20, 22)]
    ws2 = WStream(k, "w", wslots, eng=WENG)
    ws2_lim = []
    for si, S in enumerate(seq_lens):
        ws2_lim.append(0)
        for c in range(S // T):
            ws2.add("k", Win_p[:, O1 // 256], [128, KB, 256])
            ws2.add("v", Win_p[:, O2 // 256], [128, KB, 256])
            for hp in range(4):
                ws2.add("q%d" % hp, Win_p[:, hp], [128, KB, 256])
            for qt in range(4):
                ws2.add("ga%d" % qt, Win_p[:, O5 // 256 + qt], [128, KB, 256])
                ws2.add("pa%d" % qt, Wap_p[:, qt], [128, KB, 256])
                ws2.add("gr%d" % qt, Win_p[:, O6 // 256 + qt], [128, KB, 256])
                ws2.add("pr%d" % qt, Wrp_p[:, qt], [128, LB, 256])
            for half in range(2):
                for kg in range(2):
                    ws2.add("wo%d%d" % (half, kg), Wout_p[:, half, kg * 4:(kg + 1) * 4, :], [128, 4, 512])
            for g2 in range(FB // 2):
                ws2.add("fg%d" % g2, Wfi_p[:, g2], [128, KB, 256])
                ws2.add("fu%d" % g2, Wfi_p[:, DFF // 256 + g2], [128, KB, 256])
            for half in range(2):
                for (j0, j1) in FO_PIECES:
                    ws2.add("fo%d_%d" % (half, j0), Wfo_p[:, half, j0:j1, :], [128, j1 - j0, 512])
        ws2_lim[si] = len(ws2.plan)

    def rstd_from(st, src_cols, n, dst0, add_col=None):
        if add_col is None:
            k.ts("dve", st[:, 40:40 + n], st[:, src_cols:src_cols + n], 1.0 / D, cst[:, 1:2], ALU.mult, ALU.add)
        else:
            k.ts("dve", st[:, 40:40 + n], st[:, src_cols:src_cols + n], st[:, add_col:add_col + 1], 1.0 / D, ALU.add, ALU.mult)
        k.act(st[:, 41 + n:41 + 2 * n], st[:, 40:40 + n], AF.Ln, bias=(0.0 if add_col is None else cst[:, 1:2]))
        k.act(st[:, dst0:dst0 + n], st[:, 41 + n:41 + 2 * n], AF.Exp, scale=-0.5)

    def rope(raw, kvh, c_lo, n, bank_raw, bank_sw, tA, tB, tab_lo):
        k.mm(bank_sw[0:32, 0:n], prot[:, 0:32], raw[:, kvh, c_lo:c_lo + n])
        k.tt("dve", tA[0:32, 0:n], bank_sw[0:32, 0:n], ropeC[0:32, 1, tab_lo:tab_lo + n], ALU.mult)
        k.tt("dve", tB[0:32, 0:n], bank_raw[0:32, 0:n], ropeC[0:32, 0, tab_lo:tab_lo + n], ALU.mult)
        k.tt("dve", raw[0:32, kvh, c_lo:c_lo + n], tA[0:32, 0:n], tB[0:32, 0:n], ALU.add)

    for si, S in enumerate(seq_lens):
        NT = S // 128
        NCH = S // T
        xd = xs[si]
        P.stage = "p1a"
        for g in range(NT // 4):
            for j in range(4):
                t = g * 4 + j
                k.dma("x1l%d" % (t % 8), x1[t % 8][:, :], xd[t * 128:(t + 1) * 128, :])
            for j in range(4):
                t = g * 4 + j
                k.act(junk1[:, :], x1[t % 8][:, :], AF.Square, accum=stA[:, j:j + 1])
            rstd_from(stA, 0, 4, 8)
            for j in range(4):
                t = g * 4 + j
                xnb = xn[t % 2]
                k.ts("dve", xnb[:, :], x1[t % 8][:, :], stA[:, 8 + j:9 + j], None, ALU.mult)
                bank = t % 2
                for kb in range(KB):
                    k.tr(PSH[bank][:, kb * 128:(kb + 1) * 128], xnb[:, kb * 128:(kb + 1) * 128], ident[:, :])
                srcv = PSH[bank][:, :]
                dstv = hT[:, :, t * 128:(t + 1) * 128]
                srcap = PSH[bank].t[:, :].rearrange("p (a b) -> p a b", a=KB)
                if t % 2:
                    P.op("dve", lambda e, d=dstv, s=srcap: e.tensor_copy(out=d.ap, in_=s), reads=[srcv], writes=[dstv])
                else:
                    P.op("act", lambda e, d=dstv, s=srcap: e.activation(out=d.ap, in_=s, func=AF.Copy), reads=[srcv], writes=[dstv])

        if si == 0:
            dbg("hT", hT[:, :, 0:S], [128, KB, S], BF16)
        P.stage = "p1b"
        Rb = P.sb("R_%d" % si, [128, S + 4], F32, off=R_off)
        XC = P.sb("XC_%d" % si, [128, S], F32, off=XC_off)
        XCB = P.sb("XCB_%d" % si, [128, S], BF16, off=XCB_off)
        B = [P.sb("B%d_%d" % (i, si), [128, S], F32, off=B_off[i]) for i in range(4)]
        RB = P.sb("RB_%d" % si, [128, S], BF16, off=RB_off)
        ws1 = WStream(k, "v", w1slots)
        k.dma("m0", GW[:, :, :, :], Wg_b)
        for bp in range(LB // 2):
            ws1.add("rx", Win_p[:, O3 // 256 + bp], [128, KB, 256])
            ws1.add("gz", Win_p[:, O4 // 256 + bp], [128, KB, 256])
        for blk in range(LB):
            if blk % 2 == 0:
                wrx = ws1.next("rx", 2)
            k.memset("pool", Rb[:, 0:2], 0.0)
            k.memset("pool", Rb[:, S + 2:S + 4], 0.0)
            for c in range(NCH):
                bank = PSB[c % 2]
                for kb in range(KB):
                    k.mm(bank[:, :], wrx[:, kb * 256 + (blk % 2) * 128:kb * 256 + (blk % 2 + 1) * 128], hT[:, kb, c * T:(c + 1) * T], start=(kb == 0), stop=(kb == KB - 1))
                k.act(Rb[:, 2 + c * T:2 + (c + 1) * T], bank[:, :], AF.Copy)

            def cw(tp, blk=blk):
                return vecs[:, VC_CONVW + blk * 4 + tp:VC_CONVW + blk * 4 + tp + 1]

            halves = [(0, S // 2), (S // 2, S)] if S >= 1024 else [(0, S)]
            for (a0, b0) in halves:
                k.ts("dve", XC[:, a0:b0], Rb[:, a0 + 2:b0 + 2], cw(2), vecs[:, VC_CONVB + blk:VC_CONVB + blk + 1], ALU.mult, ALU.add)
                for tp in (0, 1, 3):
                    k.stt("dve", XC[:, a0:b0], Rb[:, a0 + tp:b0 + tp], cw(tp), XC[:, a0:b0], ALU.mult, ALU.add)
                k.copy("dve", XCB[:, a0:b0], XC[:, a0:b0])
            if si == 0 and blk == 0:
                dbg("xc0", XC[:, :], [128, S], F32)
            for dr in range(2):
                THA, THI, AA = (B[0], B[1], B[2]) if dr == 0 else (Rb, B[3], B[2])
                col = dr * LB + blk
                for c in range(NCH):
                    ba, bi = PSB[2 + (c % 2) * 2], PSB[3 + (c % 2) * 2]
                    k.mm(ba[:, :], GW[:, 0, col, :], XCB[:, c * T:(c + 1) * T])
                    k.mm(bi[:, :], GW[:, 1, col, :], XCB[:, c * T:(c + 1) * T])
                    k.act(THA[:, c * T:(c + 1) * T], ba[:, :], AF.Tanh, scale=0.5, bias=der[:, col:col + 1])
                    k.act(THI[:, c * T:(c + 1) * T], bi[:, :], AF.Tanh, scale=0.5, bias=der[:, 20 + col:21 + col])
                if dr == 1:
                    if blk % 2 == 0:
                        wgt = ws1.next("gz", 2)
                    for c in range(NCH):
                        bank = PSB[6 + c % 2]
                        for kb in range(KB):
                            k.mm(bank[:, :], wgt[:, kb * 256 + (blk % 2) * 128:kb * 256 + (blk % 2 + 1) * 128], hT[:, kb, c * T:(c + 1) * T], start=(kb == 0), stop=(kb == KB - 1))
                        k.act(B[1][:, c * T:(c + 1) * T], bank[:, :], AF.Gelu_apprx_tanh)
                for (a0, b0) in halves:
                    k.act(AA[:, a0:b0], THA[:, a0:b0], AF.Exp, scale=der2[:, col:col + 1], bias=der2[:, col:col + 1])
                    k.tt("dve", THA[:, a0:b0], AA[:, a0:b0], AA[:, a0:b0], ALU.mult)
                    k.stt("dve", THI[:, a0:b0], THI[:, a0:b0], 1.0, XC[:, a0:b0], ALU.add, ALU.mult)
                for (a0, b0) in halves:
                    k.act(THA[:, a0:b0], THA[:, a0:b0], AF.Sqrt, scale=-0.25, bias=cst[:, 0:1])
                    k.tt("dve", THI[:, a0:b0], THI[:, a0:b0], THA[:, a0:b0], ALU.mult)
                if dr == 0:
                    for hi_, (a0, b0) in enumerate(halves):
                        init = 0.0 if hi_ == 0 else THA[:, a0 - 1:a0]
                        k.scan(THA[:, a0:b0], AA[:, a0:b0], THI[:, a0:b0], init)
                else:
                    for hi_, (a0, b0) in enumerate(reversed(halves)):
                        init = 0.0 if hi_ == 0 else THI[:, b0:b0 + 1]
                        rs = slice(b0 - 1, (a0 - 1 if a0 > 0 else None), -1)
                        k.scan(THI[:, rs], AA[:, rs], THI[:, rs], init)
            if si == 0 and blk == 0:
                dbg("hf0", B[0][:, :], [128, S], F32)
                dbg("hb0", B[3][:, :], [128, S], F32)
            for (a0, b0) in halves:
                k.tt("dve", B[0][:, a0:b0], B[0][:, a0:b0], B[3][:, a0:b0], ALU.add)
                k.tt("dve", RB[:, a0:b0], B[0][:, a0:b0], B[1][:, a0:b0], ALU.mult)
            k.dma("recst", recT_d[si][:, blk, :], RB[:, :])
            if si == 0 and blk == 0:
                dbg("rec0", RB[:, :], [128, S], BF16)
        P.stream_wait(["recst"])

        ws2.limit = ws2_lim[si]
        for c in range(NCH):
            c0 = c * T
            lo, hi = max(0, c0 - 128), min(S, c0 + T + 128)
            W = hi - lo
            tl_lo, tl_hi = lo // 128, hi // 128
            qt0 = c0 // 128
            if c == 0:
                k.dma("rope", ropeC[0:32, :, 0:W], rope_d[:, :, lo + 128:hi + 128])
            k.dma("recl", recc[:, :, :], recT_d[si][:, :, c0:c0 + T])
            k.dma_group("xt", [(xt[:, j, :], xd[c0 + j * 128:c0 + (j + 1) * 128, :]) for j in range(4)])
            P.stage = "K"
            wk = ws2.next("k")
            pieces = [(a, min(a + 512, hi)) for a in range(lo, hi, 512)]
            pend = None
            for kv in range(2):
                for pi, (a, b) in enumerate(pieces):
                    n = b - a
                    bank, bsw = PSB[(kv * 2 + pi) % 4], PSB[4 + (kv * 2 + pi) % 2]
                    for kb in range(KB):
                        k.mm(bank[:, 0:n], wk[:, kb * 256 + kv * 128:kb * 256 + (kv + 1) * 128], hT[:, kb, a:b], start=(kb == 0), stop=(kb == KB - 1))
                    k.act(kT[:, kv, a - lo:b - lo], bank[:, 0:n], AF.Copy)
                    if pend is not None:
                        rope(*pend)
                    pend = (kT, kv, a - lo, n, bank, bsw, tmp[((kv * 2 + pi) * 2) % 8], tmp[((kv * 2 + pi) * 2 + 1) % 8], a - lo)
            P.stage = "V"
            wv = ws2.next("v")
            for jt in range(tl_lo, tl_hi):
                bank = PSB[6 + jt % 2]
                for kb in range(KB):
                    k.mm(bank[:, 0:256], hT[:, kb, jt * 128:(jt + 1) * 128], wv[:, kb * 256:(kb + 1) * 256], start=(kb == 0), stop=(kb == KB - 1))
                k.copy("act" if jt % 2 else "dve", Vt[:, jt - tl_lo, :], bank[:, 0:256])
                if pend is not None:
                    rope(*pend)
                    pend = None
            P.stage = "Q"
            for h in range(NQ):
                if h % 2 == 0:
                    wv_ = ws2.next("q%d" % (h // 2))
                bank, bsw = PSB[h % 4], PSB[4 + h % 2]
                for kb in range(KB):
                    k.mm(bank[:, :], wv_[:, kb * 256 + (h % 2) * 128:kb * 256 + (h % 2 + 1) * 128], hT[:, kb, c0:c0 + T], start=(kb == 0), stop=(kb == KB - 1))
                k.act(qT[:, h, :], bank[:, :], AF.Copy)
                if pend is not None:
                    rope(*pend)
                pend = (qT, h, 0, T, bank, bsw, tmp[(h * 2) % 8], tmp[(h * 2 + 1) % 8], c0 - lo)
            rope(*pend)
            pend = None
            if c + 1 < NCH:
                nlo, nhi = max(0, c0 + T - 128), min(S, c0 + 2 * T + 128)
                k.dma("rope", ropeC[0:32, :, 0:nhi - nlo], rope_d[:, :, nlo + 128:nhi + 128])
            if si == 0 and c == 0:
                dbg("qT", qT[:, :, :], [128, NQ, T], BF16)
                dbg("kT", kT[:, :, 0:W], [128, 2, W], BF16)
                dbg("Vt", Vt[:, 0:tl_hi - tl_lo, :], [128, tl_hi - tl_lo, 256], BF16)
            P.stage = "att"
            items = [(h, jt) for h in range(NQ) for jt in range(tl_lo, tl_hi)]

            def att_front(n_):
                h, jt = items[n_]
                kv = h // 4
                qs = [i for i in (jt - 1, jt, jt + 1) if qt0 <= i < qt0 + 4]
                qa, qb = (qs[0] - qt0) * 128, (qs[-1] - qt0 + 1) * 128
                n = qb - qa
                sb_ = PSB[n_ % 4]
                pt = PT[n_ % 4]
                k.mm(sb_[:, 0:n], kT[:, kv, (jt - tl_lo) * 128:(jt - tl_lo + 1) * 128], qT[:, h, qa:qb], start=True, stop=True, nogroup=True)
                for i in qs:
                    off = (i - qt0) * 128 - qa
                    if i == jt - 1:
                        k.mm(sb_[:, off:off + 128], ident[:, :], mlo[:, :], start=False, stop=True, nogroup=True)
                    elif i == jt + 1:
                        k.mm(sb_[:, off:off + 128], ident[:, :], mhi[:, :], start=False, stop=True, nogroup=True)
                k.act(pt[:, 0:n], sb_[:, 0:n], AF.Exp, scale=SCALE)
                return (h, jt, kv, qa, qb, n, pt)

            def att_back(info):
                h, jt, kv, qa, qb, n, pt = info
                bO, bD = PSB[4 + (h % 2) * 2], PSB[5 + (h % 2) * 2]
                first = (jt == tl_lo)
                last = (jt == tl_hi - 1)
                k.mm(bO[:, qa:qb], Vt[:, jt - tl_lo, kv * 128:(kv + 1) * 128], pt[:, 0:n], start=first, stop=last, nogroup=True)
                k.mm(bD[:, qa:qb], ones[:, :], pt[:, 0:n], start=first, stop=last, nogroup=True)
                if last:
                    td = tmp[h % 2]
                    k.act(td[:, :], bD[:, :], AF.Ln, bias=esink[:, h:h + 1])
                    k.act(td[:, :], td[:, :], AF.Exp, scale=-1.0)
                    k.tt("dve", atT[:, h, :], bO[:, :], td[:, :], ALU.mult)

            prev = None
            for n_ in range(len(items)):
                info = att_front(n_)
                if prev is not None:
                    att_back(prev)
                prev = info
            att_back(prev)
            if si == 0 and c == 0:
                dbg("atT", atT[:, :, :], [128, NQ, T], BF16)
            P.stage = "merge"
            for qt in range(4):
                tg = [tmp[(qt % 2) * 4 + m] for m in range(2)]
                tr_ = [tmp[(qt % 2) * 4 + 2 + m] for m in range(2)]
                bks = [PSB[(qt % 2) * 4 + i] for i in range(4)]
                wga = ws2.next("ga%d" % qt)
                for m in range(2):
                    for kb in range(KB):
                        k.mm(bks[m][:, :], wga[:, kb * 256 + m * 128:kb * 256 + (m + 1) * 128], hT[:, kb, c0:c0 + T], start=(kb == 0), stop=(kb == KB - 1))
                    k.act(tg[m][:, :], bks[m][:, :], AF.Sigmoid)
                wpa = ws2.next("pa%d" % qt)
                for m in range(2):
                    for kb in range(NQ):
                        k.mm(bks[2 + m][:, :], wpa[:, kb * 256 + m * 128:kb * 256 + (m + 1) * 128], atT[:, kb, :], start=(kb == 0), stop=(kb == NQ - 1))
                    k.tt("dve", tg[m][:, :], bks[2 + m][:, :], tg[m][:, :], ALU.mult)
                wgr = ws2.next("gr%d" % qt)
                for m in range(2):
                    for kb in range(KB):
                        k.mm(bks[m][:, :], wgr[:, kb * 256 + m * 128:kb * 256 + (m + 1) * 128], hT[:, kb, c0:c0 + T], start=(kb == 0), stop=(kb == KB - 1))
                    k.act(tr_[m][:, :], bks[m][:, :], AF.Sigmoid)
                wpr = ws2.next("pr%d" % qt)
                for m in range(2):
                    for kb in range(LB):
                        k.mm(bks[2 + m][:, :], wpr[:, kb * 256 + m * 128:kb * 256 + (m + 1) * 128], recc[:, kb, :], start=(kb == 0), stop=(kb == LB - 1))
                    k.tt("dve", tr_[m][:, :], bks[2 + m][:, :], tr_[m][:, :], ALU.mult)
                    k.tt("dve", mgT[:, qt * 2 + m, :], tg[m][:, :], tr_[m][:, :], ALU.add)
            wo = [[ws2.next("wo00", 1), ws2.next("wo01", 2)], [ws2.next("wo10", 3), ws2.next("wo11", 4)]]

            def wout_tile(i):
                P.stage = "wout"
                st = stO[i % 2]
                for half in range(2):
                    bank = PSB[(i % 2) * 2 + half]
                    for kb in range(KB):
                        k.mm(bank[:, :], mgT[:, kb, i * 128:(i + 1) * 128], wo[half][kb // 4][:, (kb % 4) * 512:(kb % 4 + 1) * 512], start=(kb == 0), stop=(kb == KB - 1))
                    k.act(junk[:, half * 512:(half + 1) * 512], bank[:, :], AF.Square, accum=st[:, half:half + 1])
                rstd_from(st, 0, 1, 4, add_col=1)
                for half in range(2):
                    bank = PSB[(i % 2) * 2 + half]
                    tb = tmp[(i % 2) * 2 + half]
                    k.stt("dve", tb[:, :], bank[:, :], st[:, 4:5], gpost[:, 0, half * 512:(half + 1) * 512], ALU.mult, ALU.mult)
                    k.tt("dve", xt[:, i, half * 512:(half + 1) * 512], xt[:, i, half * 512:(half + 1) * 512], tb[:, :], ALU.add)
                sf = stF[i % 2]
                k.act(junk[:, :], xt[:, i, :], AF.Square, accum=sf[:, 0:1])
                rstd_from(sf, 0, 1, 8)
                k.ts("dve", xn2[i % 2][:, :], xt[:, i, :], sf[:, 8:9], None, ALU.mult)

            def ffn_tr_tile(i):
                P.stage = "ffnT"
                xnb = xn2[i % 2]
                bank = 4 + i % 2
                for kb in range(KB):
                    k.tr(PSH[bank][:, kb * 128:(kb + 1) * 128], xnb[:, kb * 128:(kb + 1) * 128], ident[:, :])
                srcv = PSH[bank][:, :]
                dstv = h2T[:, :, i * 128:(i + 1) * 128]
                srcap = PSH[bank].t[:, :].rearrange("p (a b) -> p a b", a=KB)
                if i % 2:
                    P.op("dve", lambda e, d=dstv, s=srcap: e.tensor_copy(out=d.ap, in_=s), reads=[srcv], writes=[dstv])
                else:
                    P.op("act", lambda e, d=dstv, s=srcap: e.activation(out=d.ap, in_=s, func=AF.Copy), reads=[srcv], writes=[dstv])

            if si == 0 and c == 0:
                dbg("mgT", mgT[:, :, :], [128, NQ, T], BF16)
            if OPT_B:
                for i in range(4):
                    wout_tile(i)
                    if i >= 1:
                        ffn_tr_tile(i - 1)
                ffn_tr_tile(3)
            else:
                for i in range(4):
                    wout_tile(i)
                    ffn_tr_tile(i)
            if si == 0 and c == 0:
                dbg("x1", xt[:, :, :], [128, 4, D], F32)
            P.stage = "ffnin"
            jj = 0
            for g2 in range(FB // 2):
                wg_ = ws2.next("fg%d" % g2, 1)
                wu_ = ws2.next("fu%d" % g2, 2)
                for m in range(2):
                    j = g2 * 2 + m
                    bG, bU = PSB[(jj % 2) * 2], PSB[(jj % 2) * 2 + 1]
                    tb = tmp[4 + jj % 4]
                    jj += 1
                    for kb in range(KB):
                        k.mm(bG[:, :], wg_[:, kb * 256 + m * 128:kb * 256 + (m + 1) * 128], h2T[:, kb, :], start=(kb == 0), stop=(kb == KB - 1))
                    for kb in range(KB):
                        k.mm(bU[:, :], wu_[:, kb * 256 + m * 128:kb * 256 + (m + 1) * 128], h2T[:, kb, :], start=(kb == 0), stop=(kb == KB - 1))
                    k.act(tb[:, :], bG[:, :], AF.Silu)
                    k.tt("dve", acT[:, j, :], bU[:, :], tb[:, :], ALU.mult)
            if si == 0 and c == 0:
                dbg("acT", acT[:, :, :], [128, FB, T], BF16)
            P.stage = "ffnout"
            for half in range(2):
                for (j0, j1) in FO_PIECES:
                    wf = ws2.next("fo%d_%d" % (half, j0))
                    for i in range(4):
                        bank = PSB[half * 4 + i]
                        for j in range(j0, j1):
                            k.mm(bank[:, :], acT[:, j, i * 128:(i + 1) * 128], wf[:, (j - j0) * 512:(j - j0 + 1) * 512], start=(j == 0), stop=(j == FB - 1))
            for i in range(4):
                st = stY[i % 2]
                for half in range(2):
                    k.act(junk[:, 0:512], PSB[half * 4 + i][:, :], AF.Square, accum=st[:, half:half + 1])
                rstd_from(st, 0, 1, 4, add_col=1)
                for half in range(2):
                    tb = tmp[(i % 2) * 2 + half]
                    k.stt("dve", tb[:, :], PSB[half * 4 + i][:, :], st[:, 4:5], gpost[:, 1, half * 512:(half + 1) * 512], ALU.mult, ALU.mult)
                    k.tt("dve", xt[:, i, half * 512:(half + 1) * 512], xt[:, i, half * 512:(half + 1) * 512], tb[:, :], ALU.add)
                k.dma("yst%d" % i, ys[si][c0 + i * 128:c0 + (i + 1) * 128, :], xt[:, i, :])
    P.emit(final_streams=["yst0", "yst1", "yst2", "yst3", "dbg"])
    import os
    if os.environ.get("MK_DUMP_STAGES"):
        with open(os.environ["MK_DUMP_STAGES"], "w") as f:
            for e in ("pe", "act", "dve"):
                f.write(e + ":" + ",".join(o.stage for o in P.ops[e] if o.fn is not None) + "\n")
    return nc


def host_prep(inputs, smax):
    f = np.float32
    g = lambda n: np.ascontiguousarray(np.asarray(inputs[n], dtype=f)[0])
    conv_w, conv_b = g("conv_w"), g("conv_b")
    vec = np.zeros((128, NVEC), f)
    vec[:, VC_CONVW:VC_CONVW + 40] = conv_w.reshape(4, LB, 128).transpose(2, 1, 0).reshape(128, 40)
    vec[:, VC_CONVB:VC_CONVB + 10] = conv_b.reshape(LB, 128).T
    vec[:, VC_BA:VC_BA + 20] = g("lru_b_a").reshape(2 * LB, 128).T
    vec[:, VC_BI:VC_BI + 20] = g("lru_b_i").reshape(2 * LB, 128).T
    vec[:, VC_LAM:VC_LAM + 20] = g("lru_lambda").reshape(2 * LB, 128).T
    vec[:, VC_GPRE:VC_GPRE + 8] = g("norm_mix_pre").reshape(KB, 128).T
    vec[:, VC_GFFN:VC_GFFN + 8] = g("norm_ffn_pre").reshape(KB, 128).T
    gp = np.stack([g("norm_mix_post"), g("norm_ffn_post")], 0)
    sink = g("attn_sink").reshape(1, NQ)
    half = 16
    inv = (np.float32(500000.0) ** (-np.arange(half, dtype=f) / np.float32(half))).astype(f)
    pos = np.arange(-128, smax + 128).astype(f)
    ang = (pos[None, :] * inv[:, None]).astype(f)
    rope = np.zeros((32, 2, smax + 256), f)
    rope[0:16, 0] = np.cos(ang.astype(np.float64))
    rope[16:32, 0] = np.cos(ang.astype(np.float64))
    rope[0:16, 1] = np.sin(ang.astype(np.float64))
    rope[16:32, 1] = np.sin(ang.astype(np.float64))
    common = {
        "w_in": g("w_in"), "w_attn_proj": g("w_attn_proj"), "w_rec_proj": g("w_rec_proj"), "w_out": g("w_out"),
        "w_ffn_in": g("w_ffn_in"), "w_ffn_out": g("w_ffn_out"), "lru_w_a": g("lru_w_a"), "lru_w_i": g("lru_w_i"),
        "vecs": vec, "gpost": gp, "sink": sink, "rope": rope,
    }
    return common


_CACHE = {}


def run_layer(inputs, per_core_seqs, n_cores):
    seq_lens = tuple(a.shape[0] for a in per_core_seqs[0])
    smax = max(seq_lens)
    if seq_lens not in _CACHE:
        _CACHE[seq_lens] = build_program(list(seq_lens), smax)
    nc = _CACHE[seq_lens]
    common = host_prep(inputs, smax)
    in_maps = []
    for cseqs in per_core_seqs:
        m = dict(common)
        for i, a in enumerate(cseqs):
            m["x%d" % i] = np.ascontiguousarray(a, dtype=np.float32)
        in_maps.append(m)
    res = run_bass_kernel_spmd(nc, in_maps, core_ids=list(range(n_cores)))
    global LAST_RES
    LAST_RES = res
    return [[r["y%d" % i] for i in range(len(seq_lens))] for r in res.results]


def kernel(**inputs):
    xp = np.asarray(inputs["x_prompt"], dtype=np.float32)
    xsm = np.asarray(inputs["x_sample"], dtype=np.float32)
    n = 8
    per_core = [[xp[2 * c], xp[2 * c + 1], xsm[c]] for c in range(n)]
    outs = run_layer(inputs, per_core, n)
    yp = np.empty_like(xp)
    ys = np.empty_like(xsm)
    for c in range(n):
        yp[2 * c], yp[2 * c + 1], ys[c] = outs[c][0], outs[c][1], outs[c][2]
    return (yp, ys)
```
